# Optimizing a Trainium2 kernel written in Bass

```python
import jax, jax.numpy as jnp
from jax import lax
import numpy as np

D_MODEL = 1024
BATCH = 32
SEQ = 2048
DEPTH = 2
DEC_BATCH = 32
DEC_SEQ = 32
PAST_LEN = 4096

CHUNK = 64
N_A_LAYERS = DEPTH // 2
N_B_LAYERS = DEPTH - N_A_LAYERS
PLE_DIM = 256
D_FF = 2816
NORM_EPS = 1e-6
SSM_EXPAND = 2
D_INNER = SSM_EXPAND * D_MODEL
SSM_HEAD_DIM = 64
SSM_HEADS = D_INNER // SSM_HEAD_DIM
SSM_GROUPS = 4
SSM_HEADS_PER_GROUP = SSM_HEADS // SSM_GROUPS
D_STATE = 128
CONV_W = 4
CONV_DIM = D_INNER + 2 * SSM_GROUPS * D_STATE
IN_DIM = D_INNER + CONV_DIM + SSM_HEADS
SSD_CHUNK = CHUNK
SB_HEAD_DIM = 64
SB_HEADS = D_MODEL // SB_HEAD_DIM
SB_KV_HEADS = 4
SB_Q_PER_KV = SB_HEADS // SB_KV_HEADS
Q_BLOCK = 128

kernel_name = "yoco_mamba2_stickbreak_stream_step"


def rms_norm(x, g):
    xf = x.astype(jnp.float32)
    y = xf * lax.rsqrt(jnp.mean(xf * xf, axis=-1, keepdims=True) + NORM_EPS)
    return (y * g.astype(jnp.float32)).astype(x.dtype)


def swiglu(h, w_gate, w_up, w_down):
    return (jax.nn.silu(h @ w_gate) * (h @ w_up)) @ w_down


def causal_dwconv(u, buf, w, b):
    L = u.shape[1]
    upad = jnp.concatenate([buf.astype(u.dtype), u], axis=1)
    out = b + upad[:, 0:L] * w[0]
    for k in range(1, CONV_W):
        out = out + upad[:, k:k + L] * w[k]
    return out, upad[:, -(CONV_W - 1):]


def ssd_scan(xdt, a, Bm, Cm, s0, chunk_len):
    b, L, G, E, P = xdt.shape
    nc = L // chunk_len

    def to_chunks(t):
        return jnp.moveaxis(t.reshape((b, nc, chunk_len) + t.shape[2:]), 1, 0)

    causal = jnp.tril(jnp.ones((chunk_len, chunk_len), dtype=bool))[None, :, :, None, None]

    def step(state, inp):
        x_c, a_c, B_c, C_c = inp
        a_cs = jnp.cumsum(a_c, axis=1)
        seg = a_cs[:, :, None] - a_cs[:, None, :]
        decay = jnp.exp(jnp.where(causal, seg, -jnp.inf))
        cb = jnp.einsum('bign,bjgn->bijg', C_c, B_c)
        y_diag = jnp.einsum('bijg,bijge,bjgep->bigep', cb, decay, x_c)
        y_off = jnp.einsum('bign,bgepn->bigep', C_c, state) * jnp.exp(a_cs)[..., None]
        w_end = jnp.exp(a_cs[:, -1:] - a_cs)
        new_state = state * jnp.exp(a_cs[:, -1])[..., None, None] + jnp.einsum(
            'bjgn,bjge,bjgep->bgepn', B_c, w_end, x_c)
        return new_state, y_diag + y_off

    s_fin, ys = lax.scan(step, s0, (to_chunks(xdt), to_chunks(a), to_chunks(Bm), to_chunks(Cm)))
    y = jnp.moveaxis(ys, 0, 1).reshape(b, L, G, E, P)
    return y, s_fin


def mamba2_mixer(h, ssm0, conv0, w_in, conv_w, conv_b, dt_bias, a_log, d_skip, norm_g, w_out):
    b, L, _ = h.shape
    zxbcdt = h @ w_in
    z = zxbcdt[..., :D_INNER]
    xbc = zxbcdt[..., D_INNER:D_INNER + CONV_DIM]
    dt_raw = zxbcdt[..., D_INNER + CONV_DIM:]
    xbc, conv_new = causal_dwconv(xbc, conv0, conv_w, conv_b)
    xbc = jax.nn.silu(xbc).astype(jnp.float32)
    xs = xbc[..., :D_INNER].reshape(b, L, SSM_GROUPS, SSM_HEADS_PER_GROUP, SSM_HEAD_DIM)
    Bm = xbc[..., D_INNER:D_INNER + SSM_GROUPS * D_STATE].reshape(b, L, SSM_GROUPS, D_STATE)
    Cm = xbc[..., D_INNER + SSM_GROUPS * D_STATE:].reshape(b, L, SSM_GROUPS, D_STATE)
    dt = jax.nn.softplus(dt_raw.astype(jnp.float32) + dt_bias.astype(jnp.float32))
    dt = dt.reshape(b, L, SSM_GROUPS, SSM_HEADS_PER_GROUP)
    A = -jnp.exp(a_log.astype(jnp.float32)).reshape(SSM_GROUPS, SSM_HEADS_PER_GROUP)
    s0 = ssm0.astype(jnp.float32).reshape(b, SSM_GROUPS, SSM_HEADS_PER_GROUP, SSM_HEAD_DIM, D_STATE)
    chunk_len = min(SSD_CHUNK, L)
    y, s_fin = ssd_scan(xs * dt[..., None], dt * A, Bm, Cm, s0, chunk_len)
    y = y + xs * d_skip.astype(jnp.float32).reshape(SSM_GROUPS, SSM_HEADS_PER_GROUP)[..., None]
    y = y.reshape(b, L, D_INNER) * jax.nn.silu(z.astype(jnp.float32))
    yg = y.reshape(b, L, SSM_GROUPS, D_INNER // SSM_GROUPS)
    yg = yg * lax.rsqrt(jnp.mean(yg * yg, axis=-1, keepdims=True) + NORM_EPS)
    y = yg.reshape(b, L, D_INNER) * norm_g.astype(jnp.float32)
    out = y.astype(h.dtype) @ w_out
    s_fin = s_fin.reshape(b, SSM_HEADS, SSM_HEAD_DIM, D_STATE).astype(ssm0.dtype)
    return out, s_fin, conv_new.astype(conv0.dtype)


def sb_block(q, k, v, q0):
    lq, lk = q.shape[1], k.shape[1]
    z = jnp.einsum('bqhgd,bkhd->bhgqk', q.astype(jnp.float32), k.astype(jnp.float32)) * (SB_HEAD_DIM ** -0.5)
    q_pos = q0 + jnp.arange(lq)
    mask = jnp.arange(lk)[None, :] < q_pos[:, None]
    log_1m_beta = jnp.where(mask, jax.nn.log_sigmoid(-z), 0.0)
    log_a = z + lax.cumsum(log_1m_beta, axis=4, reverse=True)
    a = jnp.exp(jnp.where(mask, log_a, -jnp.inf))
    out = jnp.einsum('bhgqk,bkhd->bqhgd', a, v.astype(jnp.float32))
    return out.astype(q.dtype)


def sb_attention(q, k, v, q_start):
    lq = q.shape[1]
    blk = min(Q_BLOCK, lq)
    outs = []
    for s0 in range(0, lq, blk):
        e = min(s0 + blk, lq)
        n_keys = q_start + e
        outs.append(sb_block(q[:, s0:e], k[:, :n_keys], v[:, :n_keys], q_start + s0))
    return jnp.concatenate(outs, axis=1)


def trunk(x, p, ssm0, conv0, k_past, v_past, q_start,
          ffn_norm, ffn_w_gate, ffn_w_up, ffn_w_down, mix_norm,
          ssm_w_in, ssm_conv_w, ssm_conv_b, ssm_dt_bias, ssm_a_log, ssm_d, ssm_norm, ssm_w_out,
          kv_norm, w_k, w_v, sb_w_q, sb_w_o,
          ple_norm, ple_w_gate, ple_w_proj, final_norm):
    b, L, _ = x.shape
    ssm_out, conv_out = [], []
    k_new = v_new = k_all = v_all = None
    for i in range(DEPTH):
        if i == N_A_LAYERS:
            kv_in = rms_norm(x, kv_norm)
            k_new = (kv_in @ w_k).reshape(b, L, SB_KV_HEADS, SB_HEAD_DIM)
            v_new = (kv_in @ w_v).reshape(b, L, SB_KV_HEADS, SB_HEAD_DIM)
            if k_past is None:
                k_all, v_all = k_new, v_new
            else:
                k_all = jnp.concatenate([k_past.astype(k_new.dtype), k_new], axis=1)
                v_all = jnp.concatenate([v_past.astype(v_new.dtype), v_new], axis=1)
        x = x + 0.5 * swiglu(rms_norm(x, ffn_norm[i, 0]), ffn_w_gate[i, 0], ffn_w_up[i, 0], ffn_w_down[i, 0])
        h = rms_norm(x, mix_norm[i])
        if i < N_A_LAYERS:
            mix, s_fin, c_fin = mamba2_mixer(h, ssm0[i], conv0[i], ssm_w_in[i], ssm_conv_w[i], ssm_conv_b[i],
                                             ssm_dt_bias[i], ssm_a_log[i], ssm_d[i], ssm_norm[i], ssm_w_out[i])
            ssm_out.append(s_fin)
            conv_out.append(c_fin)
        else:
            j = i - N_A_LAYERS
            q = (h @ sb_w_q[j]).reshape(b, L, SB_KV_HEADS, SB_Q_PER_KV, SB_HEAD_DIM)
            o = sb_attention(q, k_all, v_all, q_start)
            mix = o.reshape(b, L, SB_HEADS * SB_HEAD_DIM) @ sb_w_o[j]
        x = x + mix
        x = x + 0.5 * swiglu(rms_norm(x, ffn_norm[i, 1]), ffn_w_gate[i, 1], ffn_w_up[i, 1], ffn_w_down[i, 1])
        gate = jax.nn.sigmoid(rms_norm(x, ple_norm[i]) @ ple_w_gate[i])
        x = x + gate * (p[i] @ ple_w_proj[i])
    return rms_norm(x, final_norm), jnp.stack(ssm_out), jnp.stack(conv_out), k_new, v_new


def setup_inputs(seed: int = 0) -> dict:
    key = jax.random.key(seed)
    ks = jax.random.split(key, 40)

    def nrm(k, shape, scale):
        return jax.random.normal(k, shape, jnp.float32) * scale

    NA, NB = N_A_LAYERS, N_B_LAYERS
    dt0 = jnp.exp(jax.random.uniform(ks[20], (NA, SSM_HEADS), jnp.float32,
                                     minval=float(np.log(1e-3)), maxval=float(np.log(1e-1))))
    return {
        "x_prompt": nrm(ks[0], (BATCH, SEQ, D_MODEL), 1.0),
        "x_sample": nrm(ks[1], (DEC_BATCH, DEC_SEQ, D_MODEL), 1.0),
        "p_prompt": nrm(ks[2], (DEPTH, BATCH, SEQ, PLE_DIM), 1.0),
        "p_sample": nrm(ks[3], (DEPTH, DEC_BATCH, DEC_SEQ, PLE_DIM), 1.0),
        "state_ssm": nrm(ks[4], (NA, DEC_BATCH, SSM_HEADS, SSM_HEAD_DIM, D_STATE), 0.5),
        "state_conv": nrm(ks[5], (NA, DEC_BATCH, CONV_W - 1, CONV_DIM), 1.0),
        "cache_k": nrm(ks[6], (DEC_BATCH, PAST_LEN, SB_KV_HEADS, SB_HEAD_DIM), 1.0),
        "cache_v": nrm(ks[7], (DEC_BATCH, PAST_LEN, SB_KV_HEADS, SB_HEAD_DIM), 1.0),
        "ffn_norm": 1.0 + nrm(ks[8], (DEPTH, 2, D_MODEL), 0.05),
        "ffn_w_gate": nrm(ks[9], (DEPTH, 2, D_MODEL, D_FF), D_MODEL ** -0.5),
        "ffn_w_up": nrm(ks[10], (DEPTH, 2, D_MODEL, D_FF), D_MODEL ** -0.5),
        "ffn_w_down": nrm(ks[11], (DEPTH, 2, D_FF, D_MODEL), D_FF ** -0.5),
        "mix_norm": 1.0 + nrm(ks[12], (DEPTH, D_MODEL), 0.05),
        "ssm_w_in": nrm(ks[13], (NA, D_MODEL, IN_DIM), D_MODEL ** -0.5),
        "ssm_conv_w": nrm(ks[14], (NA, CONV_W, CONV_DIM), CONV_W ** -0.5),
        "ssm_conv_b": nrm(ks[15], (NA, CONV_DIM), 0.01),
        "ssm_dt_bias": dt0 + jnp.log(-jnp.expm1(-dt0)),
        "ssm_a_log": jnp.log(jax.random.uniform(ks[16], (NA, SSM_HEADS), jnp.float32, minval=1.0, maxval=16.0)),
        "ssm_d": 1.0 + nrm(ks[17], (NA, SSM_HEADS), 0.1),
        "ssm_norm": 1.0 + nrm(ks[18], (NA, D_INNER), 0.05),
        "ssm_w_out": nrm(ks[19], (NA, D_INNER, D_MODEL), D_INNER ** -0.5),
        "kv_norm": 1.0 + nrm(ks[21], (D_MODEL,), 0.05),
        "w_k": nrm(ks[22], (D_MODEL, SB_KV_HEADS * SB_HEAD_DIM), D_MODEL ** -0.5),
        "w_v": nrm(ks[23], (D_MODEL, SB_KV_HEADS * SB_HEAD_DIM), D_MODEL ** -0.5),
        "sb_w_q": nrm(ks[24], (NB, D_MODEL, SB_HEADS * SB_HEAD_DIM), D_MODEL ** -0.5),
        "sb_w_o": nrm(ks[25], (NB, SB_HEADS * SB_HEAD_DIM, D_MODEL), (SB_HEADS * SB_HEAD_DIM) ** -0.5),
        "ple_norm": 1.0 + nrm(ks[26], (DEPTH, D_MODEL), 0.05),
        "ple_w_gate": nrm(ks[27], (DEPTH, D_MODEL, D_MODEL), D_MODEL ** -0.5),
        "ple_w_proj": nrm(ks[28], (DEPTH, PLE_DIM, D_MODEL), PLE_DIM ** -0.5),
        "final_norm": 1.0 + nrm(ks[29], (D_MODEL,), 0.05),
    }


def reference(x_prompt, x_sample, p_prompt, p_sample, state_ssm, state_conv, cache_k, cache_v,
              ffn_norm, ffn_w_gate, ffn_w_up, ffn_w_down, mix_norm,
              ssm_w_in, ssm_conv_w, ssm_conv_b, ssm_dt_bias, ssm_a_log, ssm_d, ssm_norm, ssm_w_out,
              kv_norm, w_k, w_v, sb_w_q, sb_w_o,
              ple_norm, ple_w_gate, ple_w_proj, final_norm):
    weights = (ffn_norm, ffn_w_gate, ffn_w_up, ffn_w_down, mix_norm,
               ssm_w_in, ssm_conv_w, ssm_conv_b, ssm_dt_bias, ssm_a_log, ssm_d, ssm_norm, ssm_w_out,
               kv_norm, w_k, w_v, sb_w_q, sb_w_o,
               ple_norm, ple_w_gate, ple_w_proj, final_norm)
    bp = x_prompt.shape[0]
    ssm0_p = jnp.zeros((N_A_LAYERS, bp, SSM_HEADS, SSM_HEAD_DIM, D_STATE), state_ssm.dtype)
    conv0_p = jnp.zeros((N_A_LAYERS, bp, CONV_W - 1, CONV_DIM), state_conv.dtype)
    y_prompt, ssm_p, conv_p, k_p, v_p = trunk(x_prompt, p_prompt, ssm0_p, conv0_p, None, None, 0, *weights)
    y_sample, ssm_s, conv_s, k_s, v_s = trunk(x_sample, p_sample, state_ssm, state_conv, cache_k, cache_v,
                                              cache_k.shape[1], *weights)
    return (y_prompt, y_sample, ssm_p, conv_p, k_p, v_p, ssm_s, conv_s, k_s, v_s)
```

```python
import numpy as np
import concourse.bass as bass
import concourse.mybir as mybir
from concourse.bass_utils import run_bass_kernel_spmd

F32 = mybir.dt.float32
BF16 = mybir.dt.bfloat16
AF = mybir.ActivationFunctionType
ALU = mybir.AluOpType

D = 1024
KC = 8
DFF = 2816
FC = 22
DIN = 2048
NH = 32
HD = 64
NG = 4
DST = 128
CONVD = 3072
INDIM = 5152
PLE = 256
EPS = 1e-6
SLOT = 2816
PASS_T = 1024


class Prog:
    ENG = ("pe", "act", "dve", "pool", "sp")

    def __init__(self, nc, ring_k=8):
        self.nc = nc
        self.ops = {e: [] for e in self.ENG}
        self.last_w = {}
        self.readers = {}
        self.rings = {"sp": {"n": 0, "K": ring_k}, "pool": {"n": 0, "K": ring_k}}
        self.gdeps = set()

    def _deps(self, eng, reads, writes, is_dma, nobarrier):
        deps = set()
        for k in reads:
            lw = self.last_w.get(k)
            if lw is not None:
                deps.add(lw)
        for k in writes:
            lw = self.last_w.get(k)
            if lw is not None and not (lw[0] == "eng" and lw[1] == eng and not is_dma):
                deps.add(lw)
            for r in self.readers.get(k, ()):
                if not (r[0] == "eng" and r[1] == eng and not is_dma):
                    deps.add(r)
        if not nobarrier:
            deps |= self.gdeps
        return deps

    def op(self, eng, fn, reads=(), writes=(), nobarrier=False):
        psr = tuple(k for k in reads if isinstance(k, tuple) and k[0] == "ps" and k not in writes)
        writes = tuple(writes) + psr
        deps = self._deps(eng, reads, writes, False, nobarrier)
        idx = len(self.ops[eng])
        ref = ("eng", eng, idx)
        self.ops[eng].append({"fn": fn, "deps": deps, "dma": None, "inc": False})
        self._record(ref, reads, writes)
        return ref

    def dma(self, q, fn, reads=(), writes=(), nobarrier=False):
        ring = self.rings[q]
        n = ring["n"]
        ring["n"] += 1
        deps = self._deps(q, reads, writes, True, nobarrier)
        if n >= ring["K"]:
            deps.add(("dma", q, n - ring["K"]))
        ref = ("dma", q, n)
        self.ops[q].append({"fn": fn, "deps": deps, "dma": n, "inc": False})
        self._record(ref, reads, writes)
        return ref

    def _record(self, ref, reads, writes):
        for k in reads:
            self.readers.setdefault(k, []).append(ref)
        for k in writes:
            self.last_w[k] = ref
            self.readers[k] = []

    def barrier(self):
        g = set()
        for e in self.ENG:
            for i in range(len(self.ops[e]) - 1, -1, -1):
                if self.ops[e][i]["dma"] is None:
                    g.add(("eng", e, i))
                    break
        ring = self.rings["sp"]
        for n in range(max(0, ring["n"] - ring["K"]), ring["n"]):
            g.add(("dma", "sp", n))
        self.gdeps = g

    def emit(self, block, sems, ring_sems):
        for e in self.ENG:
            for o in self.ops[e]:
                for d in o["deps"]:
                    if d[0] == "eng":
                        self.ops[d[1]][d[2]]["inc"] = True
        counts = {}
        for e in self.ENG:
            c = 0
            cl = []
            for o in self.ops[e]:
                if o["inc"] and o["dma"] is None:
                    c += 1
                cl.append(c)
            counts[e] = cl
        rings = self.rings

        def resolve(d):
            if d[0] == "eng":
                return (("e", d[1]), counts[d[1]][d[2]])
            K = rings[d[1]]["K"]
            return (("r", d[1], d[2] % K), 16 * (d[2] // K + 1))

        def semof(key):
            if key[0] == "e":
                return sems[key[1]]
            return ring_sems[key[1]][key[2]]

        def run(ename, engine):
            known = {}
            for o in self.ops[ename]:
                need = {}
                for d in o["deps"]:
                    k, v = resolve(d)
                    if v > need.get(k, 0):
                        need[k] = v
                for k, v in need.items():
                    if v > known.get(k, 0):
                        engine.wait_ge(semof(k), v)
                        known[k] = v
                inst = o["fn"](engine)
                if o["dma"] is not None:
                    K = rings[ename]["K"]
                    inst.then_inc(ring_sems[ename][o["dma"] % K], 16)
                elif o["inc"]:
                    inst.then_inc(sems[ename], 1)
            if ename == "sp":
                ring = rings["sp"]
                K = ring["K"]
                for n in range(max(0, ring["n"] - K), ring["n"]):
                    k, v = resolve(("dma", "sp", n))
                    if v > known.get(k, 0):
                        engine.wait_ge(semof(k), v)
                        known[k] = v

        @block.tensor
        def _(e):
            run("pe", e)

        @block.scalar
        def _(e):
            run("act", e)

        @block.vector
        def _(e):
            run("dve", e)

        @block.gpsimd
        def _(e):
            run("pool", e)

        @block.sync
        def _(e):
            run("sp", e)


class Arena:
    def __init__(self, ap_bf16, nbytes):
        self.ap = ap_bf16
        self.nbytes = nbytes
        self.off = 0
        self.peak = 0

    def reset(self):
        self.off = 0

    def alloc(self, shape, dtype, parts=128):
        esz = 4 if dtype == F32 else 2
        n = 1
        for s in shape:
            n *= s
        nb = (n * esz + 31) // 32 * 32
        assert self.off + nb <= self.nbytes, f"arena overflow {self.off}+{nb}>{self.nbytes}"
        v = self.ap[0:parts, self.off // 2:(self.off + n * esz) // 2]
        self.off += nb
        self.peak = max(self.peak, self.off)
        if dtype == F32:
            v = v.bitcast(F32)
        if len(shape) == 2:
            v = v.rearrange("p (a b) -> p a b", a=shape[0])
        elif len(shape) == 3:
            v = v.rearrange("p (a b c) -> p a b c", a=shape[0], b=shape[1])
        return v


def _const_tables():
    i = np.arange(128)
    r, c = i[:, None], i[None, :]
    ident = (r == c)
    U = (r > c)
    tri = (r <= c)
    ones = np.ones((128, 128), bool)
    cf = np.concatenate([ident, U, tri, ones], axis=1).astype(np.float32)
    triGE = (r >= c)
    compl = (r < c)
    cmask = (r < c)
    cbmask = (c >= r)
    cb = np.concatenate([ident, ones, triGE, compl, cmask, cbmask], axis=1).astype(np.float32)
    return cf, cb


CF_IDENT, CF_U, CF_TRI, CF_ONES = 0, 128, 256, 384
CB_IDENT, CB_ONES, CB_TRIGE, CB_COMPL, CB_CMASK, CB_CBMASK = 0, 128, 256, 384, 512, 640
V_FFN = 0
V_MIX = 32
V_KV = 48
V_PLE = 56
V_FIN = 72
V_SSMN = 80
V_CW = 96
V_CB = 192
V_DTB = 216
V_ALOG = 248
V_D = 280
NV = 312


def _vec_table(inp):
    def fm(v):
        v = np.asarray(v, np.float32).reshape(-1, 128)
        return v.T
    cols = []
    for i in range(2):
        for j in range(2):
            cols.append(fm(inp["ffn_norm"][i, j]))
    for i in range(2):
        cols.append(fm(inp["mix_norm"][i]))
    cols.append(fm(inp["kv_norm"]))
    for i in range(2):
        cols.append(fm(inp["ple_norm"][i]))
    cols.append(fm(inp["final_norm"]))
    cols.append(fm(inp["ssm_norm"][0]))
    for k in range(4):
        cols.append(fm(inp["ssm_conv_w"][0, k]))
    cols.append(fm(inp["ssm_conv_b"][0]))
    for nm in ("ssm_dt_bias", "ssm_a_log", "ssm_d"):
        cols.append(np.broadcast_to(np.asarray(inp[nm][0], np.float32)[None, :], (128, 32)))
    out = np.ascontiguousarray(np.concatenate(cols, axis=1), dtype=np.float32)
    assert out.shape == (128, NV)
    return out


def build_program(NPB, S, NSB, PAST, phases=("ffn", "mamba", "ple", "kv", "attn"), nslot=5, work_kb=105.5):
    assert S % PASS_T == 0 and PAST % 128 == 0
    nc = bass.Bass("TRN2", target_bir_lowering=False)
    TS = NSB * 32

    def din(name, shape):
        return nc.dram_tensor(name, list(shape), F32, kind="ExternalInput").ap()

    def dout(name, shape):
        return nc.dram_tensor(name, list(shape), F32, kind="ExternalOutput").ap()

    xp = din("xp", [NPB, S, D])
    xs = din("xs", [TS, D])
    pp = din("pp", [2, NPB, S, PLE])
    psm = din("psm", [2, TS, PLE])
    sssm = din("sssm", [NSB, NH, HD, DST])
    sconv = din("sconv", [NSB, 3, CONVD])
    ck = din("ck", [NSB, PAST, 256])
    cv = din("cv", [NSB, PAST, 256])
    w_gate = din("w_gate", [2, 2, D, DFF])
    w_up = din("w_up", [2, 2, D, DFF])
    w_down = din("w_down", [2, 2, DFF, D])
    w_in = din("w_in", [D, INDIM])
    w_out = din("w_out", [DIN, D])
    w_k = din("w_k", [D, 256])
    w_v = din("w_v", [D, 256])
    w_q = din("w_q", [D, D])
    w_o = din("w_o", [D, D])
    w_pg = din("w_pg", [2, D, D])
    w_pp = din("w_pp", [2, PLE, D])
    cf_d = din("cf", [128, 512])
    cb_d = din("cb", [128, 768])
    vec_d = din("vec", [128, NV])

    y_p = dout("y_p", [NPB, S, D])
    y_s = dout("y_s", [TS, D])
    ssm_p = dout("ssm_p", [NPB, NH, HD, DST])
    conv_p = dout("conv_p", [NPB, 3, CONVD])
    k_p = dout("k_p", [NPB, S, 256])
    v_p = dout("v_p", [NPB, S, 256])
    ssm_s = dout("ssm_s", [NSB, NH, HD, DST])
    conv_s = dout("conv_s", [NSB, 3, CONVD])
    k_s = dout("k_s", [TS, 256])
    v_s = dout("v_s", [TS, 256])

    WORKB = int(work_kb * 1024)
    import contextlib
    es = contextlib.ExitStack()
    with es:
        def sb(name, shape, dt):
            return es.enter_context(nc.sbuf_tensor(name, list(shape), dt))

        X = sb("X", [128, KC, PASS_T], F32)
        CF = sb("CF", [128, 512], F32)
        CB = sb("CB", [128, 768], BF16)
        VEC = sb("VEC", [128, NV], F32)
        ABC = sb("ABC", [128, 32], F32)
        RING = sb("RING", [128, nslot, SLOT], BF16)
        STATE = sb("STATE", [128, DIN], F32)
        STATEB = sb("STATEB", [128, DIN], BF16)
        CTAIL = sb("CTAIL", [128, 24, 4, 3], F32)
        KT = sb("KT", [64, NG, S], BF16)
        VV = sb("VV", [128, S // 128, 256], BF16)
        WORK = sb("WORK", [128, WORKB // 2], BF16)
        PS = es.enter_context(nc.psum_tensor("PS", [128, 8, 512], F32))
        sems = {e: es.enter_context(nc.semaphore("s_" + e)) for e in Prog.ENG}
        P = Prog(nc, ring_k=8)
        ring_sems = {q: [es.enter_context(nc.semaphore(f"r_{q}{i}")) for i in range(8)] for q in ("sp", "pool")}
        block = es.enter_context(nc.Block())
        W = Arena(WORK, WORKB)

        ident_f = CF[:, CF_IDENT:CF_IDENT + 128]
        U_f = CF[:, CF_U:CF_U + 128]
        tri_f = CF[:, CF_TRI:CF_TRI + 128]
        ones_f = CF[:, CF_ONES:CF_ONES + 128]
        ident_b = CB[:, CB_IDENT:CB_IDENT + 128]
        ones_b = CB[:, CB_ONES:CB_ONES + 128]
        trige_b = CB[:, CB_TRIGE:CB_TRIGE + 128]
        compl_b = CB[:, CB_COMPL:CB_COMPL + 128]
        cmask_b = CB[:, CB_CMASK:CB_CMASK + 128]
        cbmask_b = CB[:, CB_CBMASK:CB_CBMASK + 128]

        def PSB(b):
            return PS[:, b, :]

        def PSBH(b):
            return PS[:, b, :].bitcast(BF16)

        def mm(out, lhsT, rhs, start, stop, reads, writes, skip=False):
            if skip:
                P.op("pe", lambda e: e.matmul(out, lhsT, rhs, start=start, stop=stop, skip_group_check=True),
                     reads, writes)
            else:
                P.op("pe", lambda e: e.matmul(out, lhsT, rhs, start=start, stop=stop), reads, writes)

        def tp(out, in_, ident, reads, writes):
            P.op("pe", lambda e: e.transpose(out, in_, ident), reads, writes)

        def act(out, in_, func, reads, writes, bias=None, scale=None):
            kw = {}
            if bias is not None:
                kw["bias"] = bias
            if scale is not None:
                kw["scale"] = scale
            P.op("act", lambda e: e.activation(out, in_, func, **kw), reads, writes)

        def tt(eng, out, in0, in1, op, reads, writes):
            P.op(eng, lambda e: e.tensor_tensor(out, in0, in1, op), reads, writes)

        def ts(eng, out, in0, s1, s2, op0, op1, reads, writes):
            if op1 is None:
                P.op(eng, lambda e: e.tensor_scalar(out, in0, s1, None, op0), reads, writes)
            else:
                P.op(eng, lambda e: e.tensor_scalar(out, in0, s1, s2, op0, op1), reads, writes)

        def stt(out, in0, scalar, in1, op0, op1, reads, writes):
            P.op("dve", lambda e: e.scalar_tensor_tensor(out, in0, scalar, in1, op0, op1), reads, writes)

        def cp(eng, out, in_, reads, writes):
            if eng == "act":
                P.op("act", lambda e: e.copy(out, in_), reads, writes)
            else:
                P.op(eng, lambda e: e.tensor_copy(out, in_), reads, writes)

        def memset(eng, ap, val, writes):
            P.op(eng, lambda e: e.memset(ap, val), (), writes)

        ring_state = {"n": 0}

        def wload(src_ap, shape, parts=128):
            s = ring_state["n"] % nslot
            ring_state["n"] += 1
            a, b = shape
            assert a * b <= SLOT
            dst = RING[0:parts, s, 0:a * b].rearrange("p (a b) -> p a b", a=a)
            key = ("ring", s)
            P.dma("pool", lambda e: e.dma_start(out=dst, in_=src_ap), (), (key,), nobarrier=True)
            return dst, key

        P.dma("sp", lambda e: e.dma_start(out=CF[:, :], in_=cf_d[:, :]), (), ("CF",))
        P.dma("sp", lambda e: e.dma_start(out=VEC[:, :], in_=vec_d[:, :]), (), ("VEC",))
        P.dma("pool", lambda e: e.dma_start(out=CB[:, :], in_=cb_d[:, :]), (), ("CB",))
        act(ABC[:, :], VEC[:, V_ALOG:V_ALOG + 32], AF.Exp, ("VEC",), ("ABC",))
        ts("dve", ABC[:, :], ABC[:, :], -1.0, None, ALU.mult, None, ("ABC",), ("ABC",))
        CONSTK = ("CF", "CB", "VEC", "ABC")

        def gcol(base, k):
            return VEC[:, base + k:base + k + 1]

        def phase_begin():
            P.barrier()
            W.reset()

        def tiles_of(T):
            TW = min(512, T)
            return TW, T // TW

        def rmsnorm_tile(H, vbase, t, TW, SQ, RS, psb):
            cs = slice(t * TW, (t + 1) * TW)
            act(SQ[:, :, 0:TW], X[:, :, cs], AF.Square, (("X", t),), ("SQ",))
            for k in range(KC):
                mm(PSB(psb)[:, 0:TW], ones_b, SQ[:, k, 0:TW], k == 0, k == KC - 1, ("SQ", "CB"), (("ps", psb),))
            act(RS[:, 0:TW], PSB(psb)[:, 0:TW], AF.Sqrt, (("ps", psb),), ("RS",), bias=EPSB[:, 0:1], scale=1.0 / D)
            P.op("dve", lambda e: e.reciprocal(RS[:, 0:TW], RS[:, 0:TW]), ("RS",), ("RS",))
            for k in range(KC):
                stt(H[:, k, cs], X[:, k, cs], gcol(vbase, k), RS[:, 0:TW], ALU.mult, ALU.mult,
                    (("X", t), "RS", "VEC"), (("H", t),))

        EPSB = sb("EPSB", [128, 1], F32)
        memset("dve", EPSB[:, :], EPS, ("EPSB",))
        HALFB = sb("HALFB", [128, 1], F32)
        memset("dve", HALFB[:, :], 0.5, ("HALFB",))

        def load_x(src_rows, T):
            phase_begin()
            STG = [W.alloc([D], F32) for _ in range(2)]
            for blk in range(T // 128):
                st = STG[blk % 2]
                sk = ("xstg", blk % 2)
                src = src_rows[blk * 128:(blk + 1) * 128, :]
                P.dma("sp", lambda e, st=st, src=src: e.dma_start(out=st, in_=src), (), (sk,))
                for half in range(2):
                    b = (blk * 2 + half) % 4
                    for j in range(4):
                        k = half * 4 + j
                        tp(PSB(b)[:, j * 128:(j + 1) * 128], st[:, k * 128:(k + 1) * 128], ident_f,
                           (sk, "CF"), (("ps", b),))
                    cp("act" if half == 0 else "dve",
                       X[:, half * 4:half * 4 + 4, blk * 128:(blk + 1) * 128],
                       PSB(b).rearrange("p (a b) -> p a b", a=4), (("ps", b),), (("X", blk // 4),))

        def store_y(dst_rows, T):
            phase_begin()
            TW, NT = tiles_of(T)
            SQ = W.alloc([KC, TW], BF16)
            RS = W.alloc([TW], F32)
            YN = W.alloc([KC, TW], F32)
            STG = [W.alloc([D], F32) for _ in range(2)]
            for t in range(NT):
                cs = slice(t * TW, (t + 1) * TW)
                act(SQ[:, :, 0:TW], X[:, :, cs], AF.Square, (("X", t),), ("SQ",))
                for k in range(KC):
                    mm(PSB(7)[:, 0:TW], ones_b, SQ[:, k, 0:TW], k == 0, k == KC - 1, ("SQ", "CB"), (("ps", 7),))
                act(RS[:, 0:TW], PSB(7)[:, 0:TW], AF.Sqrt, (("ps", 7),), ("RS",), bias=EPSB[:, 0:1], scale=1.0 / D)
                P.op("dve", lambda e: e.reciprocal(RS[:, 0:TW], RS[:, 0:TW]), ("RS",), ("RS",))
                for k in range(KC):
                    stt(YN[:, k, 0:TW], X[:, k, cs], gcol(V_FIN, k), RS[:, 0:TW], ALU.mult, ALU.mult,
                        (("X", t), "RS", "VEC"), ("YN",))
                for bl in range(TW // 128):
                    gb = t * (TW // 128) + bl
                    st = STG[gb % 2]
                    sk = ("ystg", gb % 2)
                    for half in range(2):
                        b = (gb * 2 + half) % 4
                        for j in range(4):
                            k = half * 4 + j
                            tp(PSB(b)[:, j * 128:(j + 1) * 128], YN[:, k, bl * 128:(bl + 1) * 128], ident_f,
                               ("YN", "CF"), (("ps", b),))
                        cp("act" if half == 0 else "dve", st[:, half * 512:(half + 1) * 512], PSB(b),
                           (("ps", b),), (sk,))
                    dst = dst_rows[gb * 128:(gb + 1) * 128, :]
                    P.dma("sp", lambda e, st=st, dst=dst: e.dma_start(out=dst, in_=st), (sk,), ())

        def ffn(i, j, T):
            phase_begin()
            TW, NT = tiles_of(T)
            H = W.alloc([KC, T], BF16)
            AB = W.alloc([FC, T], BF16)
            SQ = W.alloc([KC, TW], BF16)
            RS = W.alloc([TW], F32)
            SG = [W.alloc([TW], F32) for _ in range(2)]
            for t in range(NT):
                rmsnorm_tile(H, V_FFN + (i * 2 + j) * 8, t, TW, SQ, RS, 7)
            import os
            dbg = int(os.environ.get("FFN_DBG", "9"))
            if dbg < 1:
                return
            wg_d = w_gate[i, j].rearrange("(k p) n -> p k n", p=128)
            wu_d = w_up[i, j].rearrange("(k p) n -> p k n", p=128)
            wd_d = w_down[i, j].rearrange("(f p) n -> p f n", p=128)
            cnt = 0
            for fp in range(FC // 2):
                wg, kg = wload(wg_d[:, :, fp * 256:(fp + 1) * 256], (KC, 256))
                wu, ku = wload(wu_d[:, :, fp * 256:(fp + 1) * 256], (KC, 256))
                for t in range(NT):
                    cs = slice(t * TW, (t + 1) * TW)
                    for fi in range(2):
                        f = fp * 2 + fi
                        ba = (cnt * 2) % 6
                        bb = (cnt * 2 + 1) % 6
                        cnt += 1
                        for k in range(KC):
                            mm(PSB(ba)[:, 0:TW], wg[:, k, fi * 128:(fi + 1) * 128], H[:, k, cs], k == 0, k == KC - 1,
                               (kg, ("H", t)), (("ps", ba),))
                        for k in range(KC):
                            mm(PSB(bb)[:, 0:TW], wu[:, k, fi * 128:(fi + 1) * 128], H[:, k, cs], k == 0, k == KC - 1,
                               (ku, ("H", t)), (("ps", bb),))
                        sg = SG[cnt % 2]
                        sgk = ("SG", cnt % 2)
                        act(sg[:, 0:TW], PSB(ba)[:, 0:TW], AF.Silu, (("ps", ba),), (sgk,))
                        tt("dve", AB[:, f, cs], sg[:, 0:TW], PSB(bb)[:, 0:TW], ALU.mult, (sgk, ("ps", bb)),
                           (("AB", f, t),))
            if dbg < 2:
                return
            cnt = 0
            for m in range(KC):
                wd, kd = wload(wd_d[:, :, m * 128:(m + 1) * 128], (FC, 128))
                for t in range(NT):
                    cs = slice(t * TW, (t + 1) * TW)
                    b = 6 + (cnt % 2)
                    cnt += 1
                    if dbg < 3:
                        continue
                    for f in range(FC):
                        mm(PSB(b)[:, 0:TW], wd[:, f, :], AB[:, f, cs], f == 0, f == FC - 1,
                           (kd, ("AB", f, t)), (("ps", b),))
                    if dbg < 4:
                        continue
                    stt(X[:, m, cs], PSB(b)[:, 0:TW], HALFB[:, 0:1], X[:, m, cs], ALU.mult, ALU.add,
                        (("ps", b), ("X", t)), (("X", t),))

        def ple(i, prow, T):
            phase_begin()
            TW, NT = tiles_of(T)
            H = W.alloc([KC, T], BF16)
            PT = W.alloc([2, T], BF16)
            SQ = W.alloc([KC, TW], BF16)
            RS = W.alloc([TW], F32)
            SG = [W.alloc([TW], F32) for _ in range(2)]
            STG = [W.alloc([PLE], F32) for _ in range(2)]
            for t in range(NT):
                rmsnorm_tile(H, V_PLE + i * 8, t, TW, SQ, RS, 7)
            for blk in range(T // 128):
                st = STG[blk % 2]
                sk = ("pstg", blk % 2)
                src = prow[blk * 128:(blk + 1) * 128, :]
                P.dma("sp", lambda e, st=st, src=src: e.dma_start(out=st, in_=src), (), (sk,))
                b = 4 + blk % 2
                for k2 in range(2):
                    tp(PSB(b)[:, k2 * 128:(k2 + 1) * 128], st[:, k2 * 128:(k2 + 1) * 128], ident_f, (sk, "CF"),
                       (("ps", b),))
                cp("act", PT[:, :, blk * 128:(blk + 1) * 128],
                   PSB(b)[:, 0:256].rearrange("p (a b) -> p a b", a=2), (("ps", b),), ("PT",))
            wpp, kpp = wload(w_pp[i].rearrange("(k p) n -> p k n", p=128), (2, D))
            wg_d = w_pg[i].rearrange("(k p) n -> p k n", p=128)
            cnt = 0
            for mp in range(4):
                wg, kg = wload(wg_d[:, :, mp * 256:(mp + 1) * 256], (KC, 256))
                for mi in range(2):
                    m = mp * 2 + mi
                    for t in range(NT):
                        cs = slice(t * TW, (t + 1) * TW)
                        ba = (cnt * 2) % 4
                        bb = (cnt * 2 + 1) % 4
                        cnt += 1
                        for k in range(KC):
                            mm(PSB(ba)[:, 0:TW], wg[:, k, mi * 128:(mi + 1) * 128], H[:, k, cs], k == 0, k == KC - 1,
                               (kg, ("H", t)), (("ps", ba),))
                        for k2 in range(2):
                            mm(PSB(bb)[:, 0:TW], wpp[:, k2, m * 128:(m + 1) * 128], PT[:, k2, cs], k2 == 0, k2 == 1,
                               (kpp, "PT"), (("ps", bb),))
                        sg = SG[cnt % 2]
                        sgk = ("SG", cnt % 2)
                        act(sg[:, 0:TW], PSB(ba)[:, 0:TW], AF.Sigmoid, (("ps", ba),), (sgk,))
                        tt("dve", sg[:, 0:TW], sg[:, 0:TW], PSB(bb)[:, 0:TW], ALU.mult, (sgk, ("ps", bb)), (sgk,))
                        tt("dve", X[:, m, cs], X[:, m, cs], sg[:, 0:TW], ALU.add, (sgk, ("X", t)), (("X", t),))

        def kv(T, pos0, kdst, vdst, KTdst, Vdst_fn, RB=128):
            phase_begin()
            TW, NT = tiles_of(T)
            H = W.alloc([KC, T], BF16)
            SQ = W.alloc([KC, TW], BF16)
            RS = W.alloc([TW], F32)
            KVO = [W.alloc([512], F32) for _ in range(2)]
            for t in range(NT):
                rmsnorm_tile(H, V_KV, t, TW, SQ, RS, 7)
            wk, kk = wload(w_k.rearrange("(k p) n -> p k n", p=128), (KC, 256))
            wv, kvk = wload(w_v.rearrange("(k p) n -> p k n", p=128), (KC, 256))
            import os
            kdbg = int(os.environ.get("KV_DBG", "9"))
            cnt = 0
            for h in range(NG if kdbg >= 1 else 0):
                for t in range(NT):
                    cs = slice(t * TW, (t + 1) * TW)
                    b = cnt % 2
                    cnt += 1
                    for k in range(KC):
                        mm(PSB(b)[0:64, 0:TW], wk[:, k, h * 64:(h + 1) * 64], H[:, k, cs], k == 0, k == KC - 1,
                           (kk, ("H", t)), (("ps", b),))
                    cp("act", KTdst[:, h, cs], PSB(b)[0:64, 0:TW], (("ps", b),), ("KT",))
            for blk in range(T // RB if kdbg >= 2 else 0):
                b = 2 + blk % 2
                bs = slice(blk * RB, (blk + 1) * RB)
                t = (blk * RB) // TW
                for k in range(KC):
                    mm(PSB(b)[0:RB, 0:256], H[:, k, bs], wk[:, k, :], k == 0, k == KC - 1, (kk, ("H", t)), (("ps", b),))
                for k in range(KC):
                    mm(PSB(b)[0:RB, 256:512], H[:, k, bs], wv[:, k, :], k == 0, k == KC - 1, (kvk, ("H", t)),
                       (("ps", b),))
                kvo = KVO[blk % 2]
                kk2 = ("KVO", blk % 2)
                cp("act", kvo[0:RB, :], PSB(b)[0:RB, :], (("ps", b),), (kk2,))
                cp("dve", Vdst_fn(blk), PSB(b)[0:RB, 256:512], (("ps", b),), ("VV",))
                if kdbg < 3:
                    continue
                kd = kdst[blk * RB:(blk + 1) * RB, :]
                vd = vdst[blk * RB:(blk + 1) * RB, :]
                P.dma("sp", lambda e, kvo=kvo, kd=kd: e.dma_start(out=kd, in_=kvo[0:RB, 0:256]), (kk2,), ())
                P.dma("sp", lambda e, kvo=kvo, vd=vd: e.dma_start(out=vd, in_=kvo[0:RB, 256:512]), (kk2,), ())

        def mamba(T, mode, seq_first, seq_last, bidx):
            phase_begin()
            TW, NT = tiles_of(T)
            prompt = (mode == "p")
            nseg, seglen = (1, TW) if prompt else (NSB, 32)
            Lc = 128 if prompt else 32
            H = W.alloc([KC, TW], BF16)
            RS = W.alloc([TW], F32)
            XBC = W.alloc([24, TW], BF16)
            SQ = XBC[:, 0:KC, :]
            sqkeys = tuple(("XBC", m_) for m_ in range(KC))
            YT = W.alloc([16, TW], BF16)
            up_off = W.off
            UP = [W.alloc([nseg, seglen + 3], F32) for _ in range(3)]
            CVT = [W.alloc([nseg, seglen], F32) for _ in range(2)]
            upcvt_bytes = W.off - up_off
            XS_TM = W.alloc([DIN], BF16)
            XDT2 = [W.alloc([DIN], BF16) for _ in range(2)]
            XDTW = W.alloc([DIN], BF16)
            B_TM = W.alloc([512], BF16)
            SM2 = [W.alloc([8, 32], F32) for _ in range(2)]
            CBM2 = [W.alloc([NG, 128], BF16) for _ in range(2)]
            ATRI = [W.alloc([8, 128], F32)] * 2
            ESEG = [W.alloc([8, 128], BF16) for _ in range(2)]
            MT = [W.alloc([8, 128], BF16) for _ in range(2)]
            YOFF0 = W.alloc([DIN], F32)
            YB = W.alloc([DIN], BF16)
            if prompt and upcvt_bytes >= DIN * 4:
                YOFF1 = W.ap[:, up_off // 2:up_off // 2 + DIN * 2].bitcast(F32)
                yk1 = tuple(("UP", i_) for i_ in range(3)) + tuple(("UPT", i_) for i_ in range(3)) + \
                    tuple(("CVT", i_) for i_ in range(2))
            else:
                YOFF1 = W.alloc([DIN], F32)
                yk1 = ("YOFF1",)
            YOFF2 = [YOFF0, YOFF1]
            YK2 = [("YOFF",), yk1]
            SIO = YOFF0.rearrange("p (a n) -> p a n", a=16)

            win_d = w_in.rearrange("(k p) n -> p k n", p=128)
            wout_d = w_out.rearrange("(k p) n -> p k n", p=128)

            def seg_view(ap3, L):
                return ap3[:, :, 0:L]

            if prompt and seq_first:
                memset("dve", STATE[:, :], 0.0, ("STATE",))
                memset("dve", STATEB[:, :], 0.0, ("STATEB",))
                memset("dve", CTAIL[:, :, :, :], 0.0, tuple(("CTAIL", m_) for m_ in range(24)))
            if not prompt:
                for s in range(NSB):
                    for tt_ in range(3):
                        src = sconv[s, tt_, :].rearrange("(m p) -> p m", p=128)
                        P.dma("sp", lambda e, src=src, s=s, tt_=tt_: e.dma_start(
                            out=CTAIL[:, :, s, tt_], in_=src, allow_slow_non_contiguous=True), (), tuple(("CTAIL", m_) for m_ in range(24)))

            for t in range(NT):
                cs = slice(t * TW, (t + 1) * TW)
                rmsnorm_tile_local(H, V_MIX + 0, t, TW, SQ, RS, sqkeys)
                wps = {}

                def stA(m):
                    if m % 2 == 0:
                        pc = m // 2
                        wps[pc] = wload(win_d[:, :, DIN + pc * 256:DIN + (pc + 1) * 256], (KC, 256))
                    wp, kp = wps[m // 2]
                    mi = m % 2
                    b = m % 2
                    up = UP[m % 3]
                    upk = ("UP", m % 3)
                    for k in range(KC):
                        mm(PSB(b)[:, 0:TW], wp[:, k, mi * 128:(mi + 1) * 128], H[:, k, 0:TW], k == 0, k == KC - 1,
                           (kp, "Hm"), (("ps", b),))
                    cp("dve", up[:, :, 0:3], CTAIL[:, m, 0:nseg, :], (("CTAIL", m),), (("UPT", m % 3),))
                    cp("act", up[:, :, 3:3 + seglen],
                       PSB(b)[:, 0:TW].rearrange("p (s l) -> p s l", s=nseg), (("ps", b),), (upk,))
                    cp("act", CTAIL[:, m, 0:nseg, :], up[:, :, seglen:seglen + 3], (upk, ("UPT", m % 3)), (("CTAIL", m),))

                def stB1(m):
                    up = UP[m % 3]
                    upk = ("UP", m % 3)
                    cvt = CVT[m % 2]
                    cvk = ("CVT", m % 2)
                    act(cvt[:, :, :], up[:, :, 0:seglen], AF.Identity, (upk, ("UPT", m % 3), "VEC"), (cvk,),
                        bias=gcol(V_CB, m), scale=gcol(V_CW + 0 * 24, m))
                    for kk_ in range(1, 4):
                        stt(cvt[:, :, :], up[:, :, kk_:kk_ + seglen], gcol(V_CW + kk_ * 24, m), cvt[:, :, :],
                            ALU.mult, ALU.add, (upk, ("UPT", m % 3), cvk, "VEC"), (cvk,))

                def stB2(m):
                    cvt = CVT[m % 2]
                    cvk = ("CVT", m % 2)
                    act(XBC[:, m, 0:TW].rearrange("p (s l) -> p s l", s=nseg), cvt[:, :, :], AF.Silu,
                        (cvk,), (("XBC", m),))

                stA(0)
                for m in range(24):
                    if m + 1 < 24:
                        stA(m + 1)
                    stB1(m)
                    if m >= 1:
                        stB2(m - 1)
                stB2(23)
                wdt, kdt = wload(win_d[:, :, DIN + CONVD:INDIM], (KC, 32))
                import os
                nchunk = TW // Lc
                if os.environ.get("MAMBA_NOCHUNK"):
                    nchunk = 0
                L = Lc

                def stage1(c):
                    c0 = c * Lc
                    cc = slice(c0, c0 + Lc)
                    SM = SM2[c % 2]
                    smk = lambda i_: ("SM", c % 2, i_)
                    XDT = XDT2[c % 2]
                    xdk = ("XDT", c % 2)
                    CBM = CBM2[c % 2]
                    cbk = ("CBM", c % 2)
                    YOFF = YOFF2[c % 2]
                    yk = YK2[c % 2]
                    if not prompt:
                        for two in range(2):
                            src = sssm[c].rearrange("(hp two) p n -> two p hp n", two=2)[two]
                            P.dma("sp", lambda e, src=src, two=two: e.dma_start(
                                out=SIO[two * 64:(two + 1) * 64, :, :], in_=src), (), ("YOFF",))
                        for hp in range(16):
                            b = 4 + (hp // 4) % 2
                            tp(PSB(b)[:, (hp % 4) * 128:(hp % 4 + 1) * 128], SIO[:, hp, :], ident_f, ("YOFF", "CF"),
                               (("ps", b),))
                            if hp % 4 == 3:
                                q4 = hp // 4
                                cp("dve", STATE[:, q4 * 512:(q4 + 1) * 512], PSB(b), (("ps", b),), ("STATE",))
                                cp("act", STATEB[:, q4 * 512:(q4 + 1) * 512], PSB(b), (("ps", b),), ("STATEB",))
                    for k in range(KC):
                        mm(PSB(6)[0:L, 0:32], H[:, k, cc], wdt[:, k, :], k == 0, k == KC - 1, (kdt, "Hm"), (("ps", 6),))
                    tt("dve", SM[0:L, 0, :], PSB(6)[0:L, 0:32], VEC[0:L, V_DTB:V_DTB + 32], ALU.add,
                       (("ps", 6), "VEC"), (smk(0),))
                    act(SM[0:L, 1, :], SM[0:L, 0, :], AF.Exp, (smk(0),), (smk(1),))
                    act(SM[0:L, 2, :], SM[0:L, 1, :], AF.Ln, (smk(1),), (smk(2),), bias=ONEB[0:L, 0:1])
                    tt("dve", SM[0:L, 3, :], SM[0:L, 2, :], ABC[0:L, :], ALU.mult, (smk(2), "ABC"), (smk(3),))
                    mm(PSB(6)[0:L, 32:64], tri_f[0:L, 0:L], SM[0:L, 3, :], True, True, (smk(3), "CF"), (("ps", 6),))
                    mm(PSB(6)[:, 64:96], ones_f[0:L, :], SM[0:L, 3, :], True, True, (smk(3), "CF"), (("ps", 6),))
                    cp("dve", SM[0:L, 4, :], PSB(6)[0:L, 32:64], (("ps", 6),), (smk(4),))
                    act(SM[0:L, 5, :], PSB(6)[0:L, 32:64], AF.Exp, (("ps", 6),), (smk(5),))
                    tt("dve", SM[0:L, 6, :], PSB(6)[0:L, 64:96], SM[0:L, 4, :], ALU.subtract, (("ps", 6), smk(4)),
                       (smk(6),))
                    act(SM[0:L, 6, :], SM[0:L, 6, :], AF.Exp, (smk(6),), (smk(6),))
                    act(SM[:, 7, :], PSB(6)[:, 64:96], AF.Exp, (("ps", 6),), (smk(7),))
                    for m in range(16):
                        b = 4 + m // 8
                        tp(PSBH(b)[0:L, (m % 8) * 128:(m % 8 + 1) * 128], XBC[:, m, cc], ident_b,
                           (("XBC", m), "CB"), (("ps", b),))
                    for hb in range(2):
                        b = 4 + hb
                        hs = slice(hb * 1024, (hb + 1) * 1024)
                        cp("act", XS_TM[0:L, hs], PSBH(b)[0:L, :], (("ps", b),), ("XS_TM",))
                        tt("dve", XDT[0:L, hs].rearrange("p (h d) -> p h d", d=HD),
                           PSBH(b)[0:L, :].rearrange("p (h d) -> p h d", d=HD),
                           SM[0:L, 2, hb * 16:(hb + 1) * 16].unsqueeze(2).to_broadcast([L, 16, HD]), ALU.mult,
                           (("ps", b), smk(2)), (xdk,))
                    tt("dve", XDTW[0:L, :].rearrange("p (h d) -> p h d", d=HD),
                       XDT[0:L, :].rearrange("p (h d) -> p h d", d=HD),
                       SM[0:L, 6, :].unsqueeze(2).to_broadcast([L, NH, HD]), ALU.mult, (xdk, smk(6)), ("XDTW",))
                    for g in range(NG):
                        tp(PSBH(7)[0:L, g * 128:(g + 1) * 128], XBC[:, 16 + g, cc], ident_b,
                           (("XBC", 16 + g), "CB"), (("ps", 7),))
                    cp("act", B_TM[0:L, :], PSBH(7)[0:L, 0:512], (("ps", 7),), ("B_TM",))
                    for g in range(NG):
                        mm(PSB(7)[0:L, g * L:(g + 1) * L], XBC[:, 16 + g, cc], XBC[:, 20 + g, cc], True, True,
                           (("XBC", 16 + g), ("XBC", 20 + g)), (("ps", 7),))
                    tt("dve", CBM[0:L, :, 0:L], PSB(7)[0:L, 0:NG * L].rearrange("p (g i) -> p g i", g=NG),
                       cbmask_b[0:L, 0:L].unsqueeze(1).to_broadcast([L, NG, L]), ALU.mult, (("ps", 7), "CB"), (cbk,))
                    for g in range(NG):
                        mm(PSB(g)[0:L, :], XBC[:, 20 + g, cc], STATEB[:, g * 512:(g + 1) * 512], True, True,
                           (("XBC", 20 + g), "STATEB"), (("ps", g),))
                        tt("dve", YOFF[0:L, g * 512:(g + 1) * 512].rearrange("p (h d) -> p h d", d=HD),
                           PSB(g)[0:L, :].rearrange("p (h d) -> p h d", d=HD),
                           SM[0:L, 5, g * 8:(g + 1) * 8].unsqueeze(2).to_broadcast([L, 8, HD]), ALU.mult,
                           (("ps", g), smk(5)), yk)
                    tt("dve", XS_TM[0:L, :].rearrange("p (h d) -> p h d", d=HD),
                       XS_TM[0:L, :].rearrange("p (h d) -> p h d", d=HD),
                       VEC[0:L, V_D:V_D + 32].unsqueeze(2).to_broadcast([L, NH, HD]), ALU.mult,
                       ("XS_TM", "VEC"), ("XS_TM",))
                    tt("dve", YOFF[0:L, :], YOFF[0:L, :], XS_TM[0:L, :], ALU.add, yk + ("XS_TM",), yk)
                    for g in range(NG):
                        mm(PSB(g)[:, :], B_TM[0:L, g * 128:(g + 1) * 128], XDTW[0:L, g * 512:(g + 1) * 512], True, True,
                           ("B_TM", "XDTW"), (("ps", g),))
                    tt("dve", STATE[:, :].rearrange("p (h d) -> p h d", d=HD),
                       STATE[:, :].rearrange("p (h d) -> p h d", d=HD),
                       SM[:, 7, :].unsqueeze(2).to_broadcast([128, NH, HD]), ALU.mult, ("STATE", smk(7)), ("STATE",))
                    for g in range(NG):
                        tt("dve", STATE[:, g * 512:(g + 1) * 512], STATE[:, g * 512:(g + 1) * 512], PSB(g)[:, :], ALU.add,
                           ("STATE", ("ps", g)), ("STATE",))
                    cp("act", STATEB[:, :], STATE[:, :], ("STATE",), ("STATEB",))

                def stage2(c):
                    c0 = c * Lc
                    cc = slice(c0, c0 + Lc)
                    SM = SM2[c % 2]
                    smk = lambda i_: ("SM", c % 2, i_)
                    XDT = XDT2[c % 2]
                    xdk = ("XDT", c % 2)
                    CBM = CBM2[c % 2]
                    cbk = ("CBM", c % 2)
                    YOFF = YOFF2[c % 2]
                    yk = YK2[c % 2]
                    for hg in range(NG):
                        at = ATRI[hg % 2]
                        atk = ("ATRI", 0)
                        eg = ESEG[hg % 2]
                        egk = ("ESEG", hg % 2)
                        mt = MT[hg % 2]
                        mtk = ("MT", hg % 2)
                        tt("dve", at[0:L, :, 0:L], SM[0:L, 3, hg * 8:(hg + 1) * 8].unsqueeze(2).to_broadcast([L, 8, L]),
                           tri_f[0:L, 0:L].unsqueeze(1).to_broadcast([L, 8, L]), ALU.mult, (smk(3), "CF"), (atk,))
                        hper = min(max(1, 512 // L), 8)
                        for q0 in range(0, 8, hper):
                            b = 4 + (q0 // hper) % 2
                            mm(PSB(b)[0:L, 0:hper * L].rearrange("p (h i) -> p h i", h=hper), U_f[0:L, 0:L],
                               at[0:L, q0:q0 + hper, 0:L], True, True, (atk, "CF"), (("ps", b),))
                            act(eg[0:L, q0:q0 + hper, 0:L], PSB(b)[0:L, 0:hper * L].rearrange("p (h i) -> p h i", h=hper),
                                AF.Exp, (("ps", b),), (egk,))
                        tt("dve", mt[0:L, :, 0:L], eg[0:L, :, 0:L],
                           CBM[0:L, hg, 0:L].unsqueeze(1).to_broadcast([L, 8, L]), ALU.mult, (egk, cbk), (mtk,))
                        for h8 in range(8):
                            h = hg * 8 + h8
                            mm(PSB(hg)[0:L, h8 * 64:(h8 + 1) * 64], mt[0:L, h8, 0:L], XDT[0:L, h * 64:(h + 1) * 64],
                               True, True, (mtk, xdk), (("ps", hg),))
                        tt("dve", YB[0:L, hg * 512:(hg + 1) * 512], PSB(hg)[0:L, :], YOFF[0:L, hg * 512:(hg + 1) * 512],
                           ALU.add, (("ps", hg),) + yk, ("YB",))
                    for m in range(16):
                        b = 4 + m // 8
                        tp(PSBH(b)[:, (m % 8) * Lc:(m % 8) * Lc + L], YB[0:L, m * 128:(m + 1) * 128], ident_b[0:L, 0:L],
                           ("YB", "CB"), (("ps", b),))
                    for hb in range(2):
                        b = 4 + hb
                        cp("act", YT[:, hb * 8:(hb + 1) * 8, cc],
                           PSBH(b)[:, 0:8 * Lc].rearrange("p (m l) -> p m l", m=8), (("ps", b),), ("YT",))

                if prompt and not os.environ.get("MAMBA_NOPIPE"):
                    if nchunk:
                        stage1(0)
                    for c in range(nchunk):
                        if c + 1 < nchunk:
                            stage1(c + 1)
                        stage2(c)
                else:
                    for c in range(nchunk):
                        stage1(c)
                        stage2(c)
                        if not prompt:
                            store_state(ssm_s[c], SIO)
                cnt = 0
                for pc in range(8):
                    wp, kp = wload(win_d[:, :, pc * 256:(pc + 1) * 256], (KC, 256))
                    for mi in range(2):
                        m = pc * 2 + mi
                        b = cnt % 2
                        cvt = CVT[cnt % 2]
                        cvk = ("CVT", cnt % 2)
                        cnt += 1
                        for k in range(KC):
                            mm(PSB(b)[:, 0:TW], wp[:, k, mi * 128:(mi + 1) * 128], H[:, k, 0:TW], k == 0, k == KC - 1,
                               (kp, "Hm"), (("ps", b),))
                        zf = cvt.rearrange("p s l -> p (s l)")
                        act(zf[:, 0:TW], PSB(b)[:, 0:TW], AF.Silu, (("ps", b),), (cvk,))
                        tt("dve", YT[:, m, 0:TW], YT[:, m, 0:TW], zf[:, 0:TW], ALU.mult, ("YT", cvk), ("YT",))
                SQ2 = XBC
                act(SQ2[:, 0:16, 0:TW], YT[:, :, 0:TW], AF.Square, ("YT",), tuple(("XBC", m) for m in range(16)))
                for g in range(NG):
                    for c4 in range(4):
                        mm(PSB(g)[:, 0:TW], ones_b, SQ2[:, g * 4 + c4, 0:TW], c4 == 0, c4 == 3,
                           (("XBC", g * 4 + c4), "CB"), (("ps", g),))
                    rsg = CVT[g % 2].rearrange("p s l -> p (s l)")
                    rk = ("CVT", g % 2)
                    act(rsg[:, 0:TW], PSB(g)[:, 0:TW], AF.Sqrt, (("ps", g),), (rk,), bias=EPSB[:, 0:1], scale=1.0 / 512)
                    P.op("dve", lambda e, rsg=rsg: e.reciprocal(rsg[:, 0:TW], rsg[:, 0:TW]), (rk,), (rk,))
                    for c4 in range(4):
                        m = g * 4 + c4
                        stt(YT[:, m, 0:TW], YT[:, m, 0:TW], gcol(V_SSMN, m), rsg[:, 0:TW], ALU.mult, ALU.mult,
                            ("YT", rk, "VEC"), ("YT",))
                cnt = 0
                for m in range(KC):
                    wo_, ko = wload(wout_d[:, :, m * 128:(m + 1) * 128], (16, 128))
                    b = 6 + cnt % 2
                    cnt += 1
                    for k in range(16):
                        mm(PSB(b)[:, 0:TW], wo_[:, k, :], YT[:, k, 0:TW], k == 0, k == 15, (ko, "YT"), (("ps", b),))
                    tt("dve", X[:, m, cs], X[:, m, cs], PSB(b)[:, 0:TW], ALU.add, (("ps", b), ("X", t)), (("X", t),))
            if prompt and seq_last:
                store_state(ssm_p[bidx], SIO)
                for tt_ in range(3):
                    dst = conv_p[bidx, tt_, :].rearrange("(m p) -> p m", p=128)
                    P.dma("sp", lambda e, dst=dst, tt_=tt_: e.dma_start(
                        out=dst, in_=CTAIL[:, :, 0, tt_], allow_slow_non_contiguous=True), tuple(("CTAIL", m_) for m_ in range(24)), ())
            if not prompt:
                for s in range(NSB):
                    for tt_ in range(3):
                        dst = conv_s[s, tt_, :].rearrange("(m p) -> p m", p=128)
                        P.dma("sp", lambda e, dst=dst, s=s, tt_=tt_: e.dma_start(
                            out=dst, in_=CTAIL[:, :, s, tt_], allow_slow_non_contiguous=True), tuple(("CTAIL", m_) for m_ in range(24)), ())

        def store_state(dst3, SIO):
            for hp in range(16):
                b = 4 + (hp // 4) % 2
                tp(PSB(b)[:, (hp % 4) * 128:(hp % 4 + 1) * 128], STATE[:, hp * 128:(hp + 1) * 128], ident_f,
                   ("STATE", "CF"), (("ps", b),))
                if hp % 4 == 3:
                    q4 = hp // 4
                    cp("dve", SIO[:, q4 * 4:(q4 + 1) * 4, :], PSB(b).rearrange("p (a n) -> p a n", a=4), (("ps", b),),
                       ("YOFF",))
            for two in range(2):
                dst = dst3.rearrange("(hp two) p n -> two p hp n", two=2)[two]
                P.dma("sp", lambda e, dst=dst, two=two: e.dma_start(out=dst, in_=SIO[two * 64:(two + 1) * 64, :, :]),
                      ("YOFF",), ())

        ONEB = sb("ONEB", [128, 1], F32)
        memset("dve", ONEB[:, :], 1.0, ("ONEB",))

        def rmsnorm_tile_local(H, vbase, t, TW, SQ, RS, sqkeys=("SQ",)):
            cs = slice(t * TW, (t + 1) * TW)
            act(SQ[:, :, 0:TW], X[:, :, cs], AF.Square, (("X", t),), sqkeys)
            for k in range(KC):
                mm(PSB(7)[:, 0:TW], ones_b, SQ[:, k, 0:TW], k == 0, k == KC - 1, sqkeys + ("CB",), (("ps", 7),))
            act(RS[:, 0:TW], PSB(7)[:, 0:TW], AF.Sqrt, (("ps", 7),), ("RS",), bias=EPSB[:, 0:1], scale=1.0 / D)
            P.op("dve", lambda e: e.reciprocal(RS[:, 0:TW], RS[:, 0:TW]), ("RS",), ("RS",))
            for k in range(KC):
                stt(H[:, k, 0:TW], X[:, k, cs], gcol(vbase, k), RS[:, 0:TW], ALU.mult, ALU.mult,
                    (("X", t), "RS", "VEC"), ("Hm",))

        def attn(T, mode, pos0, kt_new=None, v_new=None):
            phase_begin()
            TW, NT = tiles_of(T)
            prompt = (mode == "p")
            H = W.alloc([KC, TW], BF16)
            SQ = W.alloc([KC, TW], BF16)
            RS = W.alloc([TW], F32)
            QT = W.alloc([16, TW], BF16, parts=64)
            OTS = W.alloc([8, TW], BF16)
            NEE, NSP, NWW, NAA, NVB = 3, 8, 2, 4, 4
            EE = [W.alloc([512], F32) for _ in range(NEE)]
            SP = [W.alloc([512], BF16) for _ in range(NSP)]
            WW = [W.alloc([512], F32) for _ in range(NWW)]
            AA = [W.alloc([512], BF16) for _ in range(NAA)]
            if not prompt:
                KSTG = [W.alloc([256], F32) for _ in range(2)]
                VSTG = [W.alloc([256], F32) for _ in range(2)]
                KTB = [W.alloc([NG, 128], BF16, parts=64) for _ in range(2)]
                VB = [W.alloc([256], BF16) for _ in range(NVB)]
            wq_d = w_q.rearrange("(k p) n -> p k n", p=128)
            for t in range(NT):
                cs = slice(t * TW, (t + 1) * TW)
                rmsnorm_tile_local(H, V_MIX + 8, t, TW, SQ, RS)
                cnt = 0
                for pc in range(4):
                    wp, kp = wload(wq_d[:, :, pc * 256:(pc + 1) * 256], (KC, 256))
                    for hi in range(4):
                        h = pc * 4 + hi
                        b = cnt % 2
                        cnt += 1
                        for k in range(KC):
                            mm(PSB(b)[0:64, 0:TW], wp[:, k, hi * 64:(hi + 1) * 64], H[:, k, 0:TW], k == 0, k == KC - 1,
                               (kp, "Hm"), (("ps", b),))
                        P.op("act", lambda e, h=h, b=b: e.mul(QT[:, h, 0:TW], PSB(b)[0:64, 0:TW], 0.125),
                             (("ps", b),), ("QT",))
                units = []
                if prompt:
                    for qb in range(TW // 128):
                        gq = (pos0 + t * TW) // 128 + qb
                        for kb in range(gq, -1, -1):
                            for h in range(NG):
                                units.append({"chain": (qb, h), "cidx": h, "first": kb == gq, "last": kb == 0,
                                              "diag": kb == gq, "h": h, "qb": qb, "kb": kb, "nk": 128})
                else:
                    nkb = PAST // 128
                    for s0 in range(0, NSB, 2):
                        for kb in range(nkb, -1, -1):
                            for s in range(s0, min(s0 + 2, NSB)):
                                units.append({"chain": (s,), "cidx": s - s0, "first": kb == nkb, "last": kb == 0,
                                              "diag": kb == nkb, "s": s, "kb": kb, "nk": 32 if kb == nkb else 128})
                NU = len(units)
                lastseen = {}
                for i, u in enumerate(units):
                    u["prev"] = lastseen.get(u["chain"])
                    lastseen[u["chain"]] = i

                def zbank(i):
                    return i % 2

                def pbank(u):
                    return 2 + u["cidx"]

                def emit_load(i):
                    u = units[i]
                    if prompt or u["diag"]:
                        return
                    s, kb = u["s"], u["kb"]
                    ks, vs = KSTG[i % 2], VSTG[i % 2]
                    kk_, vk_ = ("KSTG", i % 2), ("VSTG", i % 2)
                    srck = ck[s, kb * 128:(kb + 1) * 128, :]
                    srcv = cv[s, kb * 128:(kb + 1) * 128, :]
                    P.dma("sp", lambda e, ks=ks, srck=srck: e.dma_start(out=ks, in_=srck), (), (kk_,))
                    P.dma("sp", lambda e, vs=vs, srcv=srcv: e.dma_start(out=vs, in_=srcv), (), (vk_,))
                    for h in range(NG):
                        tp(PSB(zbank(i))[0:64, h * 128:(h + 1) * 128], ks[:, h * 64:(h + 1) * 64], ident_f, (kk_, "CF"),
                           (("ps", zbank(i)),))
                    cp("dve", KTB[i % 2][:, :, :], PSB(zbank(i))[0:64, :].rearrange("p (h k) -> p h k", h=NG),
                       (("ps", zbank(i)),), (("KTB", i % 2),))
                    cp("act", VB[i % NVB][:, :], vs, (vk_,), (("VB", i % NVB),))

                def kview(u, i, h):
                    if prompt:
                        return KT[:, h, u["kb"] * 128:(u["kb"] + 1) * 128], ("KT",)
                    if u["diag"]:
                        return kt_new[:, h, u["s"] * 32:(u["s"] + 1) * 32], ("KT",)
                    return KTB[i % 2][:, h, :], (("KTB", i % 2),)

                def vview(u, i, h):
                    if prompt:
                        return VV[:, u["kb"], h * 64:(h + 1) * 64], ("VV",)
                    if u["diag"]:
                        return v_new[:, u["s"], h * 64:(h + 1) * 64], ("VV",)
                    return VB[i % NVB][:, h * 64:(h + 1) * 64], (("VB", i % NVB),)

                def maskmul(buf, key, nk):
                    if prompt:
                        msk = cmask_b[0:nk, 0:128].unsqueeze(1).to_broadcast([nk, 4, 128])
                        v3 = buf[0:nk, :].rearrange("p (g q) -> p g q", g=4)
                    else:
                        msk = cmask_b[0:nk, 0:32].unsqueeze(1).to_broadcast([nk, 16, 32])
                        v3 = buf[0:nk, :].rearrange("p (g q) -> p g q", g=16)
                    tt("dve", v3, v3, msk, ALU.mult, (key, "CB"), (key,))

                def emit_Z(i):
                    u = units[i]
                    nk = u["nk"]
                    b = zbank(i)
                    emit_load(i)
                    if prompt:
                        h, qb = u["h"], u["qb"]
                        lhs, lk = kview(u, i, h)
                        mm(PSB(b)[0:nk, :].rearrange("p (g q) -> p g q", g=4), lhs,
                           QT[:, h * 4:(h + 1) * 4, qb * 128:(qb + 1) * 128], True, True, lk + ("QT",), (("ps", b),))
                    else:
                        s = u["s"]
                        for h in range(NG):
                            lhs, lk = kview(u, i, h)
                            mm(PSB(b)[0:nk, h * 128:(h + 1) * 128].rearrange("p (g q) -> p g q", g=4), lhs,
                               QT[:, h * 4:(h + 1) * 4, s * 32:(s + 1) * 32], True, True, lk + ("QT",), (("ps", b),))
                    ee, sp_ = EE[i % NEE], SP[i % NSP]
                    act(ee[0:nk, :], PSB(b)[0:nk, :], AF.Exp, (("ps", b),), (("EE", i % NEE),))
                    act(sp_[0:nk, :], ee[0:nk, :], AF.Ln, (("EE", i % NEE),), (("SP", i % NSP),), bias=ONEB[0:nk, 0:1])
                    if u["diag"]:
                        maskmul(sp_, ("SP", i % NSP), nk)

                def emit_C(i):
                    u = units[i]
                    nk = u["nk"]
                    pb = pbank(u)
                    sp_ = SP[i % NSP]
                    if u["first"]:
                        mm(PSB(pb)[0:128, :], trige_b[0:nk, 0:128], sp_[0:nk, :], True, True, (("SP", i % NSP), "CB"),
                           (("ps", pb),))
                    else:
                        pi = u["prev"]
                        pnk = units[pi]["nk"]
                        spp = SP[pi % NSP]
                        mm(PSB(pb)[0:128, :], compl_b[0:pnk, 0:128], spp[0:pnk, :], False, False,
                           (("SP", pi % NSP), "CB"), (("ps", pb),), skip=True)
                        mm(PSB(pb)[0:128, :], trige_b[0:nk, 0:128], sp_[0:nk, :], False, True, (("SP", i % NSP), "CB"),
                           (("ps", pb),), skip=True)
                    ww = WW[i % NWW]
                    act(ww[0:nk, :], PSB(pb)[0:nk, :], AF.Exp, (("ps", pb),), (("WW", i % NWW),), scale=-1.0)
                    aa = AA[i % NAA]
                    tt("dve", aa[0:nk, :], EE[i % NEE][0:nk, :], ww[0:nk, :], ALU.mult,
                       (("EE", i % NEE), ("WW", i % NWW)), (("AA", i % NAA),))
                    if u["diag"]:
                        maskmul(aa, ("AA", i % NAA), nk)

                def emit_O(i):
                    u = units[i]
                    nk = u["nk"]
                    aa = AA[i % NAA]
                    if prompt:
                        h = u["h"]
                        ob, ph = 6 + h // 2, h % 2
                        rv, rk = vview(u, i, h)
                        mm(PSB(ob)[ph * 64:(ph + 1) * 64, :], rv, aa[0:nk, :], u["first"], u["last"],
                           rk + (("AA", i % NAA),), (("ps", ob),))
                        if u["last"]:
                            qb = u["qb"]
                            j0 = (h // 2) * 4
                            cp("act", OTS[ph * 64:(ph + 1) * 64, j0:j0 + 4, qb * 128:(qb + 1) * 128],
                               PSB(ob)[ph * 64:(ph + 1) * 64, :].rearrange("p (g q) -> p g q", g=4), (("ps", ob),),
                               ("OTS",))
                    else:
                        s = u["s"]
                        ob = 6 + u["cidx"]
                        for h in range(NG):
                            ph = h % 2
                            rv, rk = vview(u, i, h)
                            c0 = (h // 2) * 128
                            mm(PSB(ob)[ph * 64:(ph + 1) * 64, c0:c0 + 128], rv, aa[0:nk, h * 128:(h + 1) * 128],
                               u["first"] and h // 2 == 0, u["last"], rk + (("AA", i % NAA),), (("ps", ob),), skip=True)
                        if u["last"]:
                            cp("act", OTS[:, :, s * 32:(s + 1) * 32],
                               PSB(ob)[:, 0:256].rearrange("p (j q) -> p j q", j=8), (("ps", ob),), ("OTS",))

                import os
                nwarm = int(os.environ.get("NWARM", "0"))
                wevery = int(os.environ.get("WEVERY", "0"))
                for it in range(NU + 3):
                    if nwarm and (it == 0 or (wevery and it % wevery == 0)) and it < NU:
                        for _ in range(nwarm):
                            mm(PSB(zbank(it))[:, 0:TW], ones_b, H[:, 0, 0:TW], True, True, ("CB", "Hm"),
                               (("ps", zbank(it)),))
                    if 0 <= it - 3 < NU:
                        emit_O(it - 3)
                    if it < NU:
                        emit_Z(it)
                    if 0 <= it - 1 < NU:
                        emit_C(it - 1)
                wo_d = w_o.rearrange("(a two g p) n -> two p a g n", two=2, g=4, p=64)
                cnt = 0
                for m in range(KC):
                    sl = ring_state["n"] % nslot
                    ring_state["n"] += 1
                    ko = ("ring", sl)
                    for two in range(2):
                        for a_ in range(2):
                            dst = RING[two * 64:(two + 1) * 64, sl, a_ * 512:(a_ + 1) * 512].rearrange(
                                "p (g n) -> p g n", g=4)
                            src = wo_d[two][:, a_, :, m * 128:(m + 1) * 128]
                            P.dma("pool", lambda e, dst=dst, src=src: e.dma_start(out=dst, in_=src), (), (ko,),
                                  nobarrier=True)
                    wo_ = RING[:, sl, 0:1024].rearrange("p (j n) -> p j n", j=8)
                    b = cnt % 2
                    cnt += 1
                    for j in range(8):
                        mm(PSB(b)[:, 0:TW], wo_[:, j, :], OTS[:, j, 0:TW], j == 0, j == 7, (ko, "OTS"), (("ps", b),))
                    tt("dve", X[:, m, cs], X[:, m, cs], PSB(b)[:, 0:TW], ALU.add, (("ps", b), ("X", t)), (("X", t),))

        KTN = KT[:, :, 0:128]
        VN = VV[0:32, 0:4, :]

        def run_pass(mode, bidx, half):
            prompt = mode == "p"
            T = PASS_T if prompt else TS
            pos0 = half * PASS_T if prompt else PAST
            if prompt:
                xrows = xp[bidx, pos0:pos0 + T, :]
                yrows = y_p[bidx, pos0:pos0 + T, :]
                prow = [pp[i, bidx, pos0:pos0 + T, :] for i in range(2)]
            else:
                xrows, yrows = xs, y_s
                prow = [psm[i] for i in range(2)]
            load_x(xrows, T)
            if "ffn" in phases:
                ffn(0, 0, T)
            if "mamba" in phases:
                mamba(T, mode, half == 0, pos0 + T == S, bidx)
            if "ffn" in phases:
                ffn(0, 1, T)
            if "ple" in phases:
                ple(0, prow[0], T)
            if "kv" in phases:
                if prompt:
                    kv(T, pos0, k_p[bidx, pos0:pos0 + T, :], v_p[bidx, pos0:pos0 + T, :], KT[:, :, pos0:pos0 + T],
                       lambda blk: VV[:, pos0 // 128 + blk, :])
                else:
                    kv(T, pos0, k_s, v_s, KTN[:, :, 0:T], lambda blk: VN[:, blk, :], RB=32)
            if "ffn" in phases:
                ffn(1, 0, T)
            if "attn" in phases:
                if prompt:
                    attn(T, mode, pos0)
                else:
                    attn(T, mode, pos0, KTN, VN)
            if "ffn" in phases:
                ffn(1, 1, T)
            if "ple" in phases:
                ple(1, prow[1], T)
            store_y(yrows, T)

        for b in range(NPB):
            for half in range(S // PASS_T):
                run_pass("p", b, half)
        if NSB > 0:
            run_pass("s", 0, 0)

        P.barrier()
        P.emit(block, sems, ring_sems)
    return nc


_W_NAMES = {
    "w_gate": "ffn_w_gate", "w_up": "ffn_w_up", "w_down": "ffn_w_down",
}


def make_in_map(inp, pb, sb_, cf, cb, vec):
    f = lambda a: np.ascontiguousarray(a, dtype=np.float32)
    xs = f(inp["x_sample"][sb_])
    ps_ = f(inp["p_sample"][:, sb_])
    m = {
        "xp": f(inp["x_prompt"][pb]),
        "xs": xs.reshape(-1, D),
        "pp": f(inp["p_prompt"][:, pb]),
        "psm": ps_.reshape(2, -1, PLE),
        "sssm": f(inp["state_ssm"][0, sb_]),
        "sconv": f(inp["state_conv"][0, sb_]),
        "ck": f(inp["cache_k"][sb_]).reshape(xs.shape[0], -1, 256),
        "cv": f(inp["cache_v"][sb_]).reshape(xs.shape[0], -1, 256),
        "w_gate": f(inp["ffn_w_gate"]), "w_up": f(inp["ffn_w_up"]), "w_down": f(inp["ffn_w_down"]),
        "w_in": f(inp["ssm_w_in"][0]), "w_out": f(inp["ssm_w_out"][0]),
        "w_k": f(inp["w_k"]), "w_v": f(inp["w_v"]), "w_q": f(inp["sb_w_q"][0]), "w_o": f(inp["sb_w_o"][0]),
        "w_pg": f(inp["ple_w_gate"]), "w_pp": f(inp["ple_w_proj"]),
        "cf": cf, "cb": cb, "vec": vec,
    }
    return m


def run(inp, n_cores, phases=("ffn", "mamba", "ple", "kv", "attn"), trace=False):
    inp = {k: np.asarray(v) for k, v in inp.items()}
    B, S = inp["x_prompt"].shape[0], inp["x_prompt"].shape[1]
    BS = inp["x_sample"].shape[0]
    PAST = inp["cache_k"].shape[1]
    NPB, NSB = B // n_cores, BS // n_cores
    nc = build_program(NPB, S, NSB, PAST, phases=phases)
    cf, cb = _const_tables()
    vec = _vec_table(inp)
    in_maps = []
    for c in range(n_cores):
        in_maps.append(make_in_map(inp, slice(c * NPB, (c + 1) * NPB), slice(c * NSB, (c + 1) * NSB), cf, cb, vec))
    res = run_bass_kernel_spmd(nc, in_maps, core_ids=list(range(n_cores)), trace=trace)
    R = res.results
    cat = lambda k: np.concatenate([r[k] for r in R], axis=0)
    y_p = cat("y_p")
    y_s = cat("y_s").reshape(BS, 32, D)
    ssm_p = cat("ssm_p")[None]
    conv_p = cat("conv_p")[None]
    k_p = cat("k_p").reshape(B, S, 4, 64)
    v_p = cat("v_p").reshape(B, S, 4, 64)
    ssm_s = cat("ssm_s")[None]
    conv_s = cat("conv_s")[None]
    k_s = cat("k_s").reshape(BS, 32, 4, 64)
    v_s = cat("v_s").reshape(BS, 32, 4, 64)
    outs = (y_p, y_s, ssm_p, conv_p, k_p, v_p, ssm_s, conv_s, k_s, v_s)
    outs = tuple(np.ascontiguousarray(o, dtype=np.float32) for o in outs)
    return outs, res


def kernel(**inputs):
    outs, _ = run(inputs, 8)
    return outs
```

```python
import numpy as np
import concourse.bass as bass
import concourse.mybir as mybir
from concourse.bass_utils import run_bass_kernel_spmd

F32 = mybir.dt.float32
BF16 = mybir.dt.bfloat16
AF = mybir.ActivationFunctionType
ALU = mybir.AluOpType

D = 1024
KC = 8
DFF = 2816
FC = 22
DIN = 2048
NH = 32
HD = 64
NG = 4
DST = 128
CONVD = 3072
INDIM = 5152
PLE = 256
EPS = 1e-6
SLOT = 2816
PASS_T = 1024


class Prog:
    ENG = ("pe", "act", "dve", "pool", "sp")

    def __init__(self, nc, ring_k=8):
        self.nc = nc
        self.ops = {e: [] for e in self.ENG}
        self.last_w = {}
        self.readers = {}
        self.rings = {"sp": {"n": 0, "K": ring_k}, "pool": {"n": 0, "K": ring_k}}
        self.gdeps = set()

    def _deps(self, eng, reads, writes, is_dma, nobarrier):
        deps = set()
        for k in reads:
            lw = self.last_w.get(k)
            if lw is not None:
                deps.add(lw)
        for k in writes:
            lw = self.last_w.get(k)
            if lw is not None and not (lw[0] == "eng" and lw[1] == eng and not is_dma):
                deps.add(lw)
            for r in self.readers.get(k, ()):
                if not (r[0] == "eng" and r[1] == eng and not is_dma):
                    deps.add(r)
        if not nobarrier:
            deps |= self.gdeps
        return deps

    def op(self, eng, fn, reads=(), writes=(), nobarrier=False):
        psr = tuple(k for k in reads if isinstance(k, tuple) and k[0] == "ps" and k not in writes)
        writes = tuple(writes) + psr
        deps = self._deps(eng, reads, writes, False, nobarrier)
        idx = len(self.ops[eng])
        ref = ("eng", eng, idx)
        self.ops[eng].append({"fn": fn, "deps": deps, "dma": None, "inc": False})
        self._record(ref, reads, writes)
        return ref

    def dma(self, q, fn, reads=(), writes=(), nobarrier=False):
        ring = self.rings[q]
        n = ring["n"]
        ring["n"] += 1
        deps = self._deps(q, reads, writes, True, nobarrier)
        if n >= ring["K"]:
            deps.add(("dma", q, n - ring["K"]))
        ref = ("dma", q, n)
        self.ops[q].append({"fn": fn, "deps": deps, "dma": n, "inc": False})
        self._record(ref, reads, writes)
        return ref

    def _record(self, ref, reads, writes):
        for k in reads:
            self.readers.setdefault(k, []).append(ref)
        for k in writes:
            self.last_w[k] = ref
            self.readers[k] = []

    def barrier(self):
        g = set()
        for e in self.ENG:
            for i in range(len(self.ops[e]) - 1, -1, -1):
                if self.ops[e][i]["dma"] is None:
                    g.add(("eng", e, i))
                    break
        ring = self.rings["sp"]
        for n in range(max(0, ring["n"] - ring["K"]), ring["n"]):
            g.add(("dma", "sp", n))
        self.gdeps = g

    def emit(self, block, sems, ring_sems):
        for e in self.ENG:
            for o in self.ops[e]:
                for d in o["deps"]:
                    if d[0] == "eng":
                        self.ops[d[1]][d[2]]["inc"] = True
        counts = {}
        for e in self.ENG:
            c = 0
            cl = []
            for o in self.ops[e]:
                if o["inc"] and o["dma"] is None:
                    c += 1
                cl.append(c)
            counts[e] = cl
        rings = self.rings

        def resolve(d):
            if d[0] == "eng":
                return (("e", d[1]), counts[d[1]][d[2]])
            K = rings[d[1]]["K"]
            return (("r", d[1], d[2] % K), 16 * (d[2] // K + 1))

        def semof(key):
            if key[0] == "e":
                return sems[key[1]]
            return ring_sems[key[1]][key[2]]

        def run(ename, engine):
            known = {}
            for o in self.ops[ename]:
                need = {}
                for d in o["deps"]:
                    k, v = resolve(d)
                    if v > need.get(k, 0):
                        need[k] = v
                for k, v in need.items():
                    if v > known.get(k, 0):
                        engine.wait_ge(semof(k), v)
                        known[k] = v
                inst = o["fn"](engine)
                if o["dma"] is not None:
                    K = rings[ename]["K"]
                    inst.then_inc(ring_sems[ename][o["dma"] % K], 16)
                elif o["inc"]:
                    inst.then_inc(sems[ename], 1)
            if ename == "sp":
                ring = rings["sp"]
                K = ring["K"]
                for n in range(max(0, ring["n"] - K), ring["n"]):
                    k, v = resolve(("dma", "sp", n))
                    if v > known.get(k, 0):
                        engine.wait_ge(semof(k), v)
                        known[k] = v

        @block.tensor
        def _(e):
            run("pe", e)

        @block.scalar
        def _(e):
            run("act", e)

        @block.vector
        def _(e):
            run("dve", e)

        @block.gpsimd
        def _(e):
            run("pool", e)

        @block.sync
        def _(e):
            run("sp", e)


class Arena:
    def __init__(self, ap_bf16, nbytes):
        self.ap = ap_bf16
        self.nbytes = nbytes
        self.off = 0
        self.peak = 0

    def reset(self):
        self.off = 0

    def alloc(self, shape, dtype, parts=128):
        esz = 4 if dtype == F32 else 2
        n = 1
        for s in shape:
            n *= s
        nb = (n * esz + 31) // 32 * 32
        assert self.off + nb <= self.nbytes, f"arena overflow {self.off}+{nb}>{self.nbytes}"
        v = self.ap[0:parts, self.off // 2:(self.off + n * esz) // 2]
        self.off += nb
        self.peak = max(self.peak, self.off)
        if dtype == F32:
            v = v.bitcast(F32)
        if len(shape) == 2:
            v = v.rearrange("p (a b) -> p a b", a=shape[0])
        elif len(shape) == 3:
            v = v.rearrange("p (a b c) -> p a b c", a=shape[0], b=shape[1])
        return v


def _const_tables():
    i = np.arange(128)
    r, c = i[:, None], i[None, :]
    ident = (r == c)
    U = (r > c)
    tri = (r <= c)
    ones = np.ones((128, 128), bool)
    cf = np.concatenate([ident, U, tri, ones], axis=1).astype(np.float32)
    triGE = (r >= c)
    compl = (r < c)
    cmask = (r < c)
    cbmask = (c >= r)
    cb = np.concatenate([ident, ones, triGE, compl, cmask, cbmask], axis=1).astype(np.float32)
    return cf, cb


CF_IDENT, CF_U, CF_TRI, CF_ONES = 0, 128, 256, 384
CB_IDENT, CB_ONES, CB_TRIGE, CB_COMPL, CB_CMASK, CB_CBMASK = 0, 128, 256, 384, 512, 640
V_FFN = 0
V_MIX = 32
V_KV = 48
V_PLE = 56
V_FIN = 72
V_SSMN = 80
V_CW = 96
V_CB = 192
V_DTB = 216
V_ALOG = 248
V_D = 280
V_DFM = 312
NV = 328


def _vec_table(inp):
    def fm(v):
        v = np.asarray(v, np.float32).reshape(-1, 128)
        return v.T
    cols = []
    for i in range(2):
        for j in range(2):
            cols.append(fm(inp["ffn_norm"][i, j]))
    for i in range(2):
        cols.append(fm(inp["mix_norm"][i]))
    cols.append(fm(inp["kv_norm"]))
    for i in range(2):
        cols.append(fm(inp["ple_norm"][i]))
    cols.append(fm(inp["final_norm"]))
    cols.append(fm(inp["ssm_norm"][0]))
    for k in range(4):
        cols.append(fm(inp["ssm_conv_w"][0, k]))
    cols.append(fm(inp["ssm_conv_b"][0]))
    for nm in ("ssm_dt_bias", "ssm_a_log", "ssm_d"):
        cols.append(np.broadcast_to(np.asarray(inp[nm][0], np.float32)[None, :], (128, 32)))
    cols.append(fm(np.repeat(np.asarray(inp["ssm_d"][0], np.float32), 64)))
    out = np.ascontiguousarray(np.concatenate(cols, axis=1), dtype=np.float32)
    assert out.shape == (128, NV)
    return out


def build_program(NPB, S, NSB, PAST, phases=("ffn", "mamba", "ple", "kv", "attn"), nslot=5, work_kb=105.5):
    assert S % PASS_T == 0 and PAST % 128 == 0
    nc = bass.Bass("TRN2", target_bir_lowering=False)
    TS = NSB * 32

    def din(name, shape):
        return nc.dram_tensor(name, list(shape), F32, kind="ExternalInput").ap()

    def dout(name, shape):
        return nc.dram_tensor(name, list(shape), F32, kind="ExternalOutput").ap()

    xp = din("xp", [NPB, S, D])
    xs = din("xs", [TS, D])
    pp = din("pp", [2, NPB, S, PLE])
    psm = din("psm", [2, TS, PLE])
    sssm = din("sssm", [NSB, NH, HD, DST])
    sconv = din("sconv", [NSB, 3, CONVD])
    ck = din("ck", [NSB, PAST, 256])
    cv = din("cv", [NSB, PAST, 256])
    w_gate = din("w_gate", [2, 2, D, DFF])
    w_up = din("w_up", [2, 2, D, DFF])
    w_down = din("w_down", [2, 2, DFF, D])
    w_in = din("w_in", [D, INDIM])
    w_out = din("w_out", [DIN, D])
    w_k = din("w_k", [D, 256])
    w_v = din("w_v", [D, 256])
    w_q = din("w_q", [D, D])
    w_o = din("w_o", [D, D])
    w_pg = din("w_pg", [2, D, D])
    w_pp = din("w_pp", [2, PLE, D])
    cf_d = din("cf", [128, 512])
    cb_d = din("cb", [128, 768])
    vec_d = din("vec", [128, NV])

    y_p = dout("y_p", [NPB, S, D])
    y_s = dout("y_s", [TS, D])
    ssm_p = dout("ssm_p", [NPB, NH, HD, DST])
    conv_p = dout("conv_p", [NPB, 3, CONVD])
    k_p = dout("k_p", [NPB, S, 256])
    v_p = dout("v_p", [NPB, S, 256])
    ssm_s = dout("ssm_s", [NSB, NH, HD, DST])
    conv_s = dout("conv_s", [NSB, 3, CONVD])
    k_s = dout("k_s", [TS, 256])
    v_s = dout("v_s", [TS, 256])

    WORKB = int(work_kb * 1024)
    import contextlib
    es = contextlib.ExitStack()
    with es:
        def sb(name, shape, dt):
            return es.enter_context(nc.sbuf_tensor(name, list(shape), dt))

        X = sb("X", [128, KC, PASS_T], F32)
        CF = sb("CF", [128, 512], F32)
        CB = sb("CB", [128, 768], BF16)
        VEC = sb("VEC", [128, NV], F32)
        ABC = sb("ABC", [128, 32], F32)
        RING = sb("RING", [128, nslot, SLOT], BF16)
        STATE = sb("STATE", [128, DIN], F32)
        STATEB = sb("STATEB", [128, DIN], BF16)
        CTAIL = sb("CTAIL", [128, 24, 4, 3], F32)
        KT = sb("KT", [64, NG, S], BF16)
        VV = sb("VV", [128, S // 128, 256], BF16)
        WORK = sb("WORK", [128, WORKB // 2], BF16)
        PS = es.enter_context(nc.psum_tensor("PS", [128, 8, 512], F32))
        sems = {e: es.enter_context(nc.semaphore("s_" + e)) for e in Prog.ENG}
        P = Prog(nc, ring_k=8)
        ring_sems = {q: [es.enter_context(nc.semaphore(f"r_{q}{i}")) for i in range(8)] for q in ("sp", "pool")}
        block = es.enter_context(nc.Block())
        W = Arena(WORK, WORKB)

        ident_f = CF[:, CF_IDENT:CF_IDENT + 128]
        U_f = CF[:, CF_U:CF_U + 128]
        tri_f = CF[:, CF_TRI:CF_TRI + 128]
        ones_f = CF[:, CF_ONES:CF_ONES + 128]
        ident_b = CB[:, CB_IDENT:CB_IDENT + 128]
        ones_b = CB[:, CB_ONES:CB_ONES + 128]
        trige_b = CB[:, CB_TRIGE:CB_TRIGE + 128]
        compl_b = CB[:, CB_COMPL:CB_COMPL + 128]
        cmask_b = CB[:, CB_CMASK:CB_CMASK + 128]
        cbmask_b = CB[:, CB_CBMASK:CB_CBMASK + 128]

        def PSB(b):
            return PS[:, b, :]

        def PSBH(b):
            return PS[:, b, :].bitcast(BF16)

        def mm(out, lhsT, rhs, start, stop, reads, writes, skip=False):
            if skip:
                P.op("pe", lambda e: e.matmul(out, lhsT, rhs, start=start, stop=stop, skip_group_check=True),
                     reads, writes)
            else:
                P.op("pe", lambda e: e.matmul(out, lhsT, rhs, start=start, stop=stop), reads, writes)

        def tp(out, in_, ident, reads, writes):
            P.op("pe", lambda e: e.transpose(out, in_, ident), reads, writes)

        def act(out, in_, func, reads, writes, bias=None, scale=None):
            kw = {}
            if bias is not None:
                kw["bias"] = bias
            if scale is not None:
                kw["scale"] = scale
            P.op("act", lambda e: e.activation(out, in_, func, **kw), reads, writes)

        def tt(eng, out, in0, in1, op, reads, writes):
            P.op(eng, lambda e: e.tensor_tensor(out, in0, in1, op), reads, writes)

        def ts(eng, out, in0, s1, s2, op0, op1, reads, writes):
            if op1 is None:
                P.op(eng, lambda e: e.tensor_scalar(out, in0, s1, None, op0), reads, writes)
            else:
                P.op(eng, lambda e: e.tensor_scalar(out, in0, s1, s2, op0, op1), reads, writes)

        def stt(out, in0, scalar, in1, op0, op1, reads, writes):
            P.op("dve", lambda e: e.scalar_tensor_tensor(out, in0, scalar, in1, op0, op1), reads, writes)

        def cp(eng, out, in_, reads, writes):
            if eng == "act":
                P.op("act", lambda e: e.copy(out, in_), reads, writes)
            else:
                P.op(eng, lambda e: e.tensor_copy(out, in_), reads, writes)

        def memset(eng, ap, val, writes):
            P.op(eng, lambda e: e.memset(ap, val), (), writes)

        ring_state = {"n": 0}

        def wload(src_ap, shape, parts=128):
            s = ring_state["n"] % nslot
            ring_state["n"] += 1
            a, b = shape
            assert a * b <= SLOT
            dst = RING[0:parts, s, 0:a * b].rearrange("p (a b) -> p a b", a=a)
            key = ("ring", s)
            P.dma("pool", lambda e: e.dma_start(out=dst, in_=src_ap), (), (key,), nobarrier=True)
            return dst, key

        P.dma("sp", lambda e: e.dma_start(out=CF[:, :], in_=cf_d[:, :]), (), ("CF",))
        P.dma("sp", lambda e: e.dma_start(out=VEC[:, :], in_=vec_d[:, :]), (), ("VEC",))
        P.dma("pool", lambda e: e.dma_start(out=CB[:, :], in_=cb_d[:, :]), (), ("CB",))
        act(ABC[:, :], VEC[:, V_ALOG:V_ALOG + 32], AF.Exp, ("VEC",), ("ABC",))
        ts("dve", ABC[:, :], ABC[:, :], -1.0, None, ALU.mult, None, ("ABC",), ("ABC",))
        CONSTK = ("CF", "CB", "VEC", "ABC")

        def gcol(base, k):
            return VEC[:, base + k:base + k + 1]

        def phase_begin():
            P.barrier()
            W.reset()

        def tiles_of(T):
            TW = min(512, T)
            return TW, T // TW

        def rmsnorm_tile(H, vbase, t, TW, SQ, RS, psb):
            cs = slice(t * TW, (t + 1) * TW)
            act(SQ[:, :, 0:TW], X[:, :, cs], AF.Square, (("X", t),), ("SQ",))
            for k in range(KC):
                mm(PSB(psb)[:, 0:TW], ones_b, SQ[:, k, 0:TW], k == 0, k == KC - 1, ("SQ", "CB"), (("ps", psb),))
            act(RS[:, 0:TW], PSB(psb)[:, 0:TW], AF.Sqrt, (("ps", psb),), ("RS",), bias=EPSB[:, 0:1], scale=1.0 / D)
            P.op("dve", lambda e: e.reciprocal(RS[:, 0:TW], RS[:, 0:TW]), ("RS",), ("RS",))
            for k in range(KC):
                stt(H[:, k, cs], X[:, k, cs], gcol(vbase, k), RS[:, 0:TW], ALU.mult, ALU.mult,
                    (("X", t), "RS", "VEC"), (("H", t),))

        EPSB = sb("EPSB", [128, 1], F32)
        memset("dve", EPSB[:, :], EPS, ("EPSB",))
        HALFB = sb("HALFB", [128, 1], F32)
        memset("dve", HALFB[:, :], 0.5, ("HALFB",))

        def load_x(src_rows, T):
            phase_begin()
            STG = [W.alloc([D], F32) for _ in range(2)]
            for blk in range(T // 128):
                st = STG[blk % 2]
                sk = ("xstg", blk % 2)
                src = src_rows[blk * 128:(blk + 1) * 128, :]
                P.dma("sp", lambda e, st=st, src=src: e.dma_start(out=st, in_=src), (), (sk,))
                for half in range(2):
                    b = (blk * 2 + half) % 4
                    for j in range(4):
                        k = half * 4 + j
                        tp(PSB(b)[:, j * 128:(j + 1) * 128], st[:, k * 128:(k + 1) * 128], ident_f,
                           (sk, "CF"), (("ps", b),))
                    cp("act" if half == 0 else "dve",
                       X[:, half * 4:half * 4 + 4, blk * 128:(blk + 1) * 128],
                       PSB(b).rearrange("p (a b) -> p a b", a=4), (("ps", b),), (("X", blk // 4),))

        def store_y(dst_rows, T):
            phase_begin()
            TW, NT = tiles_of(T)
            SQ = W.alloc([KC, TW], BF16)
            RS = W.alloc([TW], F32)
            YN = W.alloc([KC, TW], F32)
            STG = [W.alloc([D], F32) for _ in range(2)]
            for t in range(NT):
                cs = slice(t * TW, (t + 1) * TW)
                act(SQ[:, :, 0:TW], X[:, :, cs], AF.Square, (("X", t),), ("SQ",))
                for k in range(KC):
                    mm(PSB(7)[:, 0:TW], ones_b, SQ[:, k, 0:TW], k == 0, k == KC - 1, ("SQ", "CB"), (("ps", 7),))
                act(RS[:, 0:TW], PSB(7)[:, 0:TW], AF.Sqrt, (("ps", 7),), ("RS",), bias=EPSB[:, 0:1], scale=1.0 / D)
                P.op("dve", lambda e: e.reciprocal(RS[:, 0:TW], RS[:, 0:TW]), ("RS",), ("RS",))
                for k in range(KC):
                    stt(YN[:, k, 0:TW], X[:, k, cs], gcol(V_FIN, k), RS[:, 0:TW], ALU.mult, ALU.mult,
                        (("X", t), "RS", "VEC"), ("YN",))
                for bl in range(TW // 128):
                    gb = t * (TW // 128) + bl
                    st = STG[gb % 2]
                    sk = ("ystg", gb % 2)
                    for half in range(2):
                        b = (gb * 2 + half) % 4
                        for j in range(4):
                            k = half * 4 + j
                            tp(PSB(b)[:, j * 128:(j + 1) * 128], YN[:, k, bl * 128:(bl + 1) * 128], ident_f,
                               ("YN", "CF"), (("ps", b),))
                        cp("act" if half == 0 else "dve", st[:, half * 512:(half + 1) * 512], PSB(b),
                           (("ps", b),), (sk,))
                    dst = dst_rows[gb * 128:(gb + 1) * 128, :]
                    P.dma("sp", lambda e, st=st, dst=dst: e.dma_start(out=dst, in_=st), (sk,), ())

        def ffn(i, j, T):
            phase_begin()
            TW, NT = tiles_of(T)
            H = W.alloc([KC, T], BF16)
            AB = W.alloc([FC, T], BF16)
            SQ = W.alloc([KC, TW], BF16)
            RS = W.alloc([TW], F32)
            SG = [W.alloc([TW], F32) for _ in range(2)]
            for t in range(NT):
                rmsnorm_tile(H, V_FFN + (i * 2 + j) * 8, t, TW, SQ, RS, 7)
            import os
            dbg = int(os.environ.get("FFN_DBG", "9"))
            if dbg < 1:
                return
            wg_d = w_gate[i, j].rearrange("(k p) n -> p k n", p=128)
            wu_d = w_up[i, j].rearrange("(k p) n -> p k n", p=128)
            wd_d = w_down[i, j].rearrange("(f p) n -> p f n", p=128)
            cnt = 0
            for fp in range(FC // 2):
                wg, kg = wload(wg_d[:, :, fp * 256:(fp + 1) * 256], (KC, 256))
                wu, ku = wload(wu_d[:, :, fp * 256:(fp + 1) * 256], (KC, 256))
                for t in range(NT):
                    cs = slice(t * TW, (t + 1) * TW)
                    for fi in range(2):
                        f = fp * 2 + fi
                        ba = (cnt * 2) % 6
                        bb = (cnt * 2 + 1) % 6
                        cnt += 1
                        for k in range(KC):
                            mm(PSB(ba)[:, 0:TW], wg[:, k, fi * 128:(fi + 1) * 128], H[:, k, cs], k == 0, k == KC - 1,
                               (kg, ("H", t)), (("ps", ba),))
                        for k in range(KC):
                            mm(PSB(bb)[:, 0:TW], wu[:, k, fi * 128:(fi + 1) * 128], H[:, k, cs], k == 0, k == KC - 1,
                               (ku, ("H", t)), (("ps", bb),))
                        sg = SG[cnt % 2]
                        sgk = ("SG", cnt % 2)
                        act(sg[:, 0:TW], PSB(ba)[:, 0:TW], AF.Silu, (("ps", ba),), (sgk,))
                        tt("dve", AB[:, f, cs], sg[:, 0:TW], PSB(bb)[:, 0:TW], ALU.mult, (sgk, ("ps", bb)),
                           (("AB", f, t),))
            if dbg < 2:
                return
            cnt = 0
            for m in range(KC):
                wd, kd = wload(wd_d[:, :, m * 128:(m + 1) * 128], (FC, 128))
                for t in range(NT):
                    cs = slice(t * TW, (t + 1) * TW)
                    b = 6 + (cnt % 2)
                    cnt += 1
                    if dbg < 3:
                        continue
                    for f in range(FC):
                        mm(PSB(b)[:, 0:TW], wd[:, f, :], AB[:, f, cs], f == 0, f == FC - 1,
                           (kd, ("AB", f, t)), (("ps", b),))
                    if dbg < 4:
                        continue
                    stt(X[:, m, cs], PSB(b)[:, 0:TW], HALFB[:, 0:1], X[:, m, cs], ALU.mult, ALU.add,
                        (("ps", b), ("X", t)), (("X", t),))

        def ple(i, prow, T):
            phase_begin()
            TW, NT = tiles_of(T)
            H = W.alloc([KC, T], BF16)
            PT = W.alloc([2, T], BF16)
            SQ = W.alloc([KC, TW], BF16)
            RS = W.alloc([TW], F32)
            SG = [W.alloc([TW], F32) for _ in range(2)]
            STG = [W.alloc([PLE], F32) for _ in range(2)]
            for t in range(NT):
                rmsnorm_tile(H, V_PLE + i * 8, t, TW, SQ, RS, 7)
            for blk in range(T // 128):
                st = STG[blk % 2]
                sk = ("pstg", blk % 2)
                src = prow[blk * 128:(blk + 1) * 128, :]
                P.dma("sp", lambda e, st=st, src=src: e.dma_start(out=st, in_=src), (), (sk,))
                b = 4 + blk % 2
                for k2 in range(2):
                    tp(PSB(b)[:, k2 * 128:(k2 + 1) * 128], st[:, k2 * 128:(k2 + 1) * 128], ident_f, (sk, "CF"),
                       (("ps", b),))
                cp("act", PT[:, :, blk * 128:(blk + 1) * 128],
                   PSB(b)[:, 0:256].rearrange("p (a b) -> p a b", a=2), (("ps", b),), ("PT",))
            wpp, kpp = wload(w_pp[i].rearrange("(k p) n -> p k n", p=128), (2, D))
            wg_d = w_pg[i].rearrange("(k p) n -> p k n", p=128)
            cnt = 0
            for mp in range(4):
                wg, kg = wload(wg_d[:, :, mp * 256:(mp + 1) * 256], (KC, 256))
                for mi in range(2):
                    m = mp * 2 + mi
                    for t in range(NT):
                        cs = slice(t * TW, (t + 1) * TW)
                        ba = (cnt * 2) % 4
                        bb = (cnt * 2 + 1) % 4
                        cnt += 1
                        for k in range(KC):
                            mm(PSB(ba)[:, 0:TW], wg[:, k, mi * 128:(mi + 1) * 128], H[:, k, cs], k == 0, k == KC - 1,
                               (kg, ("H", t)), (("ps", ba),))
                        for k2 in range(2):
                            mm(PSB(bb)[:, 0:TW], wpp[:, k2, m * 128:(m + 1) * 128], PT[:, k2, cs], k2 == 0, k2 == 1,
                               (kpp, "PT"), (("ps", bb),))
                        sg = SG[cnt % 2]
                        sgk = ("SG", cnt % 2)
                        act(sg[:, 0:TW], PSB(ba)[:, 0:TW], AF.Sigmoid, (("ps", ba),), (sgk,))
                        tt("dve", sg[:, 0:TW], sg[:, 0:TW], PSB(bb)[:, 0:TW], ALU.mult, (sgk, ("ps", bb)), (sgk,))
                        tt("dve", X[:, m, cs], X[:, m, cs], sg[:, 0:TW], ALU.add, (sgk, ("X", t)), (("X", t),))

        def kv(T, pos0, kdst, vdst, KTdst, Vdst_fn, RB=128):
            phase_begin()
            TW, NT = tiles_of(T)
            H = W.alloc([KC, T], BF16)
            SQ = W.alloc([KC, TW], BF16)
            RS = W.alloc([TW], F32)
            KVO = [W.alloc([512], F32) for _ in range(2)]
            for t in range(NT):
                rmsnorm_tile(H, V_KV, t, TW, SQ, RS, 7)
            wk, kk = wload(w_k.rearrange("(k p) n -> p k n", p=128), (KC, 256))
            wv, kvk = wload(w_v.rearrange("(k p) n -> p k n", p=128), (KC, 256))
            import os
            kdbg = int(os.environ.get("KV_DBG", "9"))
            cnt = 0
            for h in range(NG if kdbg >= 1 else 0):
                for t in range(NT):
                    cs = slice(t * TW, (t + 1) * TW)
                    b = cnt % 2
                    cnt += 1
                    for k in range(KC):
                        mm(PSB(b)[0:64, 0:TW], wk[:, k, h * 64:(h + 1) * 64], H[:, k, cs], k == 0, k == KC - 1,
                           (kk, ("H", t)), (("ps", b),))
                    cp("act", KTdst[:, h, cs], PSB(b)[0:64, 0:TW], (("ps", b),), ("KT",))
            for blk in range(T // RB if kdbg >= 2 else 0):
                b = 2 + blk % 2
                bs = slice(blk * RB, (blk + 1) * RB)
                t = (blk * RB) // TW
                for k in range(KC):
                    mm(PSB(b)[0:RB, 0:256], H[:, k, bs], wk[:, k, :], k == 0, k == KC - 1, (kk, ("H", t)), (("ps", b),))
                for k in range(KC):
                    mm(PSB(b)[0:RB, 256:512], H[:, k, bs], wv[:, k, :], k == 0, k == KC - 1, (kvk, ("H", t)),
                       (("ps", b),))
                kvo = KVO[blk % 2]
                kk2 = ("KVO", blk % 2)
                cp("act", kvo[0:RB, :], PSB(b)[0:RB, :], (("ps", b),), (kk2,))
                cp("dve", Vdst_fn(blk), PSB(b)[0:RB, 256:512], (("ps", b),), ("VV",))
                if kdbg < 3:
                    continue
                kd = kdst[blk * RB:(blk + 1) * RB, :]
                vd = vdst[blk * RB:(blk + 1) * RB, :]
                P.dma("sp", lambda e, kvo=kvo, kd=kd: e.dma_start(out=kd, in_=kvo[0:RB, 0:256]), (kk2,), ())
                P.dma("sp", lambda e, kvo=kvo, vd=vd: e.dma_start(out=vd, in_=kvo[0:RB, 256:512]), (kk2,), ())

        def mamba(T, mode, seq_first, seq_last, bidx):
            phase_begin()
            TW, NT = tiles_of(T)
            prompt = (mode == "p")
            nseg, seglen = (1, TW) if prompt else (NSB, 32)
            Lc = 128 if prompt else 32
            H = W.alloc([KC, TW], BF16)
            RS = W.alloc([TW], F32)
            XBC = W.alloc([24, TW], BF16)
            SQ = XBC[:, 0:KC, :]
            sqkeys = tuple(("XBC", m_) for m_ in range(KC))
            YT = W.alloc([16, TW], BF16)
            up_off = W.off
            UP = [W.alloc([nseg, seglen + 3], F32) for _ in range(3)]
            CVT = [W.alloc([nseg, seglen], F32) for _ in range(2)]
            upcvt_bytes = W.off - up_off
            XS_TM = W.alloc([DIN], BF16)
            XDT2 = [W.alloc([DIN], BF16)] * 2
            XDTW = W.alloc([DIN], BF16)
            B_TM = W.alloc([512], BF16)
            SM2 = [W.alloc([8, 32], F32)] * 2
            CBM2 = [W.alloc([NG, 128], BF16)] * 2
            ATRI = [W.alloc([8, 128], F32) for _ in range(2)]
            ESEG = [W.alloc([8, 128], BF16) for _ in range(2)]
            MT = [W.alloc([8, 128], BF16) for _ in range(2)]
            YOFF0 = W.alloc([DIN], F32)
            YB = W.alloc([DIN], BF16)
            if prompt and upcvt_bytes >= DIN * 4:
                YOFF1 = W.ap[:, up_off // 2:up_off // 2 + DIN * 2].bitcast(F32)
                yk1 = tuple(("UP", i_) for i_ in range(3)) + tuple(("UPT", i_) for i_ in range(3)) + \
                    tuple(("CVT", i_) for i_ in range(2))
            else:
                YOFF1 = W.alloc([DIN], F32)
                yk1 = ("YOFF1",)
            YOFF2 = [YOFF0, YOFF1]
            YK2 = [("YOFF",), yk1]
            SIO = YOFF0.rearrange("p (a n) -> p a n", a=16)

            win_d = w_in.rearrange("(k p) n -> p k n", p=128)
            wout_d = w_out.rearrange("(k p) n -> p k n", p=128)

            def seg_view(ap3, L):
                return ap3[:, :, 0:L]

            if prompt and seq_first:
                memset("dve", STATE[:, :], 0.0, ("STATE",))
                memset("dve", STATEB[:, :], 0.0, ("STATEB",))
                memset("dve", CTAIL[:, :, :, :], 0.0, tuple(("CTAIL", m_) for m_ in range(24)))
            if not prompt:
                for s in range(NSB):
                    for tt_ in range(3):
                        src = sconv[s, tt_, :].rearrange("(m p) -> p m", p=128)
                        P.dma("sp", lambda e, src=src, s=s, tt_=tt_: e.dma_start(
                            out=CTAIL[:, :, s, tt_], in_=src, allow_slow_non_contiguous=True), (), tuple(("CTAIL", m_) for m_ in range(24)))

            for t in range(NT):
                cs = slice(t * TW, (t + 1) * TW)
                rmsnorm_tile_local(H, V_MIX + 0, t, TW, SQ, RS, sqkeys)
                wps = {}

                def stA(m):
                    if m % 2 == 0:
                        pc = m // 2
                        wps[pc] = wload(win_d[:, :, DIN + pc * 256:DIN + (pc + 1) * 256], (KC, 256))
                    wp, kp = wps[m // 2]
                    mi = m % 2
                    b = m % 2
                    up = UP[m % 3]
                    upk = ("UP", m % 3)
                    for k in range(KC):
                        mm(PSB(b)[:, 0:TW], wp[:, k, mi * 128:(mi + 1) * 128], H[:, k, 0:TW], k == 0, k == KC - 1,
                           (kp, "Hm"), (("ps", b),))
                    cp("dve", up[:, :, 0:3], CTAIL[:, m, 0:nseg, :], (("CTAIL", m),), (("UPT", m % 3),))
                    cp("act", up[:, :, 3:3 + seglen],
                       PSB(b)[:, 0:TW].rearrange("p (s l) -> p s l", s=nseg), (("ps", b),), (upk,))
                    cp("act", CTAIL[:, m, 0:nseg, :], up[:, :, seglen:seglen + 3], (upk, ("UPT", m % 3)), (("CTAIL", m),))

                def stB1(m):
                    up = UP[m % 3]
                    upk = ("UP", m % 3)
                    cvt = CVT[m % 2]
                    cvk = ("CVT", m % 2)
                    act(cvt[:, :, :], up[:, :, 0:seglen], AF.Identity, (upk, ("UPT", m % 3), "VEC"), (cvk,),
                        bias=gcol(V_CB, m), scale=gcol(V_CW + 0 * 24, m))
                    for kk_ in range(1, 4):
                        stt(cvt[:, :, :], up[:, :, kk_:kk_ + seglen], gcol(V_CW + kk_ * 24, m), cvt[:, :, :],
                            ALU.mult, ALU.add, (upk, ("UPT", m % 3), cvk, "VEC"), (cvk,))

                def stB2(m):
                    cvt = CVT[m % 2]
                    cvk = ("CVT", m % 2)
                    act(XBC[:, m, 0:TW].rearrange("p (s l) -> p s l", s=nseg), cvt[:, :, :], AF.Silu,
                        (cvk,), (("XBC", m),))

                stA(0)
                for m in range(24):
                    if m + 1 < 24:
                        stA(m + 1)
                    stB1(m)
                    if m >= 1:
                        stB2(m - 1)
                stB2(23)
                wdt, kdt = wload(win_d[:, :, DIN + CONVD:INDIM], (KC, 32))
                zpre = [wload(win_d[:, :, pc_ * 256:(pc_ + 1) * 256], (KC, 256)) for pc_ in range(4)]
                import os
                nchunk = TW // Lc
                if os.environ.get("MAMBA_NOCHUNK"):
                    nchunk = 0
                L = Lc

                def stage1(c):
                    c0 = c * Lc
                    cc = slice(c0, c0 + Lc)
                    SM = SM2[c % 2]
                    smk = lambda i_: ("SM", 0, i_)
                    XDT = XDT2[c % 2]
                    xdk = ("XDT", 0)
                    CBM = CBM2[c % 2]
                    cbk = ("CBM", 0)
                    YOFF = YOFF2[c % 2]
                    yk = YK2[c % 2]
                    if not prompt:
                        for two in range(2):
                            src = sssm[c].rearrange("(hp two) p n -> two p hp n", two=2)[two]
                            P.dma("sp", lambda e, src=src, two=two: e.dma_start(
                                out=SIO[two * 64:(two + 1) * 64, :, :], in_=src), (), ("YOFF",))
                        for hp in range(16):
                            b = 4 + (hp // 4) % 2
                            tp(PSB(b)[:, (hp % 4) * 128:(hp % 4 + 1) * 128], SIO[:, hp, :], ident_f, ("YOFF", "CF"),
                               (("ps", b),))
                            if hp % 4 == 3:
                                q4 = hp // 4
                                cp("dve", STATE[:, q4 * 512:(q4 + 1) * 512], PSB(b), (("ps", b),), ("STATE",))
                                cp("act", STATEB[:, q4 * 512:(q4 + 1) * 512], PSB(b), (("ps", b),), ("STATEB",))
                    for k in range(KC):
                        mm(PSB(6)[0:L, 0:32], H[:, k, cc], wdt[:, k, :], k == 0, k == KC - 1, (kdt, "Hm"), (("ps", 6),))
                    tt("dve", SM[0:L, 0, :], PSB(6)[0:L, 0:32], VEC[0:L, V_DTB:V_DTB + 32], ALU.add,
                       (("ps", 6), "VEC"), (smk(0),))
                    act(SM[0:L, 1, :], SM[0:L, 0, :], AF.Exp, (smk(0),), (smk(1),))
                    act(SM[0:L, 2, :], SM[0:L, 1, :], AF.Ln, (smk(1),), (smk(2),), bias=ONEB[0:L, 0:1])
                    tt("dve", SM[0:L, 3, :], SM[0:L, 2, :], ABC[0:L, :], ALU.mult, (smk(2), "ABC"), (smk(3),))
                    mm(PSB(6)[0:L, 32:64], tri_f[0:L, 0:L], SM[0:L, 3, :], True, True, (smk(3), "CF"), (("ps", 6),))
                    mm(PSB(6)[:, 64:96], ones_f[0:L, :], SM[0:L, 3, :], True, True, (smk(3), "CF"), (("ps", 6),))
                    cp("dve", SM[0:L, 4, :], PSB(6)[0:L, 32:64], (("ps", 6),), (smk(4),))
                    act(SM[0:L, 5, :], PSB(6)[0:L, 32:64], AF.Exp, (("ps", 6),), (smk(5),))
                    tt("dve", SM[0:L, 6, :], PSB(6)[0:L, 64:96], SM[0:L, 4, :], ALU.subtract, (("ps", 6), smk(4)),
                       (smk(6),))
                    act(SM[0:L, 6, :], SM[0:L, 6, :], AF.Exp, (smk(6),), (smk(6),))
                    act(SM[:, 7, :], PSB(6)[:, 64:96], AF.Exp, (("ps", 6),), (smk(7),))
                    yield
                    for m in range(16):
                        b = 4 + m // 8
                        tp(PSBH(b)[0:L, (m % 8) * 128:(m % 8 + 1) * 128], XBC[:, m, cc], ident_b,
                           (("XBC", m), "CB"), (("ps", b),))
                    for hb in range(2):
                        b = 4 + hb
                        hs = slice(hb * 1024, (hb + 1) * 1024)
                        tt("dve", XDT[0:L, hs].rearrange("p (h d) -> p h d", d=HD),
                           PSBH(b)[0:L, :].rearrange("p (h d) -> p h d", d=HD),
                           SM[0:L, 2, hb * 16:(hb + 1) * 16].unsqueeze(2).to_broadcast([L, 16, HD]), ALU.mult,
                           (("ps", b), smk(2)), (xdk,))
                    tt("dve", XDTW[0:L, :].rearrange("p (h d) -> p h d", d=HD),
                       XDT[0:L, :].rearrange("p (h d) -> p h d", d=HD),
                       SM[0:L, 6, :].unsqueeze(2).to_broadcast([L, NH, HD]), ALU.mult, (xdk, smk(6)), ("XDTW",))
                    yield
                    for g in range(NG):
                        tp(PSBH(7)[0:L, g * 128:(g + 1) * 128], XBC[:, 16 + g, cc], ident_b,
                           (("XBC", 16 + g), "CB"), (("ps", 7),))
                    cp("act", B_TM[0:L, :], PSBH(7)[0:L, 0:512], (("ps", 7),), ("B_TM",))
                    for g in range(NG):
                        mm(PSB(7)[0:L, g * L:(g + 1) * L], XBC[:, 16 + g, cc], XBC[:, 20 + g, cc], True, True,
                           (("XBC", 16 + g), ("XBC", 20 + g)), (("ps", 7),))
                    tt("dve", CBM[0:L, :, 0:L], PSB(7)[0:L, 0:NG * L].rearrange("p (g i) -> p g i", g=NG),
                       cbmask_b[0:L, 0:L].unsqueeze(1).to_broadcast([L, NG, L]), ALU.mult, (("ps", 7), "CB"), (cbk,))
                    for g in range(NG):
                        mm(PSB(g)[0:L, :], XBC[:, 20 + g, cc], STATEB[:, g * 512:(g + 1) * 512], True, True,
                           (("XBC", 20 + g), "STATEB"), (("ps", g),))
                        tt("dve", YOFF[0:L, g * 512:(g + 1) * 512].rearrange("p (h d) -> p h d", d=HD),
                           PSB(g)[0:L, :].rearrange("p (h d) -> p h d", d=HD),
                           SM[0:L, 5, g * 8:(g + 1) * 8].unsqueeze(2).to_broadcast([L, 8, HD]), ALU.mult,
                           (("ps", g), smk(5)), yk)
                    yield
                    for g in range(NG):
                        mm(PSB(g)[:, :], B_TM[0:L, g * 128:(g + 1) * 128], XDTW[0:L, g * 512:(g + 1) * 512], True, True,
                           ("B_TM", "XDTW"), (("ps", g),))
                    tt("dve", STATE[:, :].rearrange("p (h d) -> p h d", d=HD),
                       STATE[:, :].rearrange("p (h d) -> p h d", d=HD),
                       SM[:, 7, :].unsqueeze(2).to_broadcast([128, NH, HD]), ALU.mult, ("STATE", smk(7)), ("STATE",))
                    for g in range(NG):
                        tt("dve", STATE[:, g * 512:(g + 1) * 512], STATE[:, g * 512:(g + 1) * 512], PSB(g)[:, :], ALU.add,
                           ("STATE", ("ps", g)), ("STATE",))
                    cp("act", STATEB[:, :], STATE[:, :], ("STATE",), ("STATEB",))
                    yield

                def stage2(c):
                    c0 = c * Lc
                    cc = slice(c0, c0 + Lc)
                    SM = SM2[c % 2]
                    smk = lambda i_: ("SM", 0, i_)
                    XDT = XDT2[c % 2]
                    xdk = ("XDT", 0)
                    CBM = CBM2[c % 2]
                    cbk = ("CBM", 0)
                    YOFF = YOFF2[c % 2]
                    yk = YK2[c % 2]
                    def hgA(hg):
                        at = ATRI[hg % 2]
                        atk = ("ATRI", hg % 2)
                        eg = ESEG[hg % 2]
                        egk = ("ESEG", hg % 2)
                        tt("dve", at[0:L, :, 0:L], SM[0:L, 3, hg * 8:(hg + 1) * 8].unsqueeze(2).to_broadcast([L, 8, L]),
                           tri_f[0:L, 0:L].unsqueeze(1).to_broadcast([L, 8, L]), ALU.mult, (smk(3), "CF"), (atk,))
                        hper = min(max(1, 512 // L), 8)
                        for q0 in range(0, 8, hper):
                            b = 4 + (q0 // hper) % 2
                            mm(PSB(b)[0:L, 0:hper * L].rearrange("p (h i) -> p h i", h=hper), U_f[0:L, 0:L],
                               at[0:L, q0:q0 + hper, 0:L], True, True, (atk, "CF"), (("ps", b),))
                            act(eg[0:L, q0:q0 + hper, 0:L], PSB(b)[0:L, 0:hper * L].rearrange("p (h i) -> p h i", h=hper),
                                AF.Exp, (("ps", b),), (egk,))

                    def hgB(hg):
                        eg = ESEG[hg % 2]
                        egk = ("ESEG", hg % 2)
                        mt = MT[hg % 2]
                        mtk = ("MT", hg % 2)
                        tt("dve", mt[0:L, :, 0:L], eg[0:L, :, 0:L],
                           CBM[0:L, hg, 0:L].unsqueeze(1).to_broadcast([L, 8, L]), ALU.mult, (egk, cbk), (mtk,))
                        for h8 in range(8):
                            h = hg * 8 + h8
                            mm(PSB(hg)[0:L, h8 * 64:(h8 + 1) * 64], mt[0:L, h8, 0:L], XDT[0:L, h * 64:(h + 1) * 64],
                               True, True, (mtk, xdk), (("ps", hg),))
                        tt("dve", YB[0:L, hg * 512:(hg + 1) * 512], PSB(hg)[0:L, :], YOFF[0:L, hg * 512:(hg + 1) * 512],
                           ALU.add, (("ps", hg),) + yk, ("YB",))

                    hgA(0)
                    for hg in range(NG):
                        if hg + 1 < NG:
                            hgA(hg + 1)
                        hgB(hg)
                    yield
                    for m in range(16):
                        b = 4 + m // 8
                        tp(PSBH(b)[:, (m % 8) * Lc:(m % 8) * Lc + L], YB[0:L, m * 128:(m + 1) * 128], ident_b[0:L, 0:L],
                           ("YB", "CB"), (("ps", b),))
                    for hb in range(2):
                        b = 4 + hb
                        cp("act", YT[:, hb * 8:(hb + 1) * 8, cc],
                           PSBH(b)[:, 0:8 * Lc].rearrange("p (m l) -> p m l", m=8), (("ps", b),), ("YT",))
                    yield

                def drain(g):
                    for _ in g:
                        pass

                if False:
                    if nchunk:
                        drain(stage1(0))
                    for c in range(nchunk):
                        g2 = stage2(c)
                        g1 = stage1(c + 1) if c + 1 < nchunk else iter(())
                        d1 = d2 = False
                        while not (d1 and d2):
                            if not d1:
                                try:
                                    next(g1)
                                except StopIteration:
                                    d1 = True
                            if not d2:
                                try:
                                    next(g2)
                                except StopIteration:
                                    d2 = True
                else:
                    for c in range(nchunk):
                        drain(stage1(c))
                        drain(stage2(c))
                        if not prompt:
                            store_state(ssm_s[c], SIO)
                cnt = 0
                for pc in range(8):
                    wp, kp = zpre[pc] if pc < 4 else wload(win_d[:, :, pc * 256:(pc + 1) * 256], (KC, 256))
                    for mi in range(2):
                        m = pc * 2 + mi
                        b = cnt % 2
                        cvt = CVT[cnt % 2]
                        cvk = ("CVT", cnt % 2)
                        cnt += 1
                        for k in range(KC):
                            mm(PSB(b)[:, 0:TW], wp[:, k, mi * 128:(mi + 1) * 128], H[:, k, 0:TW], k == 0, k == KC - 1,
                               (kp, "Hm"), (("ps", b),))
                        zf = cvt.rearrange("p s l -> p (s l)")
                        act(zf[:, 0:TW], PSB(b)[:, 0:TW], AF.Silu, (("ps", b),), (cvk,))
                        stt(YT[:, m, 0:TW], XBC[:, m, 0:TW], gcol(V_DFM, m), YT[:, m, 0:TW], ALU.mult, ALU.add,
                            (("XBC", m), "YT", "VEC"), ("YT",))
                        tt("dve", YT[:, m, 0:TW], YT[:, m, 0:TW], zf[:, 0:TW], ALU.mult, ("YT", cvk), ("YT",))
                SQ2 = XBC
                act(SQ2[:, 0:16, 0:TW], YT[:, :, 0:TW], AF.Square, ("YT",), tuple(("XBC", m) for m in range(16)))
                for g in range(NG):
                    for c4 in range(4):
                        mm(PSB(g)[:, 0:TW], ones_b, SQ2[:, g * 4 + c4, 0:TW], c4 == 0, c4 == 3,
                           (("XBC", g * 4 + c4), "CB"), (("ps", g),))
                    rsg = CVT[g % 2].rearrange("p s l -> p (s l)")
                    rk = ("CVT", g % 2)
                    act(rsg[:, 0:TW], PSB(g)[:, 0:TW], AF.Sqrt, (("ps", g),), (rk,), bias=EPSB[:, 0:1], scale=1.0 / 512)
                    P.op("dve", lambda e, rsg=rsg: e.reciprocal(rsg[:, 0:TW], rsg[:, 0:TW]), (rk,), (rk,))
                    for c4 in range(4):
                        m = g * 4 + c4
                        stt(YT[:, m, 0:TW], YT[:, m, 0:TW], gcol(V_SSMN, m), rsg[:, 0:TW], ALU.mult, ALU.mult,
                            ("YT", rk, "VEC"), ("YT",))
                cnt = 0
                for m in range(KC):
                    wo_, ko = wload(wout_d[:, :, m * 128:(m + 1) * 128], (16, 128))
                    b = 6 + cnt % 2
                    cnt += 1
                    for k in range(16):
                        mm(PSB(b)[:, 0:TW], wo_[:, k, :], YT[:, k, 0:TW], k == 0, k == 15, (ko, "YT"), (("ps", b),))
                    tt("dve", X[:, m, cs], X[:, m, cs], PSB(b)[:, 0:TW], ALU.add, (("ps", b), ("X", t)), (("X", t),))
            if prompt and seq_last:
                store_state(ssm_p[bidx], SIO)
                for tt_ in range(3):
                    dst = conv_p[bidx, tt_, :].rearrange("(m p) -> p m", p=128)
                    P.dma("sp", lambda e, dst=dst, tt_=tt_: e.dma_start(
                        out=dst, in_=CTAIL[:, :, 0, tt_], allow_slow_non_contiguous=True), tuple(("CTAIL", m_) for m_ in range(24)), ())
            if not prompt:
                for s in range(NSB):
                    for tt_ in range(3):
                        dst = conv_s[s, tt_, :].rearrange("(m p) -> p m", p=128)
                        P.dma("sp", lambda e, dst=dst, s=s, tt_=tt_: e.dma_start(
                            out=dst, in_=CTAIL[:, :, s, tt_], allow_slow_non_contiguous=True), tuple(("CTAIL", m_) for m_ in range(24)), ())

        def store_state(dst3, SIO):
            for hp in range(16):
                b = 4 + (hp // 4) % 2
                tp(PSB(b)[:, (hp % 4) * 128:(hp % 4 + 1) * 128], STATE[:, hp * 128:(hp + 1) * 128], ident_f,
                   ("STATE", "CF"), (("ps", b),))
                if hp % 4 == 3:
                    q4 = hp // 4
                    cp("dve", SIO[:, q4 * 4:(q4 + 1) * 4, :], PSB(b).rearrange("p (a n) -> p a n", a=4), (("ps", b),),
                       ("YOFF",))
            for two in range(2):
                dst = dst3.rearrange("(hp two) p n -> two p hp n", two=2)[two]
                P.dma("sp", lambda e, dst=dst, two=two: e.dma_start(out=dst, in_=SIO[two * 64:(two + 1) * 64, :, :]),
                      ("YOFF",), ())

        ONEB = sb("ONEB", [128, 1], F32)
        memset("dve", ONEB[:, :], 1.0, ("ONEB",))

        def rmsnorm_tile_local(H, vbase, t, TW, SQ, RS, sqkeys=("SQ",)):
            cs = slice(t * TW, (t + 1) * TW)
            act(SQ[:, :, 0:TW], X[:, :, cs], AF.Square, (("X", t),), sqkeys)
            for k in range(KC):
                mm(PSB(7)[:, 0:TW], ones_b, SQ[:, k, 0:TW], k == 0, k == KC - 1, sqkeys + ("CB",), (("ps", 7),))
            act(RS[:, 0:TW], PSB(7)[:, 0:TW], AF.Sqrt, (("ps", 7),), ("RS",), bias=EPSB[:, 0:1], scale=1.0 / D)
            P.op("dve", lambda e: e.reciprocal(RS[:, 0:TW], RS[:, 0:TW]), ("RS",), ("RS",))
            for k in range(KC):
                stt(H[:, k, 0:TW], X[:, k, cs], gcol(vbase, k), RS[:, 0:TW], ALU.mult, ALU.mult,
                    (("X", t), "RS", "VEC"), ("Hm",))

        def attn(T, mode, pos0, kt_new=None, v_new=None):
            phase_begin()
            TW, NT = tiles_of(T)
            prompt = (mode == "p")
            H = W.alloc([KC, TW], BF16)
            SQ = W.alloc([KC, TW], BF16)
            RS = W.alloc([TW], F32)
            QT = W.alloc([16, TW], BF16, parts=64)
            OTS = W.alloc([8, TW], BF16)
            NEE, NSP, NWW, NAA, NVB = 3, 8, 2, 4, 4
            EE = [W.alloc([512], F32) for _ in range(NEE)]
            SP = [W.alloc([512], BF16) for _ in range(NSP)]
            WW = [W.alloc([512], F32) for _ in range(NWW)]
            AA = [W.alloc([512], BF16) for _ in range(NAA)]
            if not prompt:
                KSTG = [W.alloc([256], F32) for _ in range(2)]
                VSTG = [W.alloc([256], F32) for _ in range(2)]
                KTB = [W.alloc([NG, 128], BF16, parts=64) for _ in range(2)]
                VB = [W.alloc([256], BF16) for _ in range(NVB)]
            wq_d = w_q.rearrange("(k p) n -> p k n", p=128)
            for t in range(NT):
                cs = slice(t * TW, (t + 1) * TW)
                rmsnorm_tile_local(H, V_MIX + 8, t, TW, SQ, RS)
                cnt = 0
                for pc in range(4):
                    wp, kp = wload(wq_d[:, :, pc * 256:(pc + 1) * 256], (KC, 256))
                    for hi in range(4):
                        h = pc * 4 + hi
                        b = cnt % 2
                        cnt += 1
                        for k in range(KC):
                            mm(PSB(b)[0:64, 0:TW], wp[:, k, hi * 64:(hi + 1) * 64], H[:, k, 0:TW], k == 0, k == KC - 1,
                               (kp, "Hm"), (("ps", b),))
                        P.op("act", lambda e, h=h, b=b: e.mul(QT[:, h, 0:TW], PSB(b)[0:64, 0:TW], 0.125),
                             (("ps", b),), ("QT",))
                units = []
                if prompt:
                    for qb in range(TW // 128):
                        gq = (pos0 + t * TW) // 128 + qb
                        for kb in range(gq, -1, -1):
                            for h in range(NG):
                                units.append({"chain": (qb, h), "cidx": h, "first": kb == gq, "last": kb == 0,
                                              "diag": kb == gq, "h": h, "qb": qb, "kb": kb, "nk": 128})
                else:
                    nkb = PAST // 128
                    for s0 in range(0, NSB, 2):
                        for kb in range(nkb, -1, -1):
                            for s in range(s0, min(s0 + 2, NSB)):
                                units.append({"chain": (s,), "cidx": s - s0, "first": kb == nkb, "last": kb == 0,
                                              "diag": kb == nkb, "s": s, "kb": kb, "nk": 32 if kb == nkb else 128})
                NU = len(units)
                lastseen = {}
                for i, u in enumerate(units):
                    u["prev"] = lastseen.get(u["chain"])
                    lastseen[u["chain"]] = i

                def zbank(i):
                    return i % 2

                def pbank(u):
                    return 2 + u["cidx"]

                def emit_load(i):
                    u = units[i]
                    if prompt or u["diag"]:
                        return
                    s, kb = u["s"], u["kb"]
                    ks, vs = KSTG[i % 2], VSTG[i % 2]
                    kk_, vk_ = ("KSTG", i % 2), ("VSTG", i % 2)
                    srck = ck[s, kb * 128:(kb + 1) * 128, :]
                    srcv = cv[s, kb * 128:(kb + 1) * 128, :]
                    P.dma("sp", lambda e, ks=ks, srck=srck: e.dma_start(out=ks, in_=srck), (), (kk_,))
                    P.dma("sp", lambda e, vs=vs, srcv=srcv: e.dma_start(out=vs, in_=srcv), (), (vk_,))
                    for h in range(NG):
                        tp(PSB(zbank(i))[0:64, h * 128:(h + 1) * 128], ks[:, h * 64:(h + 1) * 64], ident_f, (kk_, "CF"),
                           (("ps", zbank(i)),))
                    cp("dve", KTB[i % 2][:, :, :], PSB(zbank(i))[0:64, :].rearrange("p (h k) -> p h k", h=NG),
                       (("ps", zbank(i)),), (("KTB", i % 2),))
                    cp("act", VB[i % NVB][:, :], vs, (vk_,), (("VB", i % NVB),))

                def kview(u, i, h):
                    if prompt:
                        return KT[:, h, u["kb"] * 128:(u["kb"] + 1) * 128], ("KT",)
                    if u["diag"]:
                        return kt_new[:, h, u["s"] * 32:(u["s"] + 1) * 32], ("KT",)
                    return KTB[i % 2][:, h, :], (("KTB", i % 2),)

                def vview(u, i, h):
                    if prompt:
                        return VV[:, u["kb"], h * 64:(h + 1) * 64], ("VV",)
                    if u["diag"]:
                        return v_new[:, u["s"], h * 64:(h + 1) * 64], ("VV",)
                    return VB[i % NVB][:, h * 64:(h + 1) * 64], (("VB", i % NVB),)

                def maskmul(buf, key, nk):
                    if prompt:
                        msk = cmask_b[0:nk, 0:128].unsqueeze(1).to_broadcast([nk, 4, 128])
                        v3 = buf[0:nk, :].rearrange("p (g q) -> p g q", g=4)
                    else:
                        msk = cmask_b[0:nk, 0:32].unsqueeze(1).to_broadcast([nk, 16, 32])
                        v3 = buf[0:nk, :].rearrange("p (g q) -> p g q", g=16)
                    tt("dve", v3, v3, msk, ALU.mult, (key, "CB"), (key,))

                def emit_Z(i):
                    u = units[i]
                    nk = u["nk"]
                    b = zbank(i)
                    emit_load(i)
                    if prompt:
                        h, qb = u["h"], u["qb"]
                        lhs, lk = kview(u, i, h)
                        mm(PSB(b)[0:nk, :].rearrange("p (g q) -> p g q", g=4), lhs,
                           QT[:, h * 4:(h + 1) * 4, qb * 128:(qb + 1) * 128], True, True, lk + ("QT",), (("ps", b),))
                    else:
                        s = u["s"]
                        for h in range(NG):
                            lhs, lk = kview(u, i, h)
                            mm(PSB(b)[0:nk, h * 128:(h + 1) * 128].rearrange("p (g q) -> p g q", g=4), lhs,
                               QT[:, h * 4:(h + 1) * 4, s * 32:(s + 1) * 32], True, True, lk + ("QT",), (("ps", b),))
                    ee, sp_ = EE[i % NEE], SP[i % NSP]
                    act(ee[0:nk, :], PSB(b)[0:nk, :], AF.Exp, (("ps", b),), (("EE", i % NEE),))
                    act(sp_[0:nk, :], ee[0:nk, :], AF.Ln, (("EE", i % NEE),), (("SP", i % NSP),), bias=ONEB[0:nk, 0:1])
                    if u["diag"]:
                        maskmul(sp_, ("SP", i % NSP), nk)

                def emit_C(i):
                    u = units[i]
                    nk = u["nk"]
                    pb = pbank(u)
                    sp_ = SP[i % NSP]
                    if u["first"]:
                        mm(PSB(pb)[0:128, :], trige_b[0:nk, 0:128], sp_[0:nk, :], True, True, (("SP", i % NSP), "CB"),
                           (("ps", pb),))
                    else:
                        pi = u["prev"]
                        pnk = units[pi]["nk"]
                        spp = SP[pi % NSP]
                        mm(PSB(pb)[0:128, :], compl_b[0:pnk, 0:128], spp[0:pnk, :], False, False,
                           (("SP", pi % NSP), "CB"), (("ps", pb),), skip=True)
                        mm(PSB(pb)[0:128, :], trige_b[0:nk, 0:128], sp_[0:nk, :], False, True, (("SP", i % NSP), "CB"),
                           (("ps", pb),), skip=True)
                    ww = WW[i % NWW]
                    act(ww[0:nk, :], PSB(pb)[0:nk, :], AF.Exp, (("ps", pb),), (("WW", i % NWW),), scale=-1.0)
                    aa = AA[i % NAA]
                    tt("dve", aa[0:nk, :], EE[i % NEE][0:nk, :], ww[0:nk, :], ALU.mult,
                       (("EE", i % NEE), ("WW", i % NWW)), (("AA", i % NAA),))
                    if u["diag"]:
                        maskmul(aa, ("AA", i % NAA), nk)

                def emit_O(i):
                    u = units[i]
                    nk = u["nk"]
                    aa = AA[i % NAA]
                    if prompt:
                        h = u["h"]
                        ob, ph = 6 + h // 2, h % 2
                        rv, rk = vview(u, i, h)
                        mm(PSB(ob)[ph * 64:(ph + 1) * 64, :], rv, aa[0:nk, :], u["first"], u["last"],
                           rk + (("AA", i % NAA),), (("ps", ob),))
                        if u["last"]:
                            qb = u["qb"]
                            j0 = (h // 2) * 4
                            cp("act", OTS[ph * 64:(ph + 1) * 64, j0:j0 + 4, qb * 128:(qb + 1) * 128],
                               PSB(ob)[ph * 64:(ph + 1) * 64, :].rearrange("p (g q) -> p g q", g=4), (("ps", ob),),
                               ("OTS",))
                    else:
                        s = u["s"]
                        ob = 6 + u["cidx"]
                        for h in range(NG):
                            ph = h % 2
                            rv, rk = vview(u, i, h)
                            c0 = (h // 2) * 128
                            mm(PSB(ob)[ph * 64:(ph + 1) * 64, c0:c0 + 128], rv, aa[0:nk, h * 128:(h + 1) * 128],
                               u["first"] and h // 2 == 0, u["last"], rk + (("AA", i % NAA),), (("ps", ob),), skip=True)
                        if u["last"]:
                            cp("act", OTS[:, :, s * 32:(s + 1) * 32],
                               PSB(ob)[:, 0:256].rearrange("p (j q) -> p j q", j=8), (("ps", ob),), ("OTS",))

                import os
                nwarm = int(os.environ.get("NWARM", "0"))
                wevery = int(os.environ.get("WEVERY", "0"))
                for it in range(NU + 3):
                    if nwarm and (it == 0 or (wevery and it % wevery == 0)) and it < NU:
                        for _ in range(nwarm):
                            mm(PSB(zbank(it))[:, 0:TW], ones_b, H[:, 0, 0:TW], True, True, ("CB", "Hm"),
                               (("ps", zbank(it)),))
                    if 0 <= it - 3 < NU:
                        emit_O(it - 3)
                    if it < NU:
                        emit_Z(it)
                    if 0 <= it - 1 < NU:
                        emit_C(it - 1)
                wo_d = w_o.rearrange("(a two g p) n -> two p a g n", two=2, g=4, p=64)
                cnt = 0
                for m in range(KC):
                    sl = ring_state["n"] % nslot
                    ring_state["n"] += 1
                    ko = ("ring", sl)
                    for two in range(2):
                        for a_ in range(2):
                            dst = RING[two * 64:(two + 1) * 64, sl, a_ * 512:(a_ + 1) * 512].rearrange(
                                "p (g n) -> p g n", g=4)
                            src = wo_d[two][:, a_, :, m * 128:(m + 1) * 128]
                            P.dma("pool", lambda e, dst=dst, src=src: e.dma_start(out=dst, in_=src), (), (ko,),
                                  nobarrier=True)
                    wo_ = RING[:, sl, 0:1024].rearrange("p (j n) -> p j n", j=8)
                    b = cnt % 2
                    cnt += 1
                    for j in range(8):
                        mm(PSB(b)[:, 0:TW], wo_[:, j, :], OTS[:, j, 0:TW], j == 0, j == 7, (ko, "OTS"), (("ps", b),))
                    tt("dve", X[:, m, cs], X[:, m, cs], PSB(b)[:, 0:TW], ALU.add, (("ps", b), ("X", t)), (("X", t),))

        KTN = KT[:, :, 0:128]
        VN = VV[0:32, 0:4, :]

        def run_pass(mode, bidx, half):
            prompt = mode == "p"
            T = PASS_T if prompt else TS
            pos0 = half * PASS_T if prompt else PAST
            if prompt:
                xrows = xp[bidx, pos0:pos0 + T, :]
                yrows = y_p[bidx, pos0:pos0 + T, :]
                prow = [pp[i, bidx, pos0:pos0 + T, :] for i in range(2)]
            else:
                xrows, yrows = xs, y_s
                prow = [psm[i] for i in range(2)]
            load_x(xrows, T)
            if "ffn" in phases:
                ffn(0, 0, T)
            if "mamba" in phases:
                mamba(T, mode, half == 0, pos0 + T == S, bidx)
            if "ffn" in phases:
                ffn(0, 1, T)
            if "ple" in phases:
                ple(0, prow[0], T)
            if "kv" in phases:
                if prompt:
                    kv(T, pos0, k_p[bidx, pos0:pos0 + T, :], v_p[bidx, pos0:pos0 + T, :], KT[:, :, pos0:pos0 + T],
                       lambda blk: VV[:, pos0 // 128 + blk, :])
                else:
                    kv(T, pos0, k_s, v_s, KTN[:, :, 0:T], lambda blk: VN[:, blk, :], RB=32)
            if "ffn" in phases:
                ffn(1, 0, T)
            if "attn" in phases:
                if prompt:
                    attn(T, mode, pos0)
                else:
                    attn(T, mode, pos0, KTN, VN)
            if "ffn" in phases:
                ffn(1, 1, T)
            if "ple" in phases:
                ple(1, prow[1], T)
            store_y(yrows, T)

        for b in range(NPB):
            for half in range(S // PASS_T):
                run_pass("p", b, half)
        if NSB > 0:
            run_pass("s", 0, 0)

        P.barrier()
        P.emit(block, sems, ring_sems)
    return nc


_W_NAMES = {
    "w_gate": "ffn_w_gate", "w_up": "ffn_w_up", "w_down": "ffn_w_down",
}


def make_in_map(inp, pb, sb_, cf, cb, vec):
    f = lambda a: np.ascontiguousarray(a, dtype=np.float32)
    xs = f(inp["x_sample"][sb_])
    ps_ = f(inp["p_sample"][:, sb_])
    m = {
        "xp": f(inp["x_prompt"][pb]),
        "xs": xs.reshape(-1, D),
        "pp": f(inp["p_prompt"][:, pb]),
        "psm": ps_.reshape(2, -1, PLE),
        "sssm": f(inp["state_ssm"][0, sb_]),
        "sconv": f(inp["state_conv"][0, sb_]),
        "ck": f(inp["cache_k"][sb_]).reshape(xs.shape[0], -1, 256),
        "cv": f(inp["cache_v"][sb_]).reshape(xs.shape[0], -1, 256),
        "w_gate": f(inp["ffn_w_gate"]), "w_up": f(inp["ffn_w_up"]), "w_down": f(inp["ffn_w_down"]),
        "w_in": f(inp["ssm_w_in"][0]), "w_out": f(inp["ssm_w_out"][0]),
        "w_k": f(inp["w_k"]), "w_v": f(inp["w_v"]), "w_q": f(inp["sb_w_q"][0]), "w_o": f(inp["sb_w_o"][0]),
        "w_pg": f(inp["ple_w_gate"]), "w_pp": f(inp["ple_w_proj"]),
        "cf": cf, "cb": cb, "vec": vec,
    }
    return m


def run(inp, n_cores, phases=("ffn", "mamba", "ple", "kv", "attn"), trace=False):
    inp = {k: np.asarray(v) for k, v in inp.items()}
    B, S = inp["x_prompt"].shape[0], inp["x_prompt"].shape[1]
    BS = inp["x_sample"].shape[0]
    PAST = inp["cache_k"].shape[1]
    NPB, NSB = B // n_cores, BS // n_cores
    nc = build_program(NPB, S, NSB, PAST, phases=phases)
    cf, cb = _const_tables()
    vec = _vec_table(inp)
    in_maps = []
    for c in range(n_cores):
        in_maps.append(make_in_map(inp, slice(c * NPB, (c + 1) * NPB), slice(c * NSB, (c + 1) * NSB), cf, cb, vec))
    res = run_bass_kernel_spmd(nc, in_maps, core_ids=list(range(n_cores)), trace=trace)
    R = res.results
    cat = lambda k: np.concatenate([r[k] for r in R], axis=0)
    y_p = cat("y_p")
    y_s = cat("y_s").reshape(BS, 32, D)
    ssm_p = cat("ssm_p")[None]
    conv_p = cat("conv_p")[None]
    k_p = cat("k_p").reshape(B, S, 4, 64)
    v_p = cat("v_p").reshape(B, S, 4, 64)
    ssm_s = cat("ssm_s")[None]
    conv_s = cat("conv_s")[None]
    k_s = cat("k_s").reshape(BS, 32, 4, 64)
    v_s = cat("v_s").reshape(BS, 32, 4, 64)
    outs = (y_p, y_s, ssm_p, conv_p, k_p, v_p, ssm_s, conv_s, k_s, v_s)
    outs = tuple(np.ascontiguousarray(o, dtype=np.float32) for o in outs)
    return outs, res


def kernel(**inputs):
    outs, _ = run(inputs, 8)
    return outs
```

```python
import numpy as np
import concourse.bass as bass
import concourse.mybir as mybir
from concourse.bass_utils import run_bass_kernel_spmd

F32 = mybir.dt.float32
BF16 = mybir.dt.bfloat16
AF = mybir.ActivationFunctionType
ALU = mybir.AluOpType

D = 1024
KC = 8
DFF = 2816
FC = 22
DIN = 2048
NH = 32
HD = 64
NG = 4
DST = 128
CONVD = 3072
INDIM = 5152
PLE = 256
EPS = 1e-6
SLOT = 2816
PASS_T = 1024


class Prog:
    ENG = ("pe", "act", "dve", "pool", "sp")

    def __init__(self, nc, ring_k=8):
        self.nc = nc
        self.ops = {e: [] for e in self.ENG}
        self.last_w = {}
        self.readers = {}
        self.rings = {"sp": {"n": 0, "K": ring_k}, "pool": {"n": 0, "K": ring_k}}
        self.gdeps = set()

    def _deps(self, eng, reads, writes, is_dma, nobarrier):
        deps = set()
        for k in reads:
            lw = self.last_w.get(k)
            if lw is not None:
                deps.add(lw)
        for k in writes:
            lw = self.last_w.get(k)
            if lw is not None and not (lw[0] == "eng" and lw[1] == eng and not is_dma):
                deps.add(lw)
            for r in self.readers.get(k, ()):
                if not (r[0] == "eng" and r[1] == eng and not is_dma):
                    deps.add(r)
        if not nobarrier:
            deps |= self.gdeps
        return deps

    def op(self, eng, fn, reads=(), writes=(), nobarrier=False):
        psr = tuple(k for k in reads if isinstance(k, tuple) and k[0] == "ps" and k not in writes)
        writes = tuple(writes) + psr
        deps = self._deps(eng, reads, writes, False, nobarrier)
        idx = len(self.ops[eng])
        ref = ("eng", eng, idx)
        self.ops[eng].append({"fn": fn, "deps": deps, "dma": None, "inc": False})
        self._record(ref, reads, writes)
        return ref

    def dma(self, q, fn, reads=(), writes=(), nobarrier=False):
        ring = self.rings[q]
        n = ring["n"]
        ring["n"] += 1
        deps = self._deps(q, reads, writes, True, nobarrier)
        if n >= ring["K"]:
            deps.add(("dma", q, n - ring["K"]))
        ref = ("dma", q, n)
        self.ops[q].append({"fn": fn, "deps": deps, "dma": n, "inc": False})
        self._record(ref, reads, writes)
        return ref

    def _record(self, ref, reads, writes):
        for k in reads:
            self.readers.setdefault(k, []).append(ref)
        for k in writes:
            self.last_w[k] = ref
            self.readers[k] = []

    def barrier(self):
        g = set()
        for e in self.ENG:
            for i in range(len(self.ops[e]) - 1, -1, -1):
                if self.ops[e][i]["dma"] is None:
                    g.add(("eng", e, i))
                    break
        ring = self.rings["sp"]
        for n in range(max(0, ring["n"] - ring["K"]), ring["n"]):
            g.add(("dma", "sp", n))
        self.gdeps = g

    def emit(self, block, sems, ring_sems):
        for e in self.ENG:
            for o in self.ops[e]:
                for d in o["deps"]:
                    if d[0] == "eng":
                        self.ops[d[1]][d[2]]["inc"] = True
        counts = {}
        for e in self.ENG:
            c = 0
            cl = []
            for o in self.ops[e]:
                if o["inc"] and o["dma"] is None:
                    c += 1
                cl.append(c)
            counts[e] = cl
        rings = self.rings

        def resolve(d):
            if d[0] == "eng":
                return (("e", d[1]), counts[d[1]][d[2]])
            K = rings[d[1]]["K"]
            return (("r", d[1], d[2] % K), 16 * (d[2] // K + 1))

        def semof(key):
            if key[0] == "e":
                return sems[key[1]]
            return ring_sems[key[1]][key[2]]

        def run(ename, engine):
            known = {}
            for o in self.ops[ename]:
                need = {}
                for d in o["deps"]:
                    k, v = resolve(d)
                    if v > need.get(k, 0):
                        need[k] = v
                for k, v in need.items():
                    if v > known.get(k, 0):
                        engine.wait_ge(semof(k), v)
                        known[k] = v
                inst = o["fn"](engine)
                if o["dma"] is not None:
                    K = rings[ename]["K"]
                    inst.then_inc(ring_sems[ename][o["dma"] % K], 16)
                elif o["inc"]:
                    inst.then_inc(sems[ename], 1)
            if ename == "sp":
                ring = rings["sp"]
                K = ring["K"]
                for n in range(max(0, ring["n"] - K), ring["n"]):
                    k, v = resolve(("dma", "sp", n))
                    if v > known.get(k, 0):
                        engine.wait_ge(semof(k), v)
                        known[k] = v

        @block.tensor
        def _(e):
            run("pe", e)

        @block.scalar
        def _(e):
            run("act", e)

        @block.vector
        def _(e):
            run("dve", e)

        @block.gpsimd
        def _(e):
            run("pool", e)

        @block.sync
        def _(e):
            run("sp", e)


class Arena:
    def __init__(self, ap_bf16, nbytes):
        self.ap = ap_bf16
        self.nbytes = nbytes
        self.off = 0
        self.peak = 0

    def reset(self):
        self.off = 0

    def alloc(self, shape, dtype, parts=128):
        esz = 4 if dtype == F32 else 2
        n = 1
        for s in shape:
            n *= s
        nb = (n * esz + 31) // 32 * 32
        assert self.off + nb <= self.nbytes, f"arena overflow {self.off}+{nb}>{self.nbytes}"
        v = self.ap[0:parts, self.off // 2:(self.off + n * esz) // 2]
        self.off += nb
        self.peak = max(self.peak, self.off)
        if dtype == F32:
            v = v.bitcast(F32)
        if len(shape) == 2:
            v = v.rearrange("p (a b) -> p a b", a=shape[0])
        elif len(shape) == 3:
            v = v.rearrange("p (a b c) -> p a b c", a=shape[0], b=shape[1])
        return v


def _const_tables():
    i = np.arange(128)
    r, c = i[:, None], i[None, :]
    ident = (r == c)
    U = (r > c)
    tri = (r <= c)
    ones = np.ones((128, 128), bool)
    cf = np.concatenate([ident, U, tri, ones], axis=1).astype(np.float32)
    triGE = (r >= c)
    compl = (r < c)
    cmask = (r < c)
    cbmask = (c >= r)
    cb = np.concatenate([ident, ones, triGE, compl, cmask, cbmask], axis=1).astype(np.float32)
    return cf, cb


CF_IDENT, CF_U, CF_TRI, CF_ONES = 0, 128, 256, 384
CB_IDENT, CB_ONES, CB_TRIGE, CB_COMPL, CB_CMASK, CB_CBMASK = 0, 128, 256, 384, 512, 640
V_FFN = 0
V_MIX = 32
V_KV = 48
V_PLE = 56
V_FIN = 72
V_SSMN = 80
V_CW = 96
V_CB = 192
V_DTB = 216
V_ALOG = 248
V_D = 280
V_DFM = 312
NV = 328


def _vec_table(inp):
    def fm(v):
        v = np.asarray(v, np.float32).reshape(-1, 128)
        return v.T
    cols = []
    for i in range(2):
        for j in range(2):
            cols.append(fm(inp["ffn_norm"][i, j]))
    for i in range(2):
        cols.append(fm(inp["mix_norm"][i]))
    cols.append(fm(inp["kv_norm"]))
    for i in range(2):
        cols.append(fm(inp["ple_norm"][i]))
    cols.append(fm(inp["final_norm"]))
    cols.append(fm(inp["ssm_norm"][0]))
    for k in range(4):
        cols.append(fm(inp["ssm_conv_w"][0, k]))
    cols.append(fm(inp["ssm_conv_b"][0]))
    for nm in ("ssm_dt_bias", "ssm_a_log", "ssm_d"):
        cols.append(np.broadcast_to(np.asarray(inp[nm][0], np.float32)[None, :], (128, 32)))
    cols.append(fm(np.repeat(np.asarray(inp["ssm_d"][0], np.float32), 64)))
    out = np.ascontiguousarray(np.concatenate(cols, axis=1), dtype=np.float32)
    assert out.shape == (128, NV)
    return out


def build_program(NPB, S, NSB, PAST, phases=("ffn", "mamba", "ple", "kv", "attn"), nslot=5, work_kb=105.5):
    assert S % PASS_T == 0 and PAST % 128 == 0
    nc = bass.Bass("TRN2", target_bir_lowering=False)
    TS = NSB * 32

    def din(name, shape):
        return nc.dram_tensor(name, list(shape), F32, kind="ExternalInput").ap()

    def dout(name, shape):
        return nc.dram_tensor(name, list(shape), F32, kind="ExternalOutput").ap()

    xp = din("xp", [NPB, S, D])
    xs = din("xs", [TS, D])
    pp = din("pp", [2, NPB, S, PLE])
    psm = din("psm", [2, TS, PLE])
    sssm = din("sssm", [NSB, NH, HD, DST])
    sconv = din("sconv", [NSB, 3, CONVD])
    ck = din("ck", [NSB, PAST, 256])
    cv = din("cv", [NSB, PAST, 256])
    w_gate = din("w_gate", [2, 2, D, DFF])
    w_up = din("w_up", [2, 2, D, DFF])
    w_down = din("w_down", [2, 2, DFF, D])
    w_in = din("w_in", [D, INDIM])
    w_out = din("w_out", [DIN, D])
    w_k = din("w_k", [D, 256])
    w_v = din("w_v", [D, 256])
    w_q = din("w_q", [D, D])
    w_o = din("w_o", [D, D])
    w_pg = din("w_pg", [2, D, D])
    w_pp = din("w_pp", [2, PLE, D])
    cf_d = din("cf", [128, 512])
    cb_d = din("cb", [128, 768])
    vec_d = din("vec", [128, NV])

    y_p = dout("y_p", [NPB, S, D])
    y_s = dout("y_s", [TS, D])
    ssm_p = dout("ssm_p", [NPB, NH, HD, DST])
    conv_p = dout("conv_p", [NPB, 3, CONVD])
    k_p = dout("k_p", [NPB, S, 256])
    v_p = dout("v_p", [NPB, S, 256])
    ssm_s = dout("ssm_s", [NSB, NH, HD, DST])
    conv_s = dout("conv_s", [NSB, 3, CONVD])
    k_s = dout("k_s", [TS, 256])
    v_s = dout("v_s", [TS, 256])

    WORKB = int(work_kb * 1024)
    import contextlib
    es = contextlib.ExitStack()
    with es:
        def sb(name, shape, dt):
            return es.enter_context(nc.sbuf_tensor(name, list(shape), dt))

        X = sb("X", [128, KC, PASS_T], F32)
        CF = sb("CF", [128, 512], F32)
        CB = sb("CB", [128, 768], BF16)
        VEC = sb("VEC", [128, NV], F32)
        ABC = sb("ABC", [128, 32], F32)
        RING = sb("RING", [128, nslot, SLOT], BF16)
        STATE = sb("STATE", [128, DIN], F32)
        STATEB = sb("STATEB", [128, DIN], BF16)
        CTAIL = sb("CTAIL", [128, 24, 4, 3], F32)
        KT = sb("KT", [128, NG, S], BF16)
        VV = sb("VV", [128, S // 128, 256], BF16)
        WORK = sb("WORK", [128, WORKB // 2], BF16)
        PS = es.enter_context(nc.psum_tensor("PS", [128, 8, 512], F32))
        sems = {e: es.enter_context(nc.semaphore("s_" + e)) for e in Prog.ENG}
        P = Prog(nc, ring_k=8)
        ring_sems = {q: [es.enter_context(nc.semaphore(f"r_{q}{i}")) for i in range(8)] for q in ("sp", "pool")}
        block = es.enter_context(nc.Block())
        W = Arena(WORK, WORKB)

        ident_f = CF[:, CF_IDENT:CF_IDENT + 128]
        U_f = CF[:, CF_U:CF_U + 128]
        tri_f = CF[:, CF_TRI:CF_TRI + 128]
        ones_f = CF[:, CF_ONES:CF_ONES + 128]
        ident_b = CB[:, CB_IDENT:CB_IDENT + 128]
        ones_b = CB[:, CB_ONES:CB_ONES + 128]
        trige_b = CB[:, CB_TRIGE:CB_TRIGE + 128]
        compl_b = CB[:, CB_COMPL:CB_COMPL + 128]
        cmask_b = CB[:, CB_CMASK:CB_CMASK + 128]
        cbmask_b = CB[:, CB_CBMASK:CB_CBMASK + 128]

        def PSB(b):
            return PS[:, b, :]

        def PSBH(b):
            return PS[:, b, :].bitcast(BF16)

        def mm(out, lhsT, rhs, start, stop, reads, writes, skip=False):
            if skip:
                P.op("pe", lambda e: e.matmul(out, lhsT, rhs, start=start, stop=stop, skip_group_check=True),
                     reads, writes)
            else:
                P.op("pe", lambda e: e.matmul(out, lhsT, rhs, start=start, stop=stop), reads, writes)

        def tp(out, in_, ident, reads, writes):
            P.op("pe", lambda e: e.transpose(out, in_, ident), reads, writes)

        def act(out, in_, func, reads, writes, bias=None, scale=None):
            kw = {}
            if bias is not None:
                kw["bias"] = bias
            if scale is not None:
                kw["scale"] = scale
            P.op("act", lambda e: e.activation(out, in_, func, **kw), reads, writes)

        def tt(eng, out, in0, in1, op, reads, writes):
            P.op(eng, lambda e: e.tensor_tensor(out, in0, in1, op), reads, writes)

        def ts(eng, out, in0, s1, s2, op0, op1, reads, writes):
            if op1 is None:
                P.op(eng, lambda e: e.tensor_scalar(out, in0, s1, None, op0), reads, writes)
            else:
                P.op(eng, lambda e: e.tensor_scalar(out, in0, s1, s2, op0, op1), reads, writes)

        def stt(out, in0, scalar, in1, op0, op1, reads, writes):
            P.op("dve", lambda e: e.scalar_tensor_tensor(out, in0, scalar, in1, op0, op1), reads, writes)

        def cp(eng, out, in_, reads, writes):
            if eng == "act":
                P.op("act", lambda e: e.copy(out, in_), reads, writes)
            else:
                P.op(eng, lambda e: e.tensor_copy(out, in_), reads, writes)

        def memset(eng, ap, val, writes):
            P.op(eng, lambda e: e.memset(ap, val), (), writes)

        ring_state = {"n": 0}

        def wload(src_ap, shape, parts=128):
            s = ring_state["n"] % nslot
            ring_state["n"] += 1
            a, b = shape
            assert a * b <= SLOT
            dst = RING[0:parts, s, 0:a * b].rearrange("p (a b) -> p a b", a=a)
            key = ("ring", s)
            P.dma("pool", lambda e: e.dma_start(out=dst, in_=src_ap), (), (key,), nobarrier=True)
            return dst, key

        P.dma("sp", lambda e: e.dma_start(out=CF[:, :], in_=cf_d[:, :]), (), ("CF",))
        P.dma("sp", lambda e: e.dma_start(out=VEC[:, :], in_=vec_d[:, :]), (), ("VEC",))
        P.dma("pool", lambda e: e.dma_start(out=CB[:, :], in_=cb_d[:, :]), (), ("CB",))
        act(ABC[:, :], VEC[:, V_ALOG:V_ALOG + 32], AF.Exp, ("VEC",), ("ABC",))
        ts("dve", ABC[:, :], ABC[:, :], -1.0, None, ALU.mult, None, ("ABC",), ("ABC",))
        CONSTK = ("CF", "CB", "VEC", "ABC")

        def gcol(base, k):
            return VEC[:, base + k:base + k + 1]

        def phase_begin():
            P.barrier()
            W.reset()

        def tiles_of(T):
            TW = min(512, T)
            return TW, T // TW

        def rmsnorm_tile(H, vbase, t, TW, SQ, RS, psb):
            cs = slice(t * TW, (t + 1) * TW)
            act(SQ[:, :, 0:TW], X[:, :, cs], AF.Square, (("X", t),), ("SQ",))
            for k in range(KC):
                mm(PSB(psb)[:, 0:TW], ones_b, SQ[:, k, 0:TW], k == 0, k == KC - 1, ("SQ", "CB"), (("ps", psb),))
            act(RS[:, 0:TW], PSB(psb)[:, 0:TW], AF.Sqrt, (("ps", psb),), ("RS",), bias=EPSB[:, 0:1], scale=1.0 / D)
            P.op("dve", lambda e: e.reciprocal(RS[:, 0:TW], RS[:, 0:TW]), ("RS",), ("RS",))
            for k in range(KC):
                stt(H[:, k, cs], X[:, k, cs], gcol(vbase, k), RS[:, 0:TW], ALU.mult, ALU.mult,
                    (("X", t), "RS", "VEC"), (("H", t),))

        EPSB = sb("EPSB", [128, 1], F32)
        memset("dve", EPSB[:, :], EPS, ("EPSB",))
        HALFB = sb("HALFB", [128, 1], F32)
        memset("dve", HALFB[:, :], 0.5, ("HALFB",))

        def load_x(src_rows, T):
            phase_begin()
            STG = [W.alloc([D], F32) for _ in range(2)]
            for blk in range(T // 128):
                st = STG[blk % 2]
                sk = ("xstg", blk % 2)
                src = src_rows[blk * 128:(blk + 1) * 128, :]
                P.dma("sp", lambda e, st=st, src=src: e.dma_start(out=st, in_=src), (), (sk,))
                for half in range(2):
                    b = (blk * 2 + half) % 4
                    for j in range(4):
                        k = half * 4 + j
                        tp(PSB(b)[:, j * 128:(j + 1) * 128], st[:, k * 128:(k + 1) * 128], ident_f,
                           (sk, "CF"), (("ps", b),))
                    cp("act" if half == 0 else "dve",
                       X[:, half * 4:half * 4 + 4, blk * 128:(blk + 1) * 128],
                       PSB(b).rearrange("p (a b) -> p a b", a=4), (("ps", b),), (("X", blk // 4),))

        def store_y(dst_rows, T):
            phase_begin()
            TW, NT = tiles_of(T)
            SQ = W.alloc([KC, TW], BF16)
            RS = W.alloc([TW], F32)
            YN = W.alloc([KC, TW], F32)
            STG = [W.alloc([D], F32) for _ in range(2)]
            for t in range(NT):
                cs = slice(t * TW, (t + 1) * TW)
                act(SQ[:, :, 0:TW], X[:, :, cs], AF.Square, (("X", t),), ("SQ",))
                for k in range(KC):
                    mm(PSB(7)[:, 0:TW], ones_b, SQ[:, k, 0:TW], k == 0, k == KC - 1, ("SQ", "CB"), (("ps", 7),))
                act(RS[:, 0:TW], PSB(7)[:, 0:TW], AF.Sqrt, (("ps", 7),), ("RS",), bias=EPSB[:, 0:1], scale=1.0 / D)
                P.op("dve", lambda e: e.reciprocal(RS[:, 0:TW], RS[:, 0:TW]), ("RS",), ("RS",))
                for k in range(KC):
                    stt(YN[:, k, 0:TW], X[:, k, cs], gcol(V_FIN, k), RS[:, 0:TW], ALU.mult, ALU.mult,
                        (("X", t), "RS", "VEC"), ("YN",))
                for bl in range(TW // 128):
                    gb = t * (TW // 128) + bl
                    st = STG[gb % 2]
                    sk = ("ystg", gb % 2)
                    for half in range(2):
                        b = (gb * 2 + half) % 4
                        for j in range(4):
                            k = half * 4 + j
                            tp(PSB(b)[:, j * 128:(j + 1) * 128], YN[:, k, bl * 128:(bl + 1) * 128], ident_f,
                               ("YN", "CF"), (("ps", b),))
                        cp("act" if half == 0 else "dve", st[:, half * 512:(half + 1) * 512], PSB(b),
                           (("ps", b),), (sk,))
                    dst = dst_rows[gb * 128:(gb + 1) * 128, :]
                    P.dma("sp", lambda e, st=st, dst=dst: e.dma_start(out=dst, in_=st), (sk,), ())

        def ffn(i, j, T):
            phase_begin()
            TW, NT = tiles_of(T)
            H = W.alloc([KC, T], BF16)
            AB = W.alloc([FC, T], BF16)
            SQ = W.alloc([KC, TW], BF16)
            RS = W.alloc([TW], F32)
            SG = [W.alloc([TW], F32) for _ in range(2)]
            for t in range(NT):
                rmsnorm_tile(H, V_FFN + (i * 2 + j) * 8, t, TW, SQ, RS, 7)
            import os
            dbg = int(os.environ.get("FFN_DBG", "9"))
            if dbg < 1:
                return
            wg_d = w_gate[i, j].rearrange("(k p) n -> p k n", p=128)
            wu_d = w_up[i, j].rearrange("(k p) n -> p k n", p=128)
            wd_d = w_down[i, j].rearrange("(f p) n -> p f n", p=128)
            cnt = 0
            for fp in range(FC // 2):
                wg, kg = wload(wg_d[:, :, fp * 256:(fp + 1) * 256], (KC, 256))
                wu, ku = wload(wu_d[:, :, fp * 256:(fp + 1) * 256], (KC, 256))
                for t in range(NT):
                    cs = slice(t * TW, (t + 1) * TW)
                    for fi in range(2):
                        f = fp * 2 + fi
                        ba = (cnt * 2) % 6
                        bb = (cnt * 2 + 1) % 6
                        cnt += 1
                        for k in range(KC):
                            mm(PSB(ba)[:, 0:TW], wg[:, k, fi * 128:(fi + 1) * 128], H[:, k, cs], k == 0, k == KC - 1,
                               (kg, ("H", t)), (("ps", ba),))
                        for k in range(KC):
                            mm(PSB(bb)[:, 0:TW], wu[:, k, fi * 128:(fi + 1) * 128], H[:, k, cs], k == 0, k == KC - 1,
                               (ku, ("H", t)), (("ps", bb),))
                        sg = SG[cnt % 2]
                        sgk = ("SG", cnt % 2)
                        act(sg[:, 0:TW], PSB(ba)[:, 0:TW], AF.Silu, (("ps", ba),), (sgk,))
                        tt("dve", AB[:, f, cs], sg[:, 0:TW], PSB(bb)[:, 0:TW], ALU.mult, (sgk, ("ps", bb)),
                           (("AB", f, t),))
            if dbg < 2:
                return
            cnt = 0
            for m in range(KC):
                wd, kd = wload(wd_d[:, :, m * 128:(m + 1) * 128], (FC, 128))
                for t in range(NT):
                    cs = slice(t * TW, (t + 1) * TW)
                    b = 6 + (cnt % 2)
                    cnt += 1
                    if dbg < 3:
                        continue
                    for f in range(FC):
                        mm(PSB(b)[:, 0:TW], wd[:, f, :], AB[:, f, cs], f == 0, f == FC - 1,
                           (kd, ("AB", f, t)), (("ps", b),))
                    if dbg < 4:
                        continue
                    stt(X[:, m, cs], PSB(b)[:, 0:TW], HALFB[:, 0:1], X[:, m, cs], ALU.mult, ALU.add,
                        (("ps", b), ("X", t)), (("X", t),))

        def ple(i, prow, T):
            phase_begin()
            TW, NT = tiles_of(T)
            H = W.alloc([KC, T], BF16)
            PT = W.alloc([2, T], BF16)
            SQ = W.alloc([KC, TW], BF16)
            RS = W.alloc([TW], F32)
            SG = [W.alloc([TW], F32) for _ in range(2)]
            STG = [W.alloc([PLE], F32) for _ in range(2)]
            for t in range(NT):
                rmsnorm_tile(H, V_PLE + i * 8, t, TW, SQ, RS, 7)
            for blk in range(T // 128):
                st = STG[blk % 2]
                sk = ("pstg", blk % 2)
                src = prow[blk * 128:(blk + 1) * 128, :]
                P.dma("sp", lambda e, st=st, src=src: e.dma_start(out=st, in_=src), (), (sk,))
                b = 4 + blk % 2
                for k2 in range(2):
                    tp(PSB(b)[:, k2 * 128:(k2 + 1) * 128], st[:, k2 * 128:(k2 + 1) * 128], ident_f, (sk, "CF"),
                       (("ps", b),))
                cp("act", PT[:, :, blk * 128:(blk + 1) * 128],
                   PSB(b)[:, 0:256].rearrange("p (a b) -> p a b", a=2), (("ps", b),), ("PT",))
            wpp, kpp = wload(w_pp[i].rearrange("(k p) n -> p k n", p=128), (2, D))
            wg_d = w_pg[i].rearrange("(k p) n -> p k n", p=128)
            cnt = 0
            for mp in range(4):
                wg, kg = wload(wg_d[:, :, mp * 256:(mp + 1) * 256], (KC, 256))
                for mi in range(2):
                    m = mp * 2 + mi
                    for t in range(NT):
                        cs = slice(t * TW, (t + 1) * TW)
                        ba = (cnt * 2) % 4
                        bb = (cnt * 2 + 1) % 4
                        cnt += 1
                        for k in range(KC):
                            mm(PSB(ba)[:, 0:TW], wg[:, k, mi * 128:(mi + 1) * 128], H[:, k, cs], k == 0, k == KC - 1,
                               (kg, ("H", t)), (("ps", ba),))
                        for k2 in range(2):
                            mm(PSB(bb)[:, 0:TW], wpp[:, k2, m * 128:(m + 1) * 128], PT[:, k2, cs], k2 == 0, k2 == 1,
                               (kpp, "PT"), (("ps", bb),))
                        sg = SG[cnt % 2]
                        sgk = ("SG", cnt % 2)
                        act(sg[:, 0:TW], PSB(ba)[:, 0:TW], AF.Sigmoid, (("ps", ba),), (sgk,))
                        tt("dve", sg[:, 0:TW], sg[:, 0:TW], PSB(bb)[:, 0:TW], ALU.mult, (sgk, ("ps", bb)), (sgk,))
                        tt("dve", X[:, m, cs], X[:, m, cs], sg[:, 0:TW], ALU.add, (sgk, ("X", t)), (("X", t),))

        def kv(T, pos0, kdst, vdst, KTdst, Vdst_fn, RB=128, dup=False):
            phase_begin()
            TW, NT = tiles_of(T)
            H = W.alloc([KC, T], BF16)
            SQ = W.alloc([KC, TW], BF16)
            RS = W.alloc([TW], F32)
            KVO = [W.alloc([512], F32) for _ in range(2)]
            for t in range(NT):
                rmsnorm_tile(H, V_KV, t, TW, SQ, RS, 7)
            wk, kk = wload(w_k.rearrange("(k p) n -> p k n", p=128), (KC, 256))
            wv, kvk = wload(w_v.rearrange("(k p) n -> p k n", p=128), (KC, 256))
            import os
            kdbg = int(os.environ.get("KV_DBG", "9"))
            cnt = 0
            for h in range(NG if kdbg >= 1 else 0):
                for t in range(NT):
                    cs = slice(t * TW, (t + 1) * TW)
                    b = cnt % 2
                    cnt += 1
                    for k in range(KC):
                        mm(PSB(b)[0:64, 0:TW], wk[:, k, h * 64:(h + 1) * 64], H[:, k, cs], k == 0, k == KC - 1,
                           (kk, ("H", t)), (("ps", b),))
                    cp("act", KTdst[0:64, h, cs], PSB(b)[0:64, 0:TW], (("ps", b),), ("KT",))
            if dup:
                P.dma("sp", lambda e: e.dma_start(out=KTdst[64:128, :, :], in_=KTdst[0:64, :, :]), ("KT",), ("KT",))
            for blk in range(T // RB if kdbg >= 2 else 0):
                b = 2 + blk % 2
                bs = slice(blk * RB, (blk + 1) * RB)
                t = (blk * RB) // TW
                for k in range(KC):
                    mm(PSB(b)[0:RB, 0:256], H[:, k, bs], wk[:, k, :], k == 0, k == KC - 1, (kk, ("H", t)), (("ps", b),))
                for k in range(KC):
                    mm(PSB(b)[0:RB, 256:512], H[:, k, bs], wv[:, k, :], k == 0, k == KC - 1, (kvk, ("H", t)),
                       (("ps", b),))
                kvo = KVO[blk % 2]
                kk2 = ("KVO", blk % 2)
                cp("act", kvo[0:RB, :], PSB(b)[0:RB, :], (("ps", b),), (kk2,))
                cp("dve", Vdst_fn(blk), PSB(b)[0:RB, 256:512], (("ps", b),), ("VV",))
                if kdbg < 3:
                    continue
                kd = kdst[blk * RB:(blk + 1) * RB, :]
                vd = vdst[blk * RB:(blk + 1) * RB, :]
                P.dma("sp", lambda e, kvo=kvo, kd=kd: e.dma_start(out=kd, in_=kvo[0:RB, 0:256]), (kk2,), ())
                P.dma("sp", lambda e, kvo=kvo, vd=vd: e.dma_start(out=vd, in_=kvo[0:RB, 256:512]), (kk2,), ())

        def mamba(T, mode, seq_first, seq_last, bidx):
            phase_begin()
            TW, NT = tiles_of(T)
            prompt = (mode == "p")
            nseg, seglen = (1, TW) if prompt else (NSB, 32)
            Lc = 128 if prompt else 32
            H = W.alloc([KC, TW], BF16)
            RS = W.alloc([TW], F32)
            XBC = W.alloc([24, TW], BF16)
            SQ = XBC[:, 0:KC, :]
            sqkeys = tuple(("XBC", m_) for m_ in range(KC))
            YT = W.alloc([16, TW], BF16)
            up_off = W.off
            UP = [W.alloc([nseg, seglen + 3], F32) for _ in range(3)]
            CVT = [W.alloc([nseg, seglen], F32) for _ in range(2)]
            upcvt_bytes = W.off - up_off
            XS_TM = W.alloc([DIN], BF16)
            XDT2 = [W.alloc([DIN], BF16)] * 2
            XDTW = W.alloc([DIN], BF16)
            B_TM = W.alloc([512], BF16)
            SM2 = [W.alloc([8, 32], F32)] * 2
            CBM2 = [W.alloc([NG, 128], BF16)] * 2
            ATRI = [W.alloc([8, 128], F32) for _ in range(2)]
            ESEG = [W.alloc([8, 128], BF16) for _ in range(2)]
            MT = [W.alloc([8, 128], BF16) for _ in range(2)]
            YOFF0 = W.alloc([DIN], F32)
            YB = W.alloc([DIN], BF16)
            if prompt and upcvt_bytes >= DIN * 4:
                YOFF1 = W.ap[:, up_off // 2:up_off // 2 + DIN * 2].bitcast(F32)
                yk1 = tuple(("UP", i_) for i_ in range(3)) + tuple(("UPT", i_) for i_ in range(3)) + \
                    tuple(("CVT", i_) for i_ in range(2))
            else:
                YOFF1 = W.alloc([DIN], F32)
                yk1 = ("YOFF1",)
            YOFF2 = [YOFF0, YOFF1]
            YK2 = [("YOFF",), yk1]
            SIO = YOFF0.rearrange("p (a n) -> p a n", a=16)

            win_d = w_in.rearrange("(k p) n -> p k n", p=128)
            wout_d = w_out.rearrange("(k p) n -> p k n", p=128)

            def seg_view(ap3, L):
                return ap3[:, :, 0:L]

            if prompt and seq_first:
                memset("dve", STATE[:, :], 0.0, ("STATE",))
                memset("dve", STATEB[:, :], 0.0, ("STATEB",))
                memset("dve", CTAIL[:, :, :, :], 0.0, tuple(("CTAIL", m_) for m_ in range(24)))
            if not prompt:
                for s in range(NSB):
                    for tt_ in range(3):
                        src = sconv[s, tt_, :].rearrange("(m p) -> p m", p=128)
                        P.dma("sp", lambda e, src=src, s=s, tt_=tt_: e.dma_start(
                            out=CTAIL[:, :, s, tt_], in_=src, allow_slow_non_contiguous=True), (), tuple(("CTAIL", m_) for m_ in range(24)))

            for t in range(NT):
                cs = slice(t * TW, (t + 1) * TW)
                rmsnorm_tile_local(H, V_MIX + 0, t, TW, SQ, RS, sqkeys)
                wps = {}

                def stA(m):
                    if m % 2 == 0:
                        pc = m // 2
                        wps[pc] = wload(win_d[:, :, DIN + pc * 256:DIN + (pc + 1) * 256], (KC, 256))
                    wp, kp = wps[m // 2]
                    mi = m % 2
                    b = m % 2
                    up = UP[m % 3]
                    upk = ("UP", m % 3)
                    for k in range(KC):
                        mm(PSB(b)[:, 0:TW], wp[:, k, mi * 128:(mi + 1) * 128], H[:, k, 0:TW], k == 0, k == KC - 1,
                           (kp, "Hm"), (("ps", b),))
                    cp("dve", up[:, :, 0:3], CTAIL[:, m, 0:nseg, :], (("CTAIL", m),), (("UPT", m % 3),))
                    cp("act", up[:, :, 3:3 + seglen],
                       PSB(b)[:, 0:TW].rearrange("p (s l) -> p s l", s=nseg), (("ps", b),), (upk,))
                    cp("act", CTAIL[:, m, 0:nseg, :], up[:, :, seglen:seglen + 3], (upk, ("UPT", m % 3)), (("CTAIL", m),))

                def stB1(m):
                    up = UP[m % 3]
                    upk = ("UP", m % 3)
                    cvt = CVT[m % 2]
                    cvk = ("CVT", m % 2)
                    act(cvt[:, :, :], up[:, :, 0:seglen], AF.Identity, (upk, ("UPT", m % 3), "VEC"), (cvk,),
                        bias=gcol(V_CB, m), scale=gcol(V_CW + 0 * 24, m))
                    for kk_ in range(1, 4):
                        stt(cvt[:, :, :], up[:, :, kk_:kk_ + seglen], gcol(V_CW + kk_ * 24, m), cvt[:, :, :],
                            ALU.mult, ALU.add, (upk, ("UPT", m % 3), cvk, "VEC"), (cvk,))

                def stB2(m):
                    cvt = CVT[m % 2]
                    cvk = ("CVT", m % 2)
                    act(XBC[:, m, 0:TW].rearrange("p (s l) -> p s l", s=nseg), cvt[:, :, :], AF.Silu,
                        (cvk,), (("XBC", m),))

                stA(0)
                for m in range(24):
                    if m + 1 < 24:
                        stA(m + 1)
                    stB1(m)
                    if m >= 1:
                        stB2(m - 1)
                stB2(23)
                wdt, kdt = wload(win_d[:, :, DIN + CONVD:INDIM], (KC, 32))
                zpre = [wload(win_d[:, :, pc_ * 256:(pc_ + 1) * 256], (KC, 256)) for pc_ in range(4)]
                import os
                nchunk = TW // Lc
                if os.environ.get("MAMBA_NOCHUNK"):
                    nchunk = 0
                L = Lc

                def stage1(c):
                    c0 = c * Lc
                    cc = slice(c0, c0 + Lc)
                    SM = SM2[c % 2]
                    smk = lambda i_: ("SM", 0, i_)
                    XDT = XDT2[c % 2]
                    xdk = ("XDT", 0)
                    CBM = CBM2[c % 2]
                    cbk = ("CBM", 0)
                    YOFF = YOFF2[c % 2]
                    yk = YK2[c % 2]
                    if not prompt:
                        for two in range(2):
                            src = sssm[c].rearrange("(hp two) p n -> two p hp n", two=2)[two]
                            P.dma("sp", lambda e, src=src, two=two: e.dma_start(
                                out=SIO[two * 64:(two + 1) * 64, :, :], in_=src), (), ("YOFF",))
                        for hp in range(16):
                            b = 4 + (hp // 4) % 2
                            tp(PSB(b)[:, (hp % 4) * 128:(hp % 4 + 1) * 128], SIO[:, hp, :], ident_f, ("YOFF", "CF"),
                               (("ps", b),))
                            if hp % 4 == 3:
                                q4 = hp // 4
                                cp("dve", STATE[:, q4 * 512:(q4 + 1) * 512], PSB(b), (("ps", b),), ("STATE",))
                                cp("act", STATEB[:, q4 * 512:(q4 + 1) * 512], PSB(b), (("ps", b),), ("STATEB",))
                    for k in range(KC):
                        mm(PSB(6)[0:L, 0:32], H[:, k, cc], wdt[:, k, :], k == 0, k == KC - 1, (kdt, "Hm"), (("ps", 6),))
                    tt("dve", SM[0:L, 0, :], PSB(6)[0:L, 0:32], VEC[0:L, V_DTB:V_DTB + 32], ALU.add,
                       (("ps", 6), "VEC"), (smk(0),))
                    act(SM[0:L, 1, :], SM[0:L, 0, :], AF.Exp, (smk(0),), (smk(1),))
                    act(SM[0:L, 2, :], SM[0:L, 1, :], AF.Ln, (smk(1),), (smk(2),), bias=ONEB[0:L, 0:1])
                    tt("dve", SM[0:L, 3, :], SM[0:L, 2, :], ABC[0:L, :], ALU.mult, (smk(2), "ABC"), (smk(3),))
                    mm(PSB(6)[0:L, 32:64], tri_f[0:L, 0:L], SM[0:L, 3, :], True, True, (smk(3), "CF"), (("ps", 6),))
                    mm(PSB(6)[:, 64:96], ones_f[0:L, :], SM[0:L, 3, :], True, True, (smk(3), "CF"), (("ps", 6),))
                    cp("dve", SM[0:L, 4, :], PSB(6)[0:L, 32:64], (("ps", 6),), (smk(4),))
                    act(SM[0:L, 5, :], PSB(6)[0:L, 32:64], AF.Exp, (("ps", 6),), (smk(5),))
                    tt("dve", SM[0:L, 6, :], PSB(6)[0:L, 64:96], SM[0:L, 4, :], ALU.subtract, (("ps", 6), smk(4)),
                       (smk(6),))
                    act(SM[0:L, 6, :], SM[0:L, 6, :], AF.Exp, (smk(6),), (smk(6),))
                    act(SM[:, 7, :], PSB(6)[:, 64:96], AF.Exp, (("ps", 6),), (smk(7),))
                    yield
                    for m in range(16):
                        b = 4 + m // 8
                        tp(PSBH(b)[0:L, (m % 8) * 128:(m % 8 + 1) * 128], XBC[:, m, cc], ident_b,
                           (("XBC", m), "CB"), (("ps", b),))
                    for hb in range(2):
                        b = 4 + hb
                        hs = slice(hb * 1024, (hb + 1) * 1024)
                        tt("dve", XDT[0:L, hs].rearrange("p (h d) -> p h d", d=HD),
                           PSBH(b)[0:L, :].rearrange("p (h d) -> p h d", d=HD),
                           SM[0:L, 2, hb * 16:(hb + 1) * 16].unsqueeze(2).to_broadcast([L, 16, HD]), ALU.mult,
                           (("ps", b), smk(2)), (xdk,))
                    tt("dve", XDTW[0:L, :].rearrange("p (h d) -> p h d", d=HD),
                       XDT[0:L, :].rearrange("p (h d) -> p h d", d=HD),
                       SM[0:L, 6, :].unsqueeze(2).to_broadcast([L, NH, HD]), ALU.mult, (xdk, smk(6)), ("XDTW",))
                    yield
                    for g in range(NG):
                        tp(PSBH(7)[0:L, g * 128:(g + 1) * 128], XBC[:, 16 + g, cc], ident_b,
                           (("XBC", 16 + g), "CB"), (("ps", 7),))
                    cp("act", B_TM[0:L, :], PSBH(7)[0:L, 0:512], (("ps", 7),), ("B_TM",))
                    for g in range(NG):
                        mm(PSB(7)[0:L, g * L:(g + 1) * L], XBC[:, 16 + g, cc], XBC[:, 20 + g, cc], True, True,
                           (("XBC", 16 + g), ("XBC", 20 + g)), (("ps", 7),))
                    tt("dve", CBM[0:L, :, 0:L], PSB(7)[0:L, 0:NG * L].rearrange("p (g i) -> p g i", g=NG),
                       cbmask_b[0:L, 0:L].unsqueeze(1).to_broadcast([L, NG, L]), ALU.mult, (("ps", 7), "CB"), (cbk,))
                    for g in range(NG):
                        mm(PSB(g)[0:L, :], XBC[:, 20 + g, cc], STATEB[:, g * 512:(g + 1) * 512], True, True,
                           (("XBC", 20 + g), "STATEB"), (("ps", g),))
                        tt("dve", YOFF[0:L, g * 512:(g + 1) * 512].rearrange("p (h d) -> p h d", d=HD),
                           PSB(g)[0:L, :].rearrange("p (h d) -> p h d", d=HD),
                           SM[0:L, 5, g * 8:(g + 1) * 8].unsqueeze(2).to_broadcast([L, 8, HD]), ALU.mult,
                           (("ps", g), smk(5)), yk)
                    yield
                    for g in range(NG):
                        mm(PSB(g)[:, :], B_TM[0:L, g * 128:(g + 1) * 128], XDTW[0:L, g * 512:(g + 1) * 512], True, True,
                           ("B_TM", "XDTW"), (("ps", g),))
                    tt("dve", STATE[:, :].rearrange("p (h d) -> p h d", d=HD),
                       STATE[:, :].rearrange("p (h d) -> p h d", d=HD),
                       SM[:, 7, :].unsqueeze(2).to_broadcast([128, NH, HD]), ALU.mult, ("STATE", smk(7)), ("STATE",))
                    for g in range(NG):
                        tt("dve", STATE[:, g * 512:(g + 1) * 512], STATE[:, g * 512:(g + 1) * 512], PSB(g)[:, :], ALU.add,
                           ("STATE", ("ps", g)), ("STATE",))
                    cp("act", STATEB[:, :], STATE[:, :], ("STATE",), ("STATEB",))
                    yield

                def stage2(c):
                    c0 = c * Lc
                    cc = slice(c0, c0 + Lc)
                    SM = SM2[c % 2]
                    smk = lambda i_: ("SM", 0, i_)
                    XDT = XDT2[c % 2]
                    xdk = ("XDT", 0)
                    CBM = CBM2[c % 2]
                    cbk = ("CBM", 0)
                    YOFF = YOFF2[c % 2]
                    yk = YK2[c % 2]
                    def hgA(hg):
                        at = ATRI[hg % 2]
                        atk = ("ATRI", hg % 2)
                        eg = ESEG[hg % 2]
                        egk = ("ESEG", hg % 2)
                        tt("dve", at[0:L, :, 0:L], SM[0:L, 3, hg * 8:(hg + 1) * 8].unsqueeze(2).to_broadcast([L, 8, L]),
                           tri_f[0:L, 0:L].unsqueeze(1).to_broadcast([L, 8, L]), ALU.mult, (smk(3), "CF"), (atk,))
                        hper = min(max(1, 512 // L), 8)
                        for q0 in range(0, 8, hper):
                            b = 4 + (q0 // hper) % 2
                            mm(PSB(b)[0:L, 0:hper * L].rearrange("p (h i) -> p h i", h=hper), U_f[0:L, 0:L],
                               at[0:L, q0:q0 + hper, 0:L], True, True, (atk, "CF"), (("ps", b),))
                            act(eg[0:L, q0:q0 + hper, 0:L], PSB(b)[0:L, 0:hper * L].rearrange("p (h i) -> p h i", h=hper),
                                AF.Exp, (("ps", b),), (egk,))

                    def hgB(hg):
                        eg = ESEG[hg % 2]
                        egk = ("ESEG", hg % 2)
                        mt = MT[hg % 2]
                        mtk = ("MT", hg % 2)
                        tt("dve", mt[0:L, :, 0:L], eg[0:L, :, 0:L],
                           CBM[0:L, hg, 0:L].unsqueeze(1).to_broadcast([L, 8, L]), ALU.mult, (egk, cbk), (mtk,))
                        for h8 in range(8):
                            h = hg * 8 + h8
                            mm(PSB(hg)[0:L, h8 * 64:(h8 + 1) * 64], mt[0:L, h8, 0:L], XDT[0:L, h * 64:(h + 1) * 64],
                               True, True, (mtk, xdk), (("ps", hg),))
                        tt("dve", YB[0:L, hg * 512:(hg + 1) * 512], PSB(hg)[0:L, :], YOFF[0:L, hg * 512:(hg + 1) * 512],
                           ALU.add, (("ps", hg),) + yk, ("YB",))

                    hgA(0)
                    for hg in range(NG):
                        if hg + 1 < NG:
                            hgA(hg + 1)
                        hgB(hg)
                    yield
                    for m in range(16):
                        b = 4 + m // 8
                        tp(PSBH(b)[:, (m % 8) * Lc:(m % 8) * Lc + L], YB[0:L, m * 128:(m + 1) * 128], ident_b[0:L, 0:L],
                           ("YB", "CB"), (("ps", b),))
                    for hb in range(2):
                        b = 4 + hb
                        cp("act", YT[:, hb * 8:(hb + 1) * 8, cc],
                           PSBH(b)[:, 0:8 * Lc].rearrange("p (m l) -> p m l", m=8), (("ps", b),), ("YT",))
                    yield

                def drain(g):
                    for _ in g:
                        pass

                if False:
                    if nchunk:
                        drain(stage1(0))
                    for c in range(nchunk):
                        g2 = stage2(c)
                        g1 = stage1(c + 1) if c + 1 < nchunk else iter(())
                        d1 = d2 = False
                        while not (d1 and d2):
                            if not d1:
                                try:
                                    next(g1)
                                except StopIteration:
                                    d1 = True
                            if not d2:
                                try:
                                    next(g2)
                                except StopIteration:
                                    d2 = True
                else:
                    for c in range(nchunk):
                        drain(stage1(c))
                        drain(stage2(c))
                        if not prompt:
                            store_state(ssm_s[c], SIO)
                cnt = 0
                for pc in range(8):
                    wp, kp = zpre[pc] if pc < 4 else wload(win_d[:, :, pc * 256:(pc + 1) * 256], (KC, 256))
                    for mi in range(2):
                        m = pc * 2 + mi
                        b = cnt % 2
                        cvt = CVT[cnt % 2]
                        cvk = ("CVT", cnt % 2)
                        cnt += 1
                        for k in range(KC):
                            mm(PSB(b)[:, 0:TW], wp[:, k, mi * 128:(mi + 1) * 128], H[:, k, 0:TW], k == 0, k == KC - 1,
                               (kp, "Hm"), (("ps", b),))
                        zf = cvt.rearrange("p s l -> p (s l)")
                        act(zf[:, 0:TW], PSB(b)[:, 0:TW], AF.Silu, (("ps", b),), (cvk,))
                        stt(YT[:, m, 0:TW], XBC[:, m, 0:TW], gcol(V_DFM, m), YT[:, m, 0:TW], ALU.mult, ALU.add,
                            (("XBC", m), "YT", "VEC"), ("YT",))
                        tt("dve", YT[:, m, 0:TW], YT[:, m, 0:TW], zf[:, 0:TW], ALU.mult, ("YT", cvk), ("YT",))
                SQ2 = XBC
                act(SQ2[:, 0:16, 0:TW], YT[:, :, 0:TW], AF.Square, ("YT",), tuple(("XBC", m) for m in range(16)))
                for g in range(NG):
                    for c4 in range(4):
                        mm(PSB(g)[:, 0:TW], ones_b, SQ2[:, g * 4 + c4, 0:TW], c4 == 0, c4 == 3,
                           (("XBC", g * 4 + c4), "CB"), (("ps", g),))
                    rsg = CVT[g % 2].rearrange("p s l -> p (s l)")
                    rk = ("CVT", g % 2)
                    act(rsg[:, 0:TW], PSB(g)[:, 0:TW], AF.Sqrt, (("ps", g),), (rk,), bias=EPSB[:, 0:1], scale=1.0 / 512)
                    P.op("dve", lambda e, rsg=rsg: e.reciprocal(rsg[:, 0:TW], rsg[:, 0:TW]), (rk,), (rk,))
                    for c4 in range(4):
                        m = g * 4 + c4
                        stt(YT[:, m, 0:TW], YT[:, m, 0:TW], gcol(V_SSMN, m), rsg[:, 0:TW], ALU.mult, ALU.mult,
                            ("YT", rk, "VEC"), ("YT",))
                cnt = 0
                for m in range(KC):
                    wo_, ko = wload(wout_d[:, :, m * 128:(m + 1) * 128], (16, 128))
                    b = 6 + cnt % 2
                    cnt += 1
                    for k in range(16):
                        mm(PSB(b)[:, 0:TW], wo_[:, k, :], YT[:, k, 0:TW], k == 0, k == 15, (ko, "YT"), (("ps", b),))
                    tt("dve", X[:, m, cs], X[:, m, cs], PSB(b)[:, 0:TW], ALU.add, (("ps", b), ("X", t)), (("X", t),))
            if prompt and seq_last:
                store_state(ssm_p[bidx], SIO)
                for tt_ in range(3):
                    dst = conv_p[bidx, tt_, :].rearrange("(m p) -> p m", p=128)
                    P.dma("sp", lambda e, dst=dst, tt_=tt_: e.dma_start(
                        out=dst, in_=CTAIL[:, :, 0, tt_], allow_slow_non_contiguous=True), tuple(("CTAIL", m_) for m_ in range(24)), ())
            if not prompt:
                for s in range(NSB):
                    for tt_ in range(3):
                        dst = conv_s[s, tt_, :].rearrange("(m p) -> p m", p=128)
                        P.dma("sp", lambda e, dst=dst, s=s, tt_=tt_: e.dma_start(
                            out=dst, in_=CTAIL[:, :, s, tt_], allow_slow_non_contiguous=True), tuple(("CTAIL", m_) for m_ in range(24)), ())

        def store_state(dst3, SIO):
            for hp in range(16):
                b = 4 + (hp // 4) % 2
                tp(PSB(b)[:, (hp % 4) * 128:(hp % 4 + 1) * 128], STATE[:, hp * 128:(hp + 1) * 128], ident_f,
                   ("STATE", "CF"), (("ps", b),))
                if hp % 4 == 3:
                    q4 = hp // 4
                    cp("dve", SIO[:, q4 * 4:(q4 + 1) * 4, :], PSB(b).rearrange("p (a n) -> p a n", a=4), (("ps", b),),
                       ("YOFF",))
            for two in range(2):
                dst = dst3.rearrange("(hp two) p n -> two p hp n", two=2)[two]
                P.dma("sp", lambda e, dst=dst, two=two: e.dma_start(out=dst, in_=SIO[two * 64:(two + 1) * 64, :, :]),
                      ("YOFF",), ())

        ONEB = sb("ONEB", [128, 1], F32)
        memset("dve", ONEB[:, :], 1.0, ("ONEB",))

        def rmsnorm_tile_local(H, vbase, t, TW, SQ, RS, sqkeys=("SQ",)):
            cs = slice(t * TW, (t + 1) * TW)
            act(SQ[:, :, 0:TW], X[:, :, cs], AF.Square, (("X", t),), sqkeys)
            for k in range(KC):
                mm(PSB(7)[:, 0:TW], ones_b, SQ[:, k, 0:TW], k == 0, k == KC - 1, sqkeys + ("CB",), (("ps", 7),))
            act(RS[:, 0:TW], PSB(7)[:, 0:TW], AF.Sqrt, (("ps", 7),), ("RS",), bias=EPSB[:, 0:1], scale=1.0 / D)
            P.op("dve", lambda e: e.reciprocal(RS[:, 0:TW], RS[:, 0:TW]), ("RS",), ("RS",))
            for k in range(KC):
                stt(H[:, k, 0:TW], X[:, k, cs], gcol(vbase, k), RS[:, 0:TW], ALU.mult, ALU.mult,
                    (("X", t), "RS", "VEC"), ("Hm",))

        def attn(T, mode, pos0, kt_new=None, v_new=None):
            phase_begin()
            TW, NT = tiles_of(T)
            prompt = (mode == "p")
            H = W.alloc([KC, TW], BF16)
            SQ = W.alloc([KC, TW], BF16)
            RS = W.alloc([TW], F32)
            QT = W.alloc([16, TW], BF16)
            OTS = W.alloc([8, TW], BF16)
            NEE, NSP, NWW, NAA, NVB = 8, 8, 4, 8, 8
            EE = [W.alloc([512], F32) for _ in range(NEE)]
            SP = [W.alloc([512], BF16) for _ in range(NSP)]
            WW = [W.alloc([512], F32) for _ in range(NWW)]
            AA = [W.alloc([512], BF16) for _ in range(NAA)]
            if not prompt:
                KSTG = [W.alloc([256], F32) for _ in range(2)]
                VSTG = [W.alloc([256], F32) for _ in range(2)]
                KTB = [W.alloc([NG, 128], BF16, parts=64) for _ in range(2)]
                VB = [W.alloc([256], BF16) for _ in range(NVB)]
            wq_d = w_q.rearrange("(k p) n -> p k n", p=128)
            for t in range(NT):
                cs = slice(t * TW, (t + 1) * TW)
                rmsnorm_tile_local(H, V_MIX + 8, t, TW, SQ, RS)
                cnt = 0
                for pc in range(4):
                    wp, kp = wload(wq_d[:, :, pc * 256:(pc + 1) * 256], (KC, 256))
                    for hi in range(4):
                        h = pc * 4 + hi
                        b = cnt % 2
                        cnt += 1
                        for k in range(KC):
                            mm(PSB(b)[0:64, 0:TW], wp[:, k, hi * 64:(hi + 1) * 64], H[:, k, 0:TW], k == 0, k == KC - 1,
                               (kp, "Hm"), (("ps", b),))
                        P.op("act", lambda e, h=h, b=b: e.mul(QT[0:64, h, 0:TW], PSB(b)[0:64, 0:TW], 0.125),
                             (("ps", b),), ("QT",))
                if prompt:
                    P.dma("sp", lambda e: e.dma_start(out=QT[64:128, :, 0:TW], in_=QT[0:64, :, 0:TW]), ("QT",), ("QT",))
                units = []
                if prompt:
                    for qb in range(TW // 128):
                        gq = (pos0 + t * TW) // 128 + qb
                        for kb in range(gq, -1, -1):
                            for h in range(NG):
                                units.append({"chain": (qb, h), "cidx": h, "first": kb == gq, "last": kb == 0,
                                              "diag": kb == gq, "h": h, "qb": qb, "kb": kb, "nk": 128})
                else:
                    nkb = PAST // 128
                    for s0 in range(0, NSB, 2):
                        for kb in range(nkb, -1, -1):
                            for s in range(s0, min(s0 + 2, NSB)):
                                units.append({"chain": (s,), "cidx": s - s0, "first": kb == nkb, "last": kb == 0,
                                              "diag": kb == nkb, "s": s, "kb": kb, "nk": 32 if kb == nkb else 128})
                NU = len(units)
                lastseen = {}
                for i, u in enumerate(units):
                    u["prev"] = lastseen.get(u["chain"])
                    lastseen[u["chain"]] = i

                def zbank(i):
                    return i % 2

                def pbank(u):
                    return 2 + u["cidx"]

                def emit_load(i):
                    u = units[i]
                    if prompt or u["diag"]:
                        return
                    s, kb = u["s"], u["kb"]
                    ks, vs = KSTG[i % 2], VSTG[i % 2]
                    kk_, vk_ = ("KSTG", i % 2), ("VSTG", i % 2)
                    srck = ck[s, kb * 128:(kb + 1) * 128, :]
                    srcv = cv[s, kb * 128:(kb + 1) * 128, :]
                    P.dma("sp", lambda e, ks=ks, srck=srck: e.dma_start(out=ks, in_=srck), (), (kk_,))
                    P.dma("sp", lambda e, vs=vs, srcv=srcv: e.dma_start(out=vs, in_=srcv), (), (vk_,))
                    for h in range(NG):
                        tp(PSB(zbank(i))[0:64, h * 128:(h + 1) * 128], ks[:, h * 64:(h + 1) * 64], ident_f, (kk_, "CF"),
                           (("ps", zbank(i)),))
                    cp("dve", KTB[i % 2][:, :, :], PSB(zbank(i))[0:64, :].rearrange("p (h k) -> p h k", h=NG),
                       (("ps", zbank(i)),), (("KTB", i % 2),))
                    cp("act", VB[i % NVB][:, :], vs, (vk_,), (("VB", i % NVB),))

                def kview(u, i, h):
                    if prompt:
                        lo = 64 * (i % 2)
                        return KT[lo:lo + 64, h, u["kb"] * 128:(u["kb"] + 1) * 128], ("KT",)
                    if u["diag"]:
                        return kt_new[0:64, h, u["s"] * 32:(u["s"] + 1) * 32], ("KT",)
                    return KTB[i % 2][:, h, :], (("KTB", i % 2),)

                def vview(u, i, h):
                    if prompt:
                        return VV[:, u["kb"], h * 64:(h + 1) * 64], ("VV",)
                    if u["diag"]:
                        return v_new[:, u["s"], h * 64:(h + 1) * 64], ("VV",)
                    return VB[i % NVB][:, h * 64:(h + 1) * 64], (("VB", i % NVB),)

                def maskmul(buf, key, nk):
                    if prompt:
                        msk = cmask_b[0:nk, 0:128].unsqueeze(1).to_broadcast([nk, 4, 128])
                        v3 = buf[0:nk, :].rearrange("p (g q) -> p g q", g=4)
                    else:
                        msk = cmask_b[0:nk, 0:32].unsqueeze(1).to_broadcast([nk, 16, 32])
                        v3 = buf[0:nk, :].rearrange("p (g q) -> p g q", g=16)
                    tt("dve", v3, v3, msk, ALU.mult, (key, "CB"), (key,))

                def emit_Zmm(i):
                    u = units[i]
                    nk = u["nk"]
                    b = zbank(i)
                    emit_load(i)
                    if prompt:
                        h, qb = u["h"], u["qb"]
                        lhs, lk = kview(u, i, h)
                        lo = 64 * (i % 2)
                        mm(PSB(b)[0:nk, :].rearrange("p (g q) -> p g q", g=4), lhs,
                           QT[lo:lo + 64, h * 4:(h + 1) * 4, qb * 128:(qb + 1) * 128], True, True, lk + ("QT",),
                           (("ps", b),))
                    else:
                        s = u["s"]
                        for h in range(NG):
                            lhs, lk = kview(u, i, h)
                            mm(PSB(b)[0:nk, h * 128:(h + 1) * 128].rearrange("p (g q) -> p g q", g=4), lhs,
                               QT[0:64, h * 4:(h + 1) * 4, s * 32:(s + 1) * 32], True, True, lk + ("QT",), (("ps", b),))

                def emit_Zact(i):
                    u = units[i]
                    nk = u["nk"]
                    b = zbank(i)
                    ee, sp_ = EE[i % NEE], SP[i % NSP]
                    act(ee[0:nk, :], PSB(b)[0:nk, :], AF.Exp, (("ps", b),), (("EE", i % NEE),))
                    act(sp_[0:nk, :], ee[0:nk, :], AF.Ln, (("EE", i % NEE),), (("SP", i % NSP),), bias=ONEB[0:nk, 0:1])
                    if u["diag"]:
                        maskmul(sp_, ("SP", i % NSP), nk)

                def emit_Cmm(i):
                    u = units[i]
                    nk = u["nk"]
                    pb = pbank(u)
                    sp_ = SP[i % NSP]
                    if u["first"]:
                        mm(PSB(pb)[0:128, :], trige_b[0:nk, 0:128], sp_[0:nk, :], True, True, (("SP", i % NSP), "CB"),
                           (("ps", pb),))
                    else:
                        pi = u["prev"]
                        pnk = units[pi]["nk"]
                        spp = SP[pi % NSP]
                        mm(PSB(pb)[0:128, :], compl_b[0:pnk, 0:128], spp[0:pnk, :], False, False,
                           (("SP", pi % NSP), "CB"), (("ps", pb),), skip=True)
                        mm(PSB(pb)[0:128, :], trige_b[0:nk, 0:128], sp_[0:nk, :], False, True, (("SP", i % NSP), "CB"),
                           (("ps", pb),), skip=True)

                def emit_Cact(i):
                    u = units[i]
                    nk = u["nk"]
                    pb = pbank(u)
                    ww = WW[i % NWW]
                    act(ww[0:nk, :], PSB(pb)[0:nk, :], AF.Exp, (("ps", pb),), (("WW", i % NWW),), scale=-1.0)
                    aa = AA[i % NAA]
                    tt("dve", aa[0:nk, :], EE[i % NEE][0:nk, :], ww[0:nk, :], ALU.mult,
                       (("EE", i % NEE), ("WW", i % NWW)), (("AA", i % NAA),))
                    if u["diag"]:
                        maskmul(aa, ("AA", i % NAA), nk)

                def emit_O(i):
                    u = units[i]
                    nk = u["nk"]
                    aa = AA[i % NAA]
                    if prompt:
                        h = u["h"]
                        ob, ph = 6 + h // 2, h % 2
                        rv, rk = vview(u, i, h)
                        mm(PSB(ob)[ph * 64:(ph + 1) * 64, :], rv, aa[0:nk, :], u["first"], u["last"],
                           rk + (("AA", i % NAA),), (("ps", ob),))
                        if u["last"]:
                            qb = u["qb"]
                            j0 = (h // 2) * 4
                            cp("act", OTS[ph * 64:(ph + 1) * 64, j0:j0 + 4, qb * 128:(qb + 1) * 128],
                               PSB(ob)[ph * 64:(ph + 1) * 64, :].rearrange("p (g q) -> p g q", g=4), (("ps", ob),),
                               ("OTS",))
                    else:
                        s = u["s"]
                        ob = 6 + u["cidx"]
                        for h in range(NG):
                            ph = h % 2
                            rv, rk = vview(u, i, h)
                            c0 = (h // 2) * 128
                            mm(PSB(ob)[ph * 64:(ph + 1) * 64, c0:c0 + 128], rv, aa[0:nk, h * 128:(h + 1) * 128],
                               u["first"] and h // 2 == 0, u["last"], rk + (("AA", i % NAA),), (("ps", ob),), skip=True)
                        if u["last"]:
                            cp("act", OTS[:, :, s * 32:(s + 1) * 32],
                               PSB(ob)[:, 0:256].rearrange("p (j q) -> p j q", j=8), (("ps", ob),), ("OTS",))

                NS = (NU + 1) // 2
                for st in range(NS + 3):
                    for i_ in (2 * (st - 3), 2 * (st - 3) + 1):
                        if 0 <= i_ < NU:
                            emit_O(i_)
                    for i_ in (2 * st, 2 * st + 1):
                        if i_ < NU:
                            emit_Zmm(i_)
                    for i_ in (2 * st, 2 * st + 1):
                        if i_ < NU:
                            emit_Zact(i_)
                    for i_ in (2 * (st - 1), 2 * (st - 1) + 1):
                        if 0 <= i_ < NU:
                            emit_Cmm(i_)
                    for i_ in (2 * (st - 1), 2 * (st - 1) + 1):
                        if 0 <= i_ < NU:
                            emit_Cact(i_)
                wo_d = w_o.rearrange("(a two g p) n -> two p a g n", two=2, g=4, p=64)
                cnt = 0
                for m in range(KC):
                    sl = ring_state["n"] % nslot
                    ring_state["n"] += 1
                    ko = ("ring", sl)
                    for two in range(2):
                        for a_ in range(2):
                            dst = RING[two * 64:(two + 1) * 64, sl, a_ * 512:(a_ + 1) * 512].rearrange(
                                "p (g n) -> p g n", g=4)
                            src = wo_d[two][:, a_, :, m * 128:(m + 1) * 128]
                            P.dma("pool", lambda e, dst=dst, src=src: e.dma_start(out=dst, in_=src), (), (ko,),
                                  nobarrier=True)
                    wo_ = RING[:, sl, 0:1024].rearrange("p (j n) -> p j n", j=8)
                    b = cnt % 2
                    cnt += 1
                    for j in range(8):
                        mm(PSB(b)[:, 0:TW], wo_[:, j, :], OTS[:, j, 0:TW], j == 0, j == 7, (ko, "OTS"), (("ps", b),))
                    tt("dve", X[:, m, cs], X[:, m, cs], PSB(b)[:, 0:TW], ALU.add, (("ps", b), ("X", t)), (("X", t),))

        KTN = KT[:, :, 0:128]
        VN = VV[0:32, 0:4, :]

        def run_pass(mode, bidx, half):
            prompt = mode == "p"
            T = PASS_T if prompt else TS
            pos0 = half * PASS_T if prompt else PAST
            if prompt:
                xrows = xp[bidx, pos0:pos0 + T, :]
                yrows = y_p[bidx, pos0:pos0 + T, :]
                prow = [pp[i, bidx, pos0:pos0 + T, :] for i in range(2)]
            else:
                xrows, yrows = xs, y_s
                prow = [psm[i] for i in range(2)]
            load_x(xrows, T)
            if "ffn" in phases:
                ffn(0, 0, T)
            if "mamba" in phases:
                mamba(T, mode, half == 0, pos0 + T == S, bidx)
            if "ffn" in phases:
                ffn(0, 1, T)
            if "ple" in phases:
                ple(0, prow[0], T)
            if "kv" in phases:
                if prompt:
                    kv(T, pos0, k_p[bidx, pos0:pos0 + T, :], v_p[bidx, pos0:pos0 + T, :], KT[:, :, pos0:pos0 + T],
                       lambda blk: VV[:, pos0 // 128 + blk, :], dup=True)
                else:
                    kv(T, pos0, k_s, v_s, KTN[:, :, 0:T], lambda blk: VN[:, blk, :], RB=32)
            if "ffn" in phases:
                ffn(1, 0, T)
            if "attn" in phases:
                if prompt:
                    attn(T, mode, pos0)
                else:
                    attn(T, mode, pos0, KTN, VN)
            if "ffn" in phases:
                ffn(1, 1, T)
            if "ple" in phases:
                ple(1, prow[1], T)
            store_y(yrows, T)

        for b in range(NPB):
            for half in range(S // PASS_T):
                run_pass("p", b, half)
        if NSB > 0:
            run_pass("s", 0, 0)

        P.barrier()
        P.emit(block, sems, ring_sems)
    return nc


_W_NAMES = {
    "w_gate": "ffn_w_gate", "w_up": "ffn_w_up", "w_down": "ffn_w_down",
}


def make_in_map(inp, pb, sb_, cf, cb, vec):
    f = lambda a: np.ascontiguousarray(a, dtype=np.float32)
    xs = f(inp["x_sample"][sb_])
    ps_ = f(inp["p_sample"][:, sb_])
    m = {
        "xp": f(inp["x_prompt"][pb]),
        "xs": xs.reshape(-1, D),
        "pp": f(inp["p_prompt"][:, pb]),
        "psm": ps_.reshape(2, -1, PLE),
        "sssm": f(inp["state_ssm"][0, sb_]),
        "sconv": f(inp["state_conv"][0, sb_]),
        "ck": f(inp["cache_k"][sb_]).reshape(xs.shape[0], -1, 256),
        "cv": f(inp["cache_v"][sb_]).reshape(xs.shape[0], -1, 256),
        "w_gate": f(inp["ffn_w_gate"]), "w_up": f(inp["ffn_w_up"]), "w_down": f(inp["ffn_w_down"]),
        "w_in": f(inp["ssm_w_in"][0]), "w_out": f(inp["ssm_w_out"][0]),
        "w_k": f(inp["w_k"]), "w_v": f(inp["w_v"]), "w_q": f(inp["sb_w_q"][0]), "w_o": f(inp["sb_w_o"][0]),
        "w_pg": f(inp["ple_w_gate"]), "w_pp": f(inp["ple_w_proj"]),
        "cf": cf, "cb": cb, "vec": vec,
    }
    return m


def run(inp, n_cores, phases=("ffn", "mamba", "ple", "kv", "attn"), trace=False):
    inp = {k: np.asarray(v) for k, v in inp.items()}
    B, S = inp["x_prompt"].shape[0], inp["x_prompt"].shape[1]
    BS = inp["x_sample"].shape[0]
    PAST = inp["cache_k"].shape[1]
    NPB, NSB = B // n_cores, BS // n_cores
    nc = build_program(NPB, S, NSB, PAST, phases=phases)
    cf, cb = _const_tables()
    vec = _vec_table(inp)
    in_maps = []
    for c in range(n_cores):
        in_maps.append(make_in_map(inp, slice(c * NPB, (c + 1) * NPB), slice(c * NSB, (c + 1) * NSB), cf, cb, vec))
    res = run_bass_kernel_spmd(nc, in_maps, core_ids=list(range(n_cores)), trace=trace)
    R = res.results
    cat = lambda k: np.concatenate([r[k] for r in R], axis=0)
    y_p = cat("y_p")
    y_s = cat("y_s").reshape(BS, 32, D)
    ssm_p = cat("ssm_p")[None]
    conv_p = cat("conv_p")[None]
    k_p = cat("k_p").reshape(B, S, 4, 64)
    v_p = cat("v_p").reshape(B, S, 4, 64)
    ssm_s = cat("ssm_s")[None]
    conv_s = cat("conv_s")[None]
    k_s = cat("k_s").reshape(BS, 32, 4, 64)
    v_s = cat("v_s").reshape(BS, 32, 4, 64)
    outs = (y_p, y_s, ssm_p, conv_p, k_p, v_p, ssm_s, conv_s, k_s, v_s)
    outs = tuple(np.ascontiguousarray(o, dtype=np.float32) for o in outs)
    return outs, res


def kernel(**inputs):
    outs, _ = run(inputs, 8)
    return outs
```

```python
import numpy as np
import concourse.bass as bass
import concourse.mybir as mybir
from concourse.bass_utils import run_bass_kernel_spmd

F32 = mybir.dt.float32
BF16 = mybir.dt.bfloat16
AF = mybir.ActivationFunctionType
ALU = mybir.AluOpType

D = 1024
KC = 8
DFF = 2816
FC = 22
DIN = 2048
NH = 32
HD = 64
NG = 4
DST = 128
CONVD = 3072
INDIM = 5152
PLE = 256
EPS = 1e-6
SLOT = 2816
PASS_T = 1024


class Prog:
    ENG = ("pe", "act", "dve", "pool", "sp")

    def __init__(self, nc, ring_k=8):
        self.nc = nc
        self.ops = {e: [] for e in self.ENG}
        self.last_w = {}
        self.readers = {}
        self.rings = {"sp": {"n": 0, "K": ring_k}, "pool": {"n": 0, "K": ring_k}}
        self.gdeps = set()

    def _deps(self, eng, reads, writes, is_dma, nobarrier):
        deps = set()
        for k in reads:
            lw = self.last_w.get(k)
            if lw is not None:
                deps.add(lw)
        for k in writes:
            lw = self.last_w.get(k)
            if lw is not None and not (lw[0] == "eng" and lw[1] == eng and not is_dma):
                deps.add(lw)
            for r in self.readers.get(k, ()):
                if not (r[0] == "eng" and r[1] == eng and not is_dma):
                    deps.add(r)
        if not nobarrier:
            deps |= self.gdeps
        return deps

    def op(self, eng, fn, reads=(), writes=(), nobarrier=False):
        psr = tuple(k for k in reads if isinstance(k, tuple) and k[0] == "ps" and k not in writes)
        writes = tuple(writes) + psr
        deps = self._deps(eng, reads, writes, False, nobarrier)
        idx = len(self.ops[eng])
        ref = ("eng", eng, idx)
        self.ops[eng].append({"fn": fn, "deps": deps, "dma": None, "inc": False})
        self._record(ref, reads, writes)
        return ref

    def dma(self, q, fn, reads=(), writes=(), nobarrier=False):
        ring = self.rings[q]
        n = ring["n"]
        ring["n"] += 1
        deps = self._deps(q, reads, writes, True, nobarrier)
        if n >= ring["K"]:
            deps.add(("dma", q, n - ring["K"]))
        ref = ("dma", q, n)
        self.ops[q].append({"fn": fn, "deps": deps, "dma": n, "inc": False})
        self._record(ref, reads, writes)
        return ref

    def _record(self, ref, reads, writes):
        for k in reads:
            self.readers.setdefault(k, []).append(ref)
        for k in writes:
            self.last_w[k] = ref
            self.readers[k] = []

    def barrier(self):
        g = set()
        for e in self.ENG:
            for i in range(len(self.ops[e]) - 1, -1, -1):
                if self.ops[e][i]["dma"] is None:
                    g.add(("eng", e, i))
                    break
        ring = self.rings["sp"]
        for n in range(max(0, ring["n"] - ring["K"]), ring["n"]):
            g.add(("dma", "sp", n))
        self.gdeps = g

    def emit(self, block, sems, ring_sems):
        for e in self.ENG:
            for o in self.ops[e]:
                for d in o["deps"]:
                    if d[0] == "eng":
                        self.ops[d[1]][d[2]]["inc"] = True
        counts = {}
        for e in self.ENG:
            c = 0
            cl = []
            for o in self.ops[e]:
                if o["inc"] and o["dma"] is None:
                    c += 1
                cl.append(c)
            counts[e] = cl
        rings = self.rings

        def resolve(d):
            if d[0] == "eng":
                return (("e", d[1]), counts[d[1]][d[2]])
            K = rings[d[1]]["K"]
            return (("r", d[1], d[2] % K), 16 * (d[2] // K + 1))

        def semof(key):
            if key[0] == "e":
                return sems[key[1]]
            return ring_sems[key[1]][key[2]]

        def run(ename, engine):
            known = {}
            for o in self.ops[ename]:
                need = {}
                for d in o["deps"]:
                    k, v = resolve(d)
                    if v > need.get(k, 0):
                        need[k] = v
                for k, v in need.items():
                    if v > known.get(k, 0):
                        engine.wait_ge(semof(k), v)
                        known[k] = v
                inst = o["fn"](engine)
                if o["dma"] is not None:
                    K = rings[ename]["K"]
                    inst.then_inc(ring_sems[ename][o["dma"] % K], 16)
                elif o["inc"]:
                    inst.then_inc(sems[ename], 1)
            if ename == "sp":
                ring = rings["sp"]
                K = ring["K"]
                for n in range(max(0, ring["n"] - K), ring["n"]):
                    k, v = resolve(("dma", "sp", n))
                    if v > known.get(k, 0):
                        engine.wait_ge(semof(k), v)
                        known[k] = v

        @block.tensor
        def _(e):
            run("pe", e)

        @block.scalar
        def _(e):
            run("act", e)

        @block.vector
        def _(e):
            run("dve", e)

        @block.gpsimd
        def _(e):
            run("pool", e)

        @block.sync
        def _(e):
            run("sp", e)


class Arena:
    def __init__(self, ap_bf16, nbytes):
        self.ap = ap_bf16
        self.nbytes = nbytes
        self.off = 0
        self.peak = 0

    def reset(self):
        self.off = 0

    def alloc(self, shape, dtype, parts=128):
        esz = 4 if dtype == F32 else 2
        n = 1
        for s in shape:
            n *= s
        nb = (n * esz + 31) // 32 * 32
        assert self.off + nb <= self.nbytes, f"arena overflow {self.off}+{nb}>{self.nbytes}"
        v = self.ap[0:parts, self.off // 2:(self.off + n * esz) // 2]
        self.off += nb
        self.peak = max(self.peak, self.off)
        if dtype == F32:
            v = v.bitcast(F32)
        if len(shape) == 2:
            v = v.rearrange("p (a b) -> p a b", a=shape[0])
        elif len(shape) == 3:
            v = v.rearrange("p (a b c) -> p a b c", a=shape[0], b=shape[1])
        return v


def _const_tables():
    i = np.arange(128)
    r, c = i[:, None], i[None, :]
    ident = (r == c)
    U = (r > c)
    tri = (r <= c)
    ones = np.ones((128, 128), bool)
    cf = np.concatenate([ident, U, tri, ones], axis=1).astype(np.float32)
    triGE = (r >= c)
    compl = (r < c)
    cmask = (r < c)
    cbmask = (c >= r)
    cb = np.concatenate([ident, ones, triGE, compl, cmask, cbmask], axis=1).astype(np.float32)
    return cf, cb


CF_IDENT, CF_U, CF_TRI, CF_ONES = 0, 128, 256, 384
CB_IDENT, CB_ONES, CB_TRIGE, CB_COMPL, CB_CMASK, CB_CBMASK = 0, 128, 256, 384, 512, 640
V_FFN = 0
V_MIX = 32
V_KV = 48
V_PLE = 56
V_FIN = 72
V_SSMN = 80
V_CW = 96
V_CB = 192
V_DTB = 216
V_ALOG = 248
V_D = 280
V_DFM = 312
NV = 328


def _vec_table(inp):
    def fm(v):
        v = np.asarray(v, np.float32).reshape(-1, 128)
        return v.T
    cols = []
    for i in range(2):
        for j in range(2):
            cols.append(fm(inp["ffn_norm"][i, j]))
    for i in range(2):
        cols.append(fm(inp["mix_norm"][i]))
    cols.append(fm(inp["kv_norm"]))
    for i in range(2):
        cols.append(fm(inp["ple_norm"][i]))
    cols.append(fm(inp["final_norm"]))
    cols.append(fm(inp["ssm_norm"][0]))
    for k in range(4):
        cols.append(fm(inp["ssm_conv_w"][0, k]))
    cols.append(fm(inp["ssm_conv_b"][0]))
    for nm in ("ssm_dt_bias", "ssm_a_log", "ssm_d"):
        cols.append(np.broadcast_to(np.asarray(inp[nm][0], np.float32)[None, :], (128, 32)))
    cols.append(fm(np.repeat(np.asarray(inp["ssm_d"][0], np.float32), 64)))
    out = np.ascontiguousarray(np.concatenate(cols, axis=1), dtype=np.float32)
    assert out.shape == (128, NV)
    return out


def build_program(NPB, S, NSB, PAST, phases=("ffn", "mamba", "ple", "kv", "attn"), nslot=5, work_kb=105.5):
    assert S % PASS_T == 0 and PAST % 128 == 0
    nc = bass.Bass("TRN2", target_bir_lowering=False)
    TS = NSB * 32

    def din(name, shape):
        return nc.dram_tensor(name, list(shape), F32, kind="ExternalInput").ap()

    def dout(name, shape):
        return nc.dram_tensor(name, list(shape), F32, kind="ExternalOutput").ap()

    xp = din("xp", [NPB, S, D])
    xs = din("xs", [TS, D])
    pp = din("pp", [2, NPB, S, PLE])
    psm = din("psm", [2, TS, PLE])
    sssm = din("sssm", [NSB, NH, HD, DST])
    sconv = din("sconv", [NSB, 3, CONVD])
    ck = din("ck", [NSB, PAST, 256])
    cv = din("cv", [NSB, PAST, 256])
    w_gate = din("w_gate", [2, 2, D, DFF])
    w_up = din("w_up", [2, 2, D, DFF])
    w_down = din("w_down", [2, 2, DFF, D])
    w_in = din("w_in", [D, INDIM])
    w_out = din("w_out", [DIN, D])
    w_k = din("w_k", [D, 256])
    w_v = din("w_v", [D, 256])
    w_q = din("w_q", [D, D])
    w_o = din("w_o", [D, D])
    w_pg = din("w_pg", [2, D, D])
    w_pp = din("w_pp", [2, PLE, D])
    cf_d = din("cf", [128, 512])
    cb_d = din("cb", [128, 768])
    vec_d = din("vec", [128, NV])

    y_p = dout("y_p", [NPB, S, D])
    y_s = dout("y_s", [TS, D])
    ssm_p = dout("ssm_p", [NPB, NH, HD, DST])
    conv_p = dout("conv_p", [NPB, 3, CONVD])
    k_p = dout("k_p", [NPB, S, 256])
    v_p = dout("v_p", [NPB, S, 256])
    ssm_s = dout("ssm_s", [NSB, NH, HD, DST])
    conv_s = dout("conv_s", [NSB, 3, CONVD])
    k_s = dout("k_s", [TS, 256])
    v_s = dout("v_s", [TS, 256])

    WORKB = int(work_kb * 1024)
    import contextlib
    es = contextlib.ExitStack()
    with es:
        def sb(name, shape, dt):
            return es.enter_context(nc.sbuf_tensor(name, list(shape), dt))

        X = sb("X", [128, KC, PASS_T], F32)
        CF = sb("CF", [128, 512], F32)
        CB = sb("CB", [128, 768], BF16)
        VEC = sb("VEC", [128, NV], F32)
        ABC = sb("ABC", [128, 32], F32)
        RING = sb("RING", [128, nslot, SLOT], BF16)
        STATE = sb("STATE", [128, DIN], F32)
        STATEB = sb("STATEB", [128, DIN], BF16)
        CTAIL = sb("CTAIL", [128, 24, 4, 3], F32)
        KT = sb("KT", [128, NG, S], BF16)
        VV = sb("VV", [128, S // 128, 256], BF16)
        WORK = sb("WORK", [128, WORKB // 2], BF16)
        PS = es.enter_context(nc.psum_tensor("PS", [128, 8, 512], F32))
        sems = {e: es.enter_context(nc.semaphore("s_" + e)) for e in Prog.ENG}
        P = Prog(nc, ring_k=8)
        ring_sems = {q: [es.enter_context(nc.semaphore(f"r_{q}{i}")) for i in range(8)] for q in ("sp", "pool")}
        block = es.enter_context(nc.Block())
        W = Arena(WORK, WORKB)

        ident_f = CF[:, CF_IDENT:CF_IDENT + 128]
        U_f = CF[:, CF_U:CF_U + 128]
        tri_f = CF[:, CF_TRI:CF_TRI + 128]
        ones_f = CF[:, CF_ONES:CF_ONES + 128]
        ident_b = CB[:, CB_IDENT:CB_IDENT + 128]
        ones_b = CB[:, CB_ONES:CB_ONES + 128]
        trige_b = CB[:, CB_TRIGE:CB_TRIGE + 128]
        compl_b = CB[:, CB_COMPL:CB_COMPL + 128]
        cmask_b = CB[:, CB_CMASK:CB_CMASK + 128]
        cbmask_b = CB[:, CB_CBMASK:CB_CBMASK + 128]

        def PSB(b):
            return PS[:, b, :]

        def PSBH(b):
            return PS[:, b, :].bitcast(BF16)

        def mm(out, lhsT, rhs, start, stop, reads, writes, skip=False):
            if skip:
                P.op("pe", lambda e: e.matmul(out, lhsT, rhs, start=start, stop=stop, skip_group_check=True),
                     reads, writes)
            else:
                P.op("pe", lambda e: e.matmul(out, lhsT, rhs, start=start, stop=stop), reads, writes)

        def tp(out, in_, ident, reads, writes):
            P.op("pe", lambda e: e.transpose(out, in_, ident), reads, writes)

        def act(out, in_, func, reads, writes, bias=None, scale=None):
            kw = {}
            if bias is not None:
                kw["bias"] = bias
            if scale is not None:
                kw["scale"] = scale
            P.op("act", lambda e: e.activation(out, in_, func, **kw), reads, writes)

        def tt(eng, out, in0, in1, op, reads, writes):
            P.op(eng, lambda e: e.tensor_tensor(out, in0, in1, op), reads, writes)

        def ts(eng, out, in0, s1, s2, op0, op1, reads, writes):
            if op1 is None:
                P.op(eng, lambda e: e.tensor_scalar(out, in0, s1, None, op0), reads, writes)
            else:
                P.op(eng, lambda e: e.tensor_scalar(out, in0, s1, s2, op0, op1), reads, writes)

        def stt(out, in0, scalar, in1, op0, op1, reads, writes):
            P.op("dve", lambda e: e.scalar_tensor_tensor(out, in0, scalar, in1, op0, op1), reads, writes)

        def cp(eng, out, in_, reads, writes):
            if eng == "act":
                P.op("act", lambda e: e.copy(out, in_), reads, writes)
            else:
                P.op(eng, lambda e: e.tensor_copy(out, in_), reads, writes)

        def memset(eng, ap, val, writes):
            P.op(eng, lambda e: e.memset(ap, val), (), writes)

        ring_state = {"n": 0}

        def wload(src_ap, shape, parts=128):
            s = ring_state["n"] % nslot
            ring_state["n"] += 1
            a, b = shape
            assert a * b <= SLOT
            dst = RING[0:parts, s, 0:a * b].rearrange("p (a b) -> p a b", a=a)
            key = ("ring", s)
            P.dma("pool", lambda e: e.dma_start(out=dst, in_=src_ap), (), (key,), nobarrier=True)
            return dst, key

        P.dma("sp", lambda e: e.dma_start(out=CF[:, :], in_=cf_d[:, :]), (), ("CF",))
        P.dma("sp", lambda e: e.dma_start(out=VEC[:, :], in_=vec_d[:, :]), (), ("VEC",))
        P.dma("pool", lambda e: e.dma_start(out=CB[:, :], in_=cb_d[:, :]), (), ("CB",))
        act(ABC[:, :], VEC[:, V_ALOG:V_ALOG + 32], AF.Exp, ("VEC",), ("ABC",))
        ts("dve", ABC[:, :], ABC[:, :], -1.0, None, ALU.mult, None, ("ABC",), ("ABC",))
        CONSTK = ("CF", "CB", "VEC", "ABC")

        def gcol(base, k):
            return VEC[:, base + k:base + k + 1]

        def phase_begin():
            P.barrier()
            W.reset()

        def tiles_of(T):
            TW = min(512, T)
            return TW, T // TW

        def rmsnorm_tile(H, vbase, t, TW, SQ, RS, psb):
            cs = slice(t * TW, (t + 1) * TW)
            act(SQ[:, :, 0:TW], X[:, :, cs], AF.Square, (("X", t),), ("SQ",))
            for k in range(KC):
                mm(PSB(psb)[:, 0:TW], ones_b, SQ[:, k, 0:TW], k == 0, k == KC - 1, ("SQ", "CB"), (("ps", psb),))
            act(RS[:, 0:TW], PSB(psb)[:, 0:TW], AF.Sqrt, (("ps", psb),), ("RS",), bias=EPSB[:, 0:1], scale=1.0 / D)
            P.op("dve", lambda e: e.reciprocal(RS[:, 0:TW], RS[:, 0:TW]), ("RS",), ("RS",))
            for k in range(KC):
                stt(H[:, k, cs], X[:, k, cs], gcol(vbase, k), RS[:, 0:TW], ALU.mult, ALU.mult,
                    (("X", t), "RS", "VEC"), (("H", t),))

        EPSB = sb("EPSB", [128, 1], F32)
        memset("dve", EPSB[:, :], EPS, ("EPSB",))
        HALFB = sb("HALFB", [128, 1], F32)
        memset("dve", HALFB[:, :], 0.5, ("HALFB",))

        def load_x(src_rows, T):
            phase_begin()
            STG = [W.alloc([D], F32) for _ in range(2)]
            for blk in range(T // 128):
                st = STG[blk % 2]
                sk = ("xstg", blk % 2)
                src = src_rows[blk * 128:(blk + 1) * 128, :]
                P.dma("sp", lambda e, st=st, src=src: e.dma_start(out=st, in_=src), (), (sk,))
                for half in range(2):
                    b = (blk * 2 + half) % 4
                    for j in range(4):
                        k = half * 4 + j
                        tp(PSB(b)[:, j * 128:(j + 1) * 128], st[:, k * 128:(k + 1) * 128], ident_f,
                           (sk, "CF"), (("ps", b),))
                    cp("act" if half == 0 else "dve",
                       X[:, half * 4:half * 4 + 4, blk * 128:(blk + 1) * 128],
                       PSB(b).rearrange("p (a b) -> p a b", a=4), (("ps", b),), (("X", blk // 4),))

        def store_y(dst_rows, T):
            phase_begin()
            TW, NT = tiles_of(T)
            SQ = W.alloc([KC, TW], BF16)
            RS = W.alloc([TW], F32)
            YN = W.alloc([KC, TW], F32)
            STG = [W.alloc([D], F32) for _ in range(2)]
            for t in range(NT):
                cs = slice(t * TW, (t + 1) * TW)
                act(SQ[:, :, 0:TW], X[:, :, cs], AF.Square, (("X", t),), ("SQ",))
                for k in range(KC):
                    mm(PSB(7)[:, 0:TW], ones_b, SQ[:, k, 0:TW], k == 0, k == KC - 1, ("SQ", "CB"), (("ps", 7),))
                act(RS[:, 0:TW], PSB(7)[:, 0:TW], AF.Sqrt, (("ps", 7),), ("RS",), bias=EPSB[:, 0:1], scale=1.0 / D)
                P.op("dve", lambda e: e.reciprocal(RS[:, 0:TW], RS[:, 0:TW]), ("RS",), ("RS",))
                for k in range(KC):
                    stt(YN[:, k, 0:TW], X[:, k, cs], gcol(V_FIN, k), RS[:, 0:TW], ALU.mult, ALU.mult,
                        (("X", t), "RS", "VEC"), ("YN",))
                for bl in range(TW // 128):
                    gb = t * (TW // 128) + bl
                    st = STG[gb % 2]
                    sk = ("ystg", gb % 2)
                    for half in range(2):
                        b = (gb * 2 + half) % 4
                        for j in range(4):
                            k = half * 4 + j
                            tp(PSB(b)[:, j * 128:(j + 1) * 128], YN[:, k, bl * 128:(bl + 1) * 128], ident_f,
                               ("YN", "CF"), (("ps", b),))
                        cp("act" if half == 0 else "dve", st[:, half * 512:(half + 1) * 512], PSB(b),
                           (("ps", b),), (sk,))
                    dst = dst_rows[gb * 128:(gb + 1) * 128, :]
                    P.dma("sp", lambda e, st=st, dst=dst: e.dma_start(out=dst, in_=st), (sk,), ())

        def ffn(i, j, T):
            phase_begin()
            TW, NT = tiles_of(T)
            H = W.alloc([KC, T], BF16)
            AB = W.alloc([FC, T], BF16)
            SQ = W.alloc([KC, TW], BF16)
            RS = W.alloc([TW], F32)
            SG = [W.alloc([TW], F32) for _ in range(2)]
            for t in range(NT):
                rmsnorm_tile(H, V_FFN + (i * 2 + j) * 8, t, TW, SQ, RS, 7)
            import os
            dbg = int(os.environ.get("FFN_DBG", "9"))
            if dbg < 1:
                return
            wg_d = w_gate[i, j].rearrange("(k p) n -> p k n", p=128)
            wu_d = w_up[i, j].rearrange("(k p) n -> p k n", p=128)
            wd_d = w_down[i, j].rearrange("(f p) n -> p f n", p=128)
            cnt = 0
            for fp in range(FC // 2):
                wg, kg = wload(wg_d[:, :, fp * 256:(fp + 1) * 256], (KC, 256))
                wu, ku = wload(wu_d[:, :, fp * 256:(fp + 1) * 256], (KC, 256))
                for t in range(NT):
                    cs = slice(t * TW, (t + 1) * TW)
                    for fi in range(2):
                        f = fp * 2 + fi
                        ba = (cnt * 2) % 6
                        bb = (cnt * 2 + 1) % 6
                        cnt += 1
                        for k in range(KC):
                            mm(PSB(ba)[:, 0:TW], wg[:, k, fi * 128:(fi + 1) * 128], H[:, k, cs], k == 0, k == KC - 1,
                               (kg, ("H", t)), (("ps", ba),))
                        for k in range(KC):
                            mm(PSB(bb)[:, 0:TW], wu[:, k, fi * 128:(fi + 1) * 128], H[:, k, cs], k == 0, k == KC - 1,
                               (ku, ("H", t)), (("ps", bb),))
                        sg = SG[cnt % 2]
                        sgk = ("SG", cnt % 2)
                        act(sg[:, 0:TW], PSB(ba)[:, 0:TW], AF.Silu, (("ps", ba),), (sgk,))
                        tt("dve", AB[:, f, cs], sg[:, 0:TW], PSB(bb)[:, 0:TW], ALU.mult, (sgk, ("ps", bb)),
                           (("AB", f, t),))
            if dbg < 2:
                return
            cnt = 0
            for m in range(KC):
                wd, kd = wload(wd_d[:, :, m * 128:(m + 1) * 128], (FC, 128))
                for t in range(NT):
                    cs = slice(t * TW, (t + 1) * TW)
                    b = 6 + (cnt % 2)
                    cnt += 1
                    if dbg < 3:
                        continue
                    for f in range(FC):
                        mm(PSB(b)[:, 0:TW], wd[:, f, :], AB[:, f, cs], f == 0, f == FC - 1,
                           (kd, ("AB", f, t)), (("ps", b),))
                    if dbg < 4:
                        continue
                    stt(X[:, m, cs], PSB(b)[:, 0:TW], HALFB[:, 0:1], X[:, m, cs], ALU.mult, ALU.add,
                        (("ps", b), ("X", t)), (("X", t),))

        def ple(i, prow, T):
            phase_begin()
            TW, NT = tiles_of(T)
            H = W.alloc([KC, T], BF16)
            PT = W.alloc([2, T], BF16)
            SQ = W.alloc([KC, TW], BF16)
            RS = W.alloc([TW], F32)
            SG = [W.alloc([TW], F32) for _ in range(2)]
            STG = [W.alloc([PLE], F32) for _ in range(2)]
            for t in range(NT):
                rmsnorm_tile(H, V_PLE + i * 8, t, TW, SQ, RS, 7)
            for blk in range(T // 128):
                st = STG[blk % 2]
                sk = ("pstg", blk % 2)
                src = prow[blk * 128:(blk + 1) * 128, :]
                P.dma("sp", lambda e, st=st, src=src: e.dma_start(out=st, in_=src), (), (sk,))
                b = 4 + blk % 2
                for k2 in range(2):
                    tp(PSB(b)[:, k2 * 128:(k2 + 1) * 128], st[:, k2 * 128:(k2 + 1) * 128], ident_f, (sk, "CF"),
                       (("ps", b),))
                cp("act", PT[:, :, blk * 128:(blk + 1) * 128],
                   PSB(b)[:, 0:256].rearrange("p (a b) -> p a b", a=2), (("ps", b),), ("PT",))
            wpp, kpp = wload(w_pp[i].rearrange("(k p) n -> p k n", p=128), (2, D))
            wg_d = w_pg[i].rearrange("(k p) n -> p k n", p=128)
            cnt = 0
            for mp in range(4):
                wg, kg = wload(wg_d[:, :, mp * 256:(mp + 1) * 256], (KC, 256))
                for mi in range(2):
                    m = mp * 2 + mi
                    for t in range(NT):
                        cs = slice(t * TW, (t + 1) * TW)
                        ba = (cnt * 2) % 4
                        bb = (cnt * 2 + 1) % 4
                        cnt += 1
                        for k in range(KC):
                            mm(PSB(ba)[:, 0:TW], wg[:, k, mi * 128:(mi + 1) * 128], H[:, k, cs], k == 0, k == KC - 1,
                               (kg, ("H", t)), (("ps", ba),))
                        for k2 in range(2):
                            mm(PSB(bb)[:, 0:TW], wpp[:, k2, m * 128:(m + 1) * 128], PT[:, k2, cs], k2 == 0, k2 == 1,
                               (kpp, "PT"), (("ps", bb),))
                        sg = SG[cnt % 2]
                        sgk = ("SG", cnt % 2)
                        act(sg[:, 0:TW], PSB(ba)[:, 0:TW], AF.Sigmoid, (("ps", ba),), (sgk,))
                        tt("dve", sg[:, 0:TW], sg[:, 0:TW], PSB(bb)[:, 0:TW], ALU.mult, (sgk, ("ps", bb)), (sgk,))
                        tt("dve", X[:, m, cs], X[:, m, cs], sg[:, 0:TW], ALU.add, (sgk, ("X", t)), (("X", t),))

        def kv(T, pos0, kdst, vdst, KTdst, Vdst_fn, RB=128, dup=False):
            phase_begin()
            TW, NT = tiles_of(T)
            H = W.alloc([KC, T], BF16)
            SQ = W.alloc([KC, TW], BF16)
            RS = W.alloc([TW], F32)
            KVO = [W.alloc([512], F32) for _ in range(2)]
            for t in range(NT):
                rmsnorm_tile(H, V_KV, t, TW, SQ, RS, 7)
            wk, kk = wload(w_k.rearrange("(k p) n -> p k n", p=128), (KC, 256))
            wv, kvk = wload(w_v.rearrange("(k p) n -> p k n", p=128), (KC, 256))
            import os
            kdbg = int(os.environ.get("KV_DBG", "9"))
            cnt = 0
            for h in range(NG if kdbg >= 1 else 0):
                for t in range(NT):
                    cs = slice(t * TW, (t + 1) * TW)
                    b = cnt % 2
                    cnt += 1
                    for k in range(KC):
                        mm(PSB(b)[0:64, 0:TW], wk[:, k, h * 64:(h + 1) * 64], H[:, k, cs], k == 0, k == KC - 1,
                           (kk, ("H", t)), (("ps", b),))
                    cp("act", KTdst[0:64, h, cs], PSB(b)[0:64, 0:TW], (("ps", b),), ("KT",))
            for blk in range(T // RB if kdbg >= 2 else 0):
                b = 2 + blk % 2
                bs = slice(blk * RB, (blk + 1) * RB)
                t = (blk * RB) // TW
                for k in range(KC):
                    mm(PSB(b)[0:RB, 0:256], H[:, k, bs], wk[:, k, :], k == 0, k == KC - 1, (kk, ("H", t)), (("ps", b),))
                for k in range(KC):
                    mm(PSB(b)[0:RB, 256:512], H[:, k, bs], wv[:, k, :], k == 0, k == KC - 1, (kvk, ("H", t)),
                       (("ps", b),))
                kvo = KVO[blk % 2]
                kk2 = ("KVO", blk % 2)
                cp("act", kvo[0:RB, :], PSB(b)[0:RB, :], (("ps", b),), (kk2,))
                cp("dve", Vdst_fn(blk), PSB(b)[0:RB, 256:512], (("ps", b),), ("VV",))
                if kdbg < 3:
                    continue
                kd = kdst[blk * RB:(blk + 1) * RB, :]
                vd = vdst[blk * RB:(blk + 1) * RB, :]
                P.dma("sp", lambda e, kvo=kvo, kd=kd: e.dma_start(out=kd, in_=kvo[0:RB, 0:256]), (kk2,), ())
                P.dma("sp", lambda e, kvo=kvo, vd=vd: e.dma_start(out=vd, in_=kvo[0:RB, 256:512]), (kk2,), ())

        def mamba(T, mode, seq_first, seq_last, bidx):
            phase_begin()
            TW, NT = tiles_of(T)
            prompt = (mode == "p")
            nseg, seglen = (1, TW) if prompt else (NSB, 32)
            Lc = 128 if prompt else 32
            H = W.alloc([KC, TW], BF16)
            RS = W.alloc([TW], F32)
            XBC = W.alloc([24, TW], BF16)
            SQ = XBC[:, 0:KC, :]
            sqkeys = tuple(("XBC", m_) for m_ in range(KC))
            YT = W.alloc([16, TW], BF16)
            up_off = W.off
            UP = [W.alloc([nseg, seglen + 3], F32) for _ in range(3)]
            CVT = [W.alloc([nseg, seglen], F32) for _ in range(2)]
            upcvt_bytes = W.off - up_off
            XS_TM = W.alloc([DIN], BF16)
            XDT2 = [W.alloc([DIN], BF16)] * 2
            XDTW = W.alloc([DIN], BF16)
            B_TM = W.alloc([512], BF16)
            SM2 = [W.alloc([8, 32], F32)] * 2
            CBM2 = [W.alloc([NG, 128], BF16)] * 2
            ATRI = [W.alloc([8, 128], F32) for _ in range(2)]
            ESEG = [W.alloc([8, 128], BF16) for _ in range(2)]
            MT = [W.alloc([8, 128], BF16) for _ in range(2)]
            YOFF0 = W.alloc([DIN], F32)
            YB = W.alloc([DIN], BF16)
            if prompt and upcvt_bytes >= DIN * 4:
                YOFF1 = W.ap[:, up_off // 2:up_off // 2 + DIN * 2].bitcast(F32)
                yk1 = tuple(("UP", i_) for i_ in range(3)) + tuple(("UPT", i_) for i_ in range(3)) + \
                    tuple(("CVT", i_) for i_ in range(2))
            else:
                YOFF1 = W.alloc([DIN], F32)
                yk1 = ("YOFF1",)
            YOFF2 = [YOFF0, YOFF1]
            YK2 = [("YOFF",), yk1]
            SIO = YOFF0.rearrange("p (a n) -> p a n", a=16)

            win_d = w_in.rearrange("(k p) n -> p k n", p=128)
            wout_d = w_out.rearrange("(k p) n -> p k n", p=128)

            def seg_view(ap3, L):
                return ap3[:, :, 0:L]

            if prompt and seq_first:
                memset("dve", STATE[:, :], 0.0, ("STATE",))
                memset("dve", STATEB[:, :], 0.0, ("STATEB",))
                memset("dve", CTAIL[:, :, :, :], 0.0, tuple(("CTAIL", m_) for m_ in range(24)))
            if not prompt:
                for s in range(NSB):
                    for tt_ in range(3):
                        src = sconv[s, tt_, :].rearrange("(m p) -> p m", p=128)
                        P.dma("sp", lambda e, src=src, s=s, tt_=tt_: e.dma_start(
                            out=CTAIL[:, :, s, tt_], in_=src, allow_slow_non_contiguous=True), (), tuple(("CTAIL", m_) for m_ in range(24)))

            for t in range(NT):
                cs = slice(t * TW, (t + 1) * TW)
                rmsnorm_tile_local(H, V_MIX + 0, t, TW, SQ, RS, sqkeys)
                wps = {}

                def stA(m):
                    if m % 2 == 0:
                        pc = m // 2
                        wps[pc] = wload(win_d[:, :, DIN + pc * 256:DIN + (pc + 1) * 256], (KC, 256))
                    wp, kp = wps[m // 2]
                    mi = m % 2
                    b = m % 2
                    up = UP[m % 3]
                    upk = ("UP", m % 3)
                    for k in range(KC):
                        mm(PSB(b)[:, 0:TW], wp[:, k, mi * 128:(mi + 1) * 128], H[:, k, 0:TW], k == 0, k == KC - 1,
                           (kp, "Hm"), (("ps", b),))
                    cp("dve", up[:, :, 0:3], CTAIL[:, m, 0:nseg, :], (("CTAIL", m),), (("UPT", m % 3),))
                    cp("act", up[:, :, 3:3 + seglen],
                       PSB(b)[:, 0:TW].rearrange("p (s l) -> p s l", s=nseg), (("ps", b),), (upk,))
                    cp("act", CTAIL[:, m, 0:nseg, :], up[:, :, seglen:seglen + 3], (upk, ("UPT", m % 3)), (("CTAIL", m),))

                def stB1(m):
                    up = UP[m % 3]
                    upk = ("UP", m % 3)
                    cvt = CVT[m % 2]
                    cvk = ("CVT", m % 2)
                    act(cvt[:, :, :], up[:, :, 0:seglen], AF.Identity, (upk, ("UPT", m % 3), "VEC"), (cvk,),
                        bias=gcol(V_CB, m), scale=gcol(V_CW + 0 * 24, m))
                    for kk_ in range(1, 4):
                        stt(cvt[:, :, :], up[:, :, kk_:kk_ + seglen], gcol(V_CW + kk_ * 24, m), cvt[:, :, :],
                            ALU.mult, ALU.add, (upk, ("UPT", m % 3), cvk, "VEC"), (cvk,))

                def stB2(m):
                    cvt = CVT[m % 2]
                    cvk = ("CVT", m % 2)
                    act(XBC[:, m, 0:TW].rearrange("p (s l) -> p s l", s=nseg), cvt[:, :, :], AF.Silu,
                        (cvk,), (("XBC", m),))

                stA(0)
                for m in range(24):
                    if m + 1 < 24:
                        stA(m + 1)
                    stB1(m)
                    if m >= 1:
                        stB2(m - 1)
                stB2(23)
                wdt, kdt = wload(win_d[:, :, DIN + CONVD:INDIM], (KC, 32))
                zpre = [wload(win_d[:, :, pc_ * 256:(pc_ + 1) * 256], (KC, 256)) for pc_ in range(4)]
                import os
                nchunk = TW // Lc
                if os.environ.get("MAMBA_NOCHUNK"):
                    nchunk = 0
                L = Lc

                def stage1(c):
                    c0 = c * Lc
                    cc = slice(c0, c0 + Lc)
                    SM = SM2[c % 2]
                    smk = lambda i_: ("SM", 0, i_)
                    XDT = XDT2[c % 2]
                    xdk = ("XDT", 0)
                    CBM = CBM2[c % 2]
                    cbk = ("CBM", 0)
                    YOFF = YOFF2[c % 2]
                    yk = YK2[c % 2]
                    if not prompt:
                        for two in range(2):
                            src = sssm[c].rearrange("(hp two) p n -> two p hp n", two=2)[two]
                            P.dma("sp", lambda e, src=src, two=two: e.dma_start(
                                out=SIO[two * 64:(two + 1) * 64, :, :], in_=src), (), ("YOFF",))
                        for hp in range(16):
                            b = 4 + (hp // 4) % 2
                            tp(PSB(b)[:, (hp % 4) * 128:(hp % 4 + 1) * 128], SIO[:, hp, :], ident_f, ("YOFF", "CF"),
                               (("ps", b),))
                            if hp % 4 == 3:
                                q4 = hp // 4
                                cp("dve", STATE[:, q4 * 512:(q4 + 1) * 512], PSB(b), (("ps", b),), ("STATE",))
                                cp("act", STATEB[:, q4 * 512:(q4 + 1) * 512], PSB(b), (("ps", b),), ("STATEB",))
                    for k in range(KC):
                        mm(PSB(6)[0:L, 0:32], H[:, k, cc], wdt[:, k, :], k == 0, k == KC - 1, (kdt, "Hm"), (("ps", 6),))
                    tt("dve", SM[0:L, 0, :], PSB(6)[0:L, 0:32], VEC[0:L, V_DTB:V_DTB + 32], ALU.add,
                       (("ps", 6), "VEC"), (smk(0),))
                    act(SM[0:L, 1, :], SM[0:L, 0, :], AF.Exp, (smk(0),), (smk(1),))
                    act(SM[0:L, 2, :], SM[0:L, 1, :], AF.Ln, (smk(1),), (smk(2),), bias=ONEB[0:L, 0:1])
                    tt("dve", SM[0:L, 3, :], SM[0:L, 2, :], ABC[0:L, :], ALU.mult, (smk(2), "ABC"), (smk(3),))
                    mm(PSB(6)[0:L, 32:64], tri_f[0:L, 0:L], SM[0:L, 3, :], True, True, (smk(3), "CF"), (("ps", 6),))
                    mm(PSB(6)[:, 64:96], ones_f[0:L, :], SM[0:L, 3, :], True, True, (smk(3), "CF"), (("ps", 6),))
                    cp("dve", SM[0:L, 4, :], PSB(6)[0:L, 32:64], (("ps", 6),), (smk(4),))
                    act(SM[0:L, 5, :], PSB(6)[0:L, 32:64], AF.Exp, (("ps", 6),), (smk(5),))
                    tt("dve", SM[0:L, 6, :], PSB(6)[0:L, 64:96], SM[0:L, 4, :], ALU.subtract, (("ps", 6), smk(4)),
                       (smk(6),))
                    act(SM[0:L, 6, :], SM[0:L, 6, :], AF.Exp, (smk(6),), (smk(6),))
                    act(SM[:, 7, :], PSB(6)[:, 64:96], AF.Exp, (("ps", 6),), (smk(7),))
                    yield
                    for m in range(16):
                        b = 4 + m // 8
                        tp(PSBH(b)[0:L, (m % 8) * 128:(m % 8 + 1) * 128], XBC[:, m, cc], ident_b,
                           (("XBC", m), "CB"), (("ps", b),))
                    for hb in range(2):
                        b = 4 + hb
                        hs = slice(hb * 1024, (hb + 1) * 1024)
                        tt("dve", XDT[0:L, hs].rearrange("p (h d) -> p h d", d=HD),
                           PSBH(b)[0:L, :].rearrange("p (h d) -> p h d", d=HD),
                           SM[0:L, 2, hb * 16:(hb + 1) * 16].unsqueeze(2).to_broadcast([L, 16, HD]), ALU.mult,
                           (("ps", b), smk(2)), (xdk,))
                    tt("dve", XDTW[0:L, :].rearrange("p (h d) -> p h d", d=HD),
                       XDT[0:L, :].rearrange("p (h d) -> p h d", d=HD),
                       SM[0:L, 6, :].unsqueeze(2).to_broadcast([L, NH, HD]), ALU.mult, (xdk, smk(6)), ("XDTW",))
                    yield
                    for g in range(NG):
                        tp(PSBH(7)[0:L, g * 128:(g + 1) * 128], XBC[:, 16 + g, cc], ident_b,
                           (("XBC", 16 + g), "CB"), (("ps", 7),))
                    cp("act", B_TM[0:L, :], PSBH(7)[0:L, 0:512], (("ps", 7),), ("B_TM",))
                    for g in range(NG):
                        mm(PSB(7)[0:L, g * L:(g + 1) * L], XBC[:, 16 + g, cc], XBC[:, 20 + g, cc], True, True,
                           (("XBC", 16 + g), ("XBC", 20 + g)), (("ps", 7),))
                    tt("dve", CBM[0:L, :, 0:L], PSB(7)[0:L, 0:NG * L].rearrange("p (g i) -> p g i", g=NG),
                       cbmask_b[0:L, 0:L].unsqueeze(1).to_broadcast([L, NG, L]), ALU.mult, (("ps", 7), "CB"), (cbk,))
                    for g in range(NG):
                        mm(PSB(g)[0:L, :], XBC[:, 20 + g, cc], STATEB[:, g * 512:(g + 1) * 512], True, True,
                           (("XBC", 20 + g), "STATEB"), (("ps", g),))
                        tt("dve", YOFF[0:L, g * 512:(g + 1) * 512].rearrange("p (h d) -> p h d", d=HD),
                           PSB(g)[0:L, :].rearrange("p (h d) -> p h d", d=HD),
                           SM[0:L, 5, g * 8:(g + 1) * 8].unsqueeze(2).to_broadcast([L, 8, HD]), ALU.mult,
                           (("ps", g), smk(5)), yk)
                    yield
                    for g in range(NG):
                        mm(PSB(g)[:, :], B_TM[0:L, g * 128:(g + 1) * 128], XDTW[0:L, g * 512:(g + 1) * 512], True, True,
                           ("B_TM", "XDTW"), (("ps", g),))
                    tt("dve", STATE[:, :].rearrange("p (h d) -> p h d", d=HD),
                       STATE[:, :].rearrange("p (h d) -> p h d", d=HD),
                       SM[:, 7, :].unsqueeze(2).to_broadcast([128, NH, HD]), ALU.mult, ("STATE", smk(7)), ("STATE",))
                    for g in range(NG):
                        tt("dve", STATE[:, g * 512:(g + 1) * 512], STATE[:, g * 512:(g + 1) * 512], PSB(g)[:, :], ALU.add,
                           ("STATE", ("ps", g)), ("STATE",))
                    cp("act", STATEB[:, :], STATE[:, :], ("STATE",), ("STATEB",))
                    yield

                def stage2(c):
                    c0 = c * Lc
                    cc = slice(c0, c0 + Lc)
                    SM = SM2[c % 2]
                    smk = lambda i_: ("SM", 0, i_)
                    XDT = XDT2[c % 2]
                    xdk = ("XDT", 0)
                    CBM = CBM2[c % 2]
                    cbk = ("CBM", 0)
                    YOFF = YOFF2[c % 2]
                    yk = YK2[c % 2]
                    def hgA(hg):
                        at = ATRI[hg % 2]
                        atk = ("ATRI", hg % 2)
                        eg = ESEG[hg % 2]
                        egk = ("ESEG", hg % 2)
                        tt("dve", at[0:L, :, 0:L], SM[0:L, 3, hg * 8:(hg + 1) * 8].unsqueeze(2).to_broadcast([L, 8, L]),
                           tri_f[0:L, 0:L].unsqueeze(1).to_broadcast([L, 8, L]), ALU.mult, (smk(3), "CF"), (atk,))
                        hper = min(max(1, 512 // L), 8)
                        for q0 in range(0, 8, hper):
                            b = 4 + (q0 // hper) % 2
                            mm(PSB(b)[0:L, 0:hper * L].rearrange("p (h i) -> p h i", h=hper), U_f[0:L, 0:L],
                               at[0:L, q0:q0 + hper, 0:L], True, True, (atk, "CF"), (("ps", b),))
                            act(eg[0:L, q0:q0 + hper, 0:L], PSB(b)[0:L, 0:hper * L].rearrange("p (h i) -> p h i", h=hper),
                                AF.Exp, (("ps", b),), (egk,))

                    def hgB(hg):
                        eg = ESEG[hg % 2]
                        egk = ("ESEG", hg % 2)
                        mt = MT[hg % 2]
                        mtk = ("MT", hg % 2)
                        tt("dve", mt[0:L, :, 0:L], eg[0:L, :, 0:L],
                           CBM[0:L, hg, 0:L].unsqueeze(1).to_broadcast([L, 8, L]), ALU.mult, (egk, cbk), (mtk,))
                        for h8 in range(8):
                            h = hg * 8 + h8
                            mm(PSB(hg)[0:L, h8 * 64:(h8 + 1) * 64], mt[0:L, h8, 0:L], XDT[0:L, h * 64:(h + 1) * 64],
                               True, True, (mtk, xdk), (("ps", hg),))
                        tt("dve", YB[0:L, hg * 512:(hg + 1) * 512], PSB(hg)[0:L, :], YOFF[0:L, hg * 512:(hg + 1) * 512],
                           ALU.add, (("ps", hg),) + yk, ("YB",))

                    hgA(0)
                    for hg in range(NG):
                        if hg + 1 < NG:
                            hgA(hg + 1)
                        hgB(hg)
                    yield
                    for m in range(16):
                        b = 4 + m // 8
                        tp(PSBH(b)[:, (m % 8) * Lc:(m % 8) * Lc + L], YB[0:L, m * 128:(m + 1) * 128], ident_b[0:L, 0:L],
                           ("YB", "CB"), (("ps", b),))
                    for hb in range(2):
                        b = 4 + hb
                        cp("act", YT[:, hb * 8:(hb + 1) * 8, cc],
                           PSBH(b)[:, 0:8 * Lc].rearrange("p (m l) -> p m l", m=8), (("ps", b),), ("YT",))
                    yield

                def drain(g):
                    for _ in g:
                        pass

                if False:
                    if nchunk:
                        drain(stage1(0))
                    for c in range(nchunk):
                        g2 = stage2(c)
                        g1 = stage1(c + 1) if c + 1 < nchunk else iter(())
                        d1 = d2 = False
                        while not (d1 and d2):
                            if not d1:
                                try:
                                    next(g1)
                                except StopIteration:
                                    d1 = True
                            if not d2:
                                try:
                                    next(g2)
                                except StopIteration:
                                    d2 = True
                else:
                    for c in range(nchunk):
                        drain(stage1(c))
                        drain(stage2(c))
                        if not prompt:
                            store_state(ssm_s[c], SIO)
                cnt = 0
                for pc in range(8):
                    wp, kp = zpre[pc] if pc < 4 else wload(win_d[:, :, pc * 256:(pc + 1) * 256], (KC, 256))
                    for mi in range(2):
                        m = pc * 2 + mi
                        b = cnt % 2
                        cvt = CVT[cnt % 2]
                        cvk = ("CVT", cnt % 2)
                        cnt += 1
                        for k in range(KC):
                            mm(PSB(b)[:, 0:TW], wp[:, k, mi * 128:(mi + 1) * 128], H[:, k, 0:TW], k == 0, k == KC - 1,
                               (kp, "Hm"), (("ps", b),))
                        zf = cvt.rearrange("p s l -> p (s l)")
                        act(zf[:, 0:TW], PSB(b)[:, 0:TW], AF.Silu, (("ps", b),), (cvk,))
                        stt(YT[:, m, 0:TW], XBC[:, m, 0:TW], gcol(V_DFM, m), YT[:, m, 0:TW], ALU.mult, ALU.add,
                            (("XBC", m), "YT", "VEC"), ("YT",))
                        tt("dve", YT[:, m, 0:TW], YT[:, m, 0:TW], zf[:, 0:TW], ALU.mult, ("YT", cvk), ("YT",))
                SQ2 = XBC
                act(SQ2[:, 0:16, 0:TW], YT[:, :, 0:TW], AF.Square, ("YT",), tuple(("XBC", m) for m in range(16)))
                for g in range(NG):
                    for c4 in range(4):
                        mm(PSB(g)[:, 0:TW], ones_b, SQ2[:, g * 4 + c4, 0:TW], c4 == 0, c4 == 3,
                           (("XBC", g * 4 + c4), "CB"), (("ps", g),))
                    rsg = CVT[g % 2].rearrange("p s l -> p (s l)")
                    rk = ("CVT", g % 2)
                    act(rsg[:, 0:TW], PSB(g)[:, 0:TW], AF.Sqrt, (("ps", g),), (rk,), bias=EPSB[:, 0:1], scale=1.0 / 512)
                    P.op("dve", lambda e, rsg=rsg: e.reciprocal(rsg[:, 0:TW], rsg[:, 0:TW]), (rk,), (rk,))
                    for c4 in range(4):
                        m = g * 4 + c4
                        stt(YT[:, m, 0:TW], YT[:, m, 0:TW], gcol(V_SSMN, m), rsg[:, 0:TW], ALU.mult, ALU.mult,
                            ("YT", rk, "VEC"), ("YT",))
                cnt = 0
                for m in range(KC):
                    wo_, ko = wload(wout_d[:, :, m * 128:(m + 1) * 128], (16, 128))
                    b = 6 + cnt % 2
                    cnt += 1
                    for k in range(16):
                        mm(PSB(b)[:, 0:TW], wo_[:, k, :], YT[:, k, 0:TW], k == 0, k == 15, (ko, "YT"), (("ps", b),))
                    tt("dve", X[:, m, cs], X[:, m, cs], PSB(b)[:, 0:TW], ALU.add, (("ps", b), ("X", t)), (("X", t),))
            if prompt and seq_last:
                store_state(ssm_p[bidx], SIO)
                for tt_ in range(3):
                    dst = conv_p[bidx, tt_, :].rearrange("(m p) -> p m", p=128)
                    P.dma("sp", lambda e, dst=dst, tt_=tt_: e.dma_start(
                        out=dst, in_=CTAIL[:, :, 0, tt_], allow_slow_non_contiguous=True), tuple(("CTAIL", m_) for m_ in range(24)), ())
            if not prompt:
                for s in range(NSB):
                    for tt_ in range(3):
                        dst = conv_s[s, tt_, :].rearrange("(m p) -> p m", p=128)
                        P.dma("sp", lambda e, dst=dst, s=s, tt_=tt_: e.dma_start(
                            out=dst, in_=CTAIL[:, :, s, tt_], allow_slow_non_contiguous=True), tuple(("CTAIL", m_) for m_ in range(24)), ())

        def store_state(dst3, SIO):
            for hp in range(16):
                b = 4 + (hp // 4) % 2
                tp(PSB(b)[:, (hp % 4) * 128:(hp % 4 + 1) * 128], STATE[:, hp * 128:(hp + 1) * 128], ident_f,
                   ("STATE", "CF"), (("ps", b),))
                if hp % 4 == 3:
                    q4 = hp // 4
                    cp("dve", SIO[:, q4 * 4:(q4 + 1) * 4, :], PSB(b).rearrange("p (a n) -> p a n", a=4), (("ps", b),),
                       ("YOFF",))
            for two in range(2):
                dst = dst3.rearrange("(hp two) p n -> two p hp n", two=2)[two]
                P.dma("sp", lambda e, dst=dst, two=two: e.dma_start(out=dst, in_=SIO[two * 64:(two + 1) * 64, :, :]),
                      ("YOFF",), ())

        ONEB = sb("ONEB", [128, 1], F32)
        memset("dve", ONEB[:, :], 1.0, ("ONEB",))

        def rmsnorm_tile_local(H, vbase, t, TW, SQ, RS, sqkeys=("SQ",)):
            cs = slice(t * TW, (t + 1) * TW)
            act(SQ[:, :, 0:TW], X[:, :, cs], AF.Square, (("X", t),), sqkeys)
            for k in range(KC):
                mm(PSB(7)[:, 0:TW], ones_b, SQ[:, k, 0:TW], k == 0, k == KC - 1, sqkeys + ("CB",), (("ps", 7),))
            act(RS[:, 0:TW], PSB(7)[:, 0:TW], AF.Sqrt, (("ps", 7),), ("RS",), bias=EPSB[:, 0:1], scale=1.0 / D)
            P.op("dve", lambda e: e.reciprocal(RS[:, 0:TW], RS[:, 0:TW]), ("RS",), ("RS",))
            for k in range(KC):
                stt(H[:, k, 0:TW], X[:, k, cs], gcol(vbase, k), RS[:, 0:TW], ALU.mult, ALU.mult,
                    (("X", t), "RS", "VEC"), ("Hm",))

        def attn(T, mode, pos0, kt_new=None, v_new=None):
            phase_begin()
            TW, NT = tiles_of(T)
            prompt = (mode == "p")
            H = W.alloc([KC, TW], BF16)
            SQ = W.alloc([KC, TW], BF16)
            RS = W.alloc([TW], F32)
            QT = W.alloc([16, TW], BF16)
            OTS = W.alloc([8, TW], BF16)
            NEE, NSP, NWW, NAA, NVB = 8, 8, 4, 8, 8
            EE = [W.alloc([512], F32) for _ in range(NEE)]
            SP = [W.alloc([512], BF16) for _ in range(NSP)]
            WW = [W.alloc([512], F32) for _ in range(NWW)]
            AA = [W.alloc([512], BF16) for _ in range(NAA)]
            if not prompt:
                KSTG = [W.alloc([256], F32) for _ in range(2)]
                VSTG = [W.alloc([256], F32) for _ in range(2)]
                KTB = [W.alloc([NG, 128], BF16, parts=64) for _ in range(2)]
                VB = [W.alloc([256], BF16) for _ in range(NVB)]
            wq_d = w_q.rearrange("(k p) n -> p k n", p=128)
            for t in range(NT):
                cs = slice(t * TW, (t + 1) * TW)
                rmsnorm_tile_local(H, V_MIX + 8, t, TW, SQ, RS)
                cnt = 0
                for pc in range(4):
                    wp, kp = wload(wq_d[:, :, pc * 256:(pc + 1) * 256], (KC, 256))
                    for hi in range(4):
                        h = pc * 4 + hi
                        b = cnt % 2
                        cnt += 1
                        for k in range(KC):
                            mm(PSB(b)[0:64, 0:TW], wp[:, k, hi * 64:(hi + 1) * 64], H[:, k, 0:TW], k == 0, k == KC - 1,
                               (kp, "Hm"), (("ps", b),))
                        P.op("act", lambda e, h=h, b=b: e.mul(QT[0:64, h, 0:TW], PSB(b)[0:64, 0:TW], 0.125),
                             (("ps", b),), ("QT",))
                units = []
                if prompt:
                    for qb in range(TW // 128):
                        gq = (pos0 + t * TW) // 128 + qb
                        for kb in range(gq, -1, -1):
                            for h in range(NG):
                                units.append({"chain": (qb, h), "cidx": h, "first": kb == gq, "last": kb == 0,
                                              "diag": kb == gq, "h": h, "qb": qb, "kb": kb, "nk": 128})
                else:
                    nkb = PAST // 128
                    for s0 in range(0, NSB, 2):
                        for kb in range(nkb, -1, -1):
                            for s in range(s0, min(s0 + 2, NSB)):
                                units.append({"chain": (s,), "cidx": s - s0, "first": kb == nkb, "last": kb == 0,
                                              "diag": kb == nkb, "s": s, "kb": kb, "nk": 32 if kb == nkb else 128})
                NU = len(units)
                lastseen = {}
                for i, u in enumerate(units):
                    u["prev"] = lastseen.get(u["chain"])
                    lastseen[u["chain"]] = i

                def zbank(i):
                    return i % 2

                def pbank(u):
                    return 2 + u["cidx"]

                def emit_load(i):
                    u = units[i]
                    if prompt or u["diag"]:
                        return
                    s, kb = u["s"], u["kb"]
                    ks, vs = KSTG[i % 2], VSTG[i % 2]
                    kk_, vk_ = ("KSTG", i % 2), ("VSTG", i % 2)
                    srck = ck[s, kb * 128:(kb + 1) * 128, :]
                    srcv = cv[s, kb * 128:(kb + 1) * 128, :]
                    P.dma("sp", lambda e, ks=ks, srck=srck: e.dma_start(out=ks, in_=srck), (), (kk_,))
                    P.dma("sp", lambda e, vs=vs, srcv=srcv: e.dma_start(out=vs, in_=srcv), (), (vk_,))
                    for h in range(NG):
                        tp(PSB(zbank(i))[0:64, h * 128:(h + 1) * 128], ks[:, h * 64:(h + 1) * 64], ident_f, (kk_, "CF"),
                           (("ps", zbank(i)),))
                    cp("dve", KTB[i % 2][:, :, :], PSB(zbank(i))[0:64, :].rearrange("p (h k) -> p h k", h=NG),
                       (("ps", zbank(i)),), (("KTB", i % 2),))
                    cp("act", VB[i % NVB][:, :], vs, (vk_,), (("VB", i % NVB),))

                def kview(u, i, h):
                    if prompt:
                        lo = 0
                        return KT[lo:lo + 64, h, u["kb"] * 128:(u["kb"] + 1) * 128], ("KT",)
                    if u["diag"]:
                        return kt_new[0:64, h, u["s"] * 32:(u["s"] + 1) * 32], ("KT",)
                    return KTB[i % 2][:, h, :], (("KTB", i % 2),)

                def vview(u, i, h):
                    if prompt:
                        return VV[:, u["kb"], h * 64:(h + 1) * 64], ("VV",)
                    if u["diag"]:
                        return v_new[:, u["s"], h * 64:(h + 1) * 64], ("VV",)
                    return VB[i % NVB][:, h * 64:(h + 1) * 64], (("VB", i % NVB),)

                def maskmul(buf, key, nk):
                    if prompt:
                        msk = cmask_b[0:nk, 0:128].unsqueeze(1).to_broadcast([nk, 4, 128])
                        v3 = buf[0:nk, :].rearrange("p (g q) -> p g q", g=4)
                    else:
                        msk = cmask_b[0:nk, 0:32].unsqueeze(1).to_broadcast([nk, 16, 32])
                        v3 = buf[0:nk, :].rearrange("p (g q) -> p g q", g=16)
                    tt("dve", v3, v3, msk, ALU.mult, (key, "CB"), (key,))

                def emit_Zmm(i):
                    u = units[i]
                    nk = u["nk"]
                    b = zbank(i)
                    emit_load(i)
                    if prompt:
                        h, qb = u["h"], u["qb"]
                        lhs, lk = kview(u, i, h)
                        lo = 0
                        mm(PSB(b)[0:nk, :].rearrange("p (g q) -> p g q", g=4), lhs,
                           QT[lo:lo + 64, h * 4:(h + 1) * 4, qb * 128:(qb + 1) * 128], True, True, lk + ("QT",),
                           (("ps", b),))
                    else:
                        s = u["s"]
                        for h in range(NG):
                            lhs, lk = kview(u, i, h)
                            mm(PSB(b)[0:nk, h * 128:(h + 1) * 128].rearrange("p (g q) -> p g q", g=4), lhs,
                               QT[0:64, h * 4:(h + 1) * 4, s * 32:(s + 1) * 32], True, True, lk + ("QT",), (("ps", b),))

                def emit_Zact(i):
                    u = units[i]
                    nk = u["nk"]
                    b = zbank(i)
                    ee, sp_ = EE[i % NEE], SP[i % NSP]
                    act(ee[0:nk, :], PSB(b)[0:nk, :], AF.Exp, (("ps", b),), (("EE", i % NEE),))
                    act(sp_[0:nk, :], ee[0:nk, :], AF.Ln, (("EE", i % NEE),), (("SP", i % NSP),), bias=ONEB[0:nk, 0:1])
                    if u["diag"]:
                        maskmul(sp_, ("SP", i % NSP), nk)

                def emit_Cmm(i):
                    u = units[i]
                    nk = u["nk"]
                    pb = pbank(u)
                    sp_ = SP[i % NSP]
                    if u["first"]:
                        mm(PSB(pb)[0:128, :], trige_b[0:nk, 0:128], sp_[0:nk, :], True, True, (("SP", i % NSP), "CB"),
                           (("ps", pb),))
                    else:
                        pi = u["prev"]
                        pnk = units[pi]["nk"]
                        spp = SP[pi % NSP]
                        mm(PSB(pb)[0:128, :], compl_b[0:pnk, 0:128], spp[0:pnk, :], False, False,
                           (("SP", pi % NSP), "CB"), (("ps", pb),), skip=True)
                        mm(PSB(pb)[0:128, :], trige_b[0:nk, 0:128], sp_[0:nk, :], False, True, (("SP", i % NSP), "CB"),
                           (("ps", pb),), skip=True)

                def emit_Cact(i):
                    u = units[i]
                    nk = u["nk"]
                    pb = pbank(u)
                    ww = WW[i % NWW]
                    act(ww[0:nk, :], PSB(pb)[0:nk, :], AF.Exp, (("ps", pb),), (("WW", i % NWW),), scale=-1.0)
                    aa = AA[i % NAA]
                    tt("dve", aa[0:nk, :], EE[i % NEE][0:nk, :], ww[0:nk, :], ALU.mult,
                       (("EE", i % NEE), ("WW", i % NWW)), (("AA", i % NAA),))
                    if u["diag"]:
                        maskmul(aa, ("AA", i % NAA), nk)

                def emit_O(i):
                    u = units[i]
                    nk = u["nk"]
                    aa = AA[i % NAA]
                    if prompt:
                        h = u["h"]
                        ob, ph = 6 + h // 2, h % 2
                        rv, rk = vview(u, i, h)
                        mm(PSB(ob)[ph * 64:(ph + 1) * 64, :], rv, aa[0:nk, :], u["first"], u["last"],
                           rk + (("AA", i % NAA),), (("ps", ob),))
                        if u["last"]:
                            qb = u["qb"]
                            j0 = (h // 2) * 4
                            cp("act", OTS[ph * 64:(ph + 1) * 64, j0:j0 + 4, qb * 128:(qb + 1) * 128],
                               PSB(ob)[ph * 64:(ph + 1) * 64, :].rearrange("p (g q) -> p g q", g=4), (("ps", ob),),
                               ("OTS",))
                    else:
                        s = u["s"]
                        ob = 6 + u["cidx"]
                        for h in range(NG):
                            ph = h % 2
                            rv, rk = vview(u, i, h)
                            c0 = (h // 2) * 128
                            mm(PSB(ob)[ph * 64:(ph + 1) * 64, c0:c0 + 128], rv, aa[0:nk, h * 128:(h + 1) * 128],
                               u["first"] and h // 2 == 0, u["last"], rk + (("AA", i % NAA),), (("ps", ob),), skip=True)
                        if u["last"]:
                            cp("act", OTS[:, :, s * 32:(s + 1) * 32],
                               PSB(ob)[:, 0:256].rearrange("p (j q) -> p j q", j=8), (("ps", ob),), ("OTS",))

                NS = (NU + 1) // 2
                for st in range(NS + 3):
                    for i_ in (2 * (st - 3), 2 * (st - 3) + 1):
                        if 0 <= i_ < NU:
                            emit_O(i_)
                    for i_ in (2 * st, 2 * st + 1):
                        if i_ < NU:
                            emit_Zmm(i_)
                    for i_ in (2 * st, 2 * st + 1):
                        if i_ < NU:
                            emit_Zact(i_)
                    for i_ in (2 * (st - 1), 2 * (st - 1) + 1):
                        if 0 <= i_ < NU:
                            emit_Cmm(i_)
                    for i_ in (2 * (st - 1), 2 * (st - 1) + 1):
                        if 0 <= i_ < NU:
                            emit_Cact(i_)
                wo_d = w_o.rearrange("(a two g p) n -> two p a g n", two=2, g=4, p=64)
                cnt = 0
                for m in range(KC):
                    sl = ring_state["n"] % nslot
                    ring_state["n"] += 1
                    ko = ("ring", sl)
                    for two in range(2):
                        for a_ in range(2):
                            dst = RING[two * 64:(two + 1) * 64, sl, a_ * 512:(a_ + 1) * 512].rearrange(
                                "p (g n) -> p g n", g=4)
                            src = wo_d[two][:, a_, :, m * 128:(m + 1) * 128]
                            P.dma("pool", lambda e, dst=dst, src=src: e.dma_start(out=dst, in_=src), (), (ko,),
                                  nobarrier=True)
                    wo_ = RING[:, sl, 0:1024].rearrange("p (j n) -> p j n", j=8)
                    b = cnt % 2
                    cnt += 1
                    for j in range(8):
                        mm(PSB(b)[:, 0:TW], wo_[:, j, :], OTS[:, j, 0:TW], j == 0, j == 7, (ko, "OTS"), (("ps", b),))
                    tt("dve", X[:, m, cs], X[:, m, cs], PSB(b)[:, 0:TW], ALU.add, (("ps", b), ("X", t)), (("X", t),))

        KTN = KT[:, :, 0:128]
        VN = VV[0:32, 0:4, :]

        def run_pass(mode, bidx, half):
            prompt = mode == "p"
            T = PASS_T if prompt else TS
            pos0 = half * PASS_T if prompt else PAST
            if prompt:
                xrows = xp[bidx, pos0:pos0 + T, :]
                yrows = y_p[bidx, pos0:pos0 + T, :]
                prow = [pp[i, bidx, pos0:pos0 + T, :] for i in range(2)]
            else:
                xrows, yrows = xs, y_s
                prow = [psm[i] for i in range(2)]
            load_x(xrows, T)
            if "ffn" in phases:
                ffn(0, 0, T)
            if "mamba" in phases:
                mamba(T, mode, half == 0, pos0 + T == S, bidx)
            if "ffn" in phases:
                ffn(0, 1, T)
            if "ple" in phases:
                ple(0, prow[0], T)
            if "kv" in phases:
                if prompt:
                    kv(T, pos0, k_p[bidx, pos0:pos0 + T, :], v_p[bidx, pos0:pos0 + T, :], KT[:, :, pos0:pos0 + T],
                       lambda blk: VV[:, pos0 // 128 + blk, :], dup=True)
                else:
                    kv(T, pos0, k_s, v_s, KTN[:, :, 0:T], lambda blk: VN[:, blk, :], RB=32)
            if "ffn" in phases:
                ffn(1, 0, T)
            if "attn" in phases:
                if prompt:
                    attn(T, mode, pos0)
                else:
                    attn(T, mode, pos0, KTN, VN)
            if "ffn" in phases:
                ffn(1, 1, T)
            if "ple" in phases:
                ple(1, prow[1], T)
            store_y(yrows, T)

        for b in range(NPB):
            for half in range(S // PASS_T):
                run_pass("p", b, half)
        if NSB > 0:
            run_pass("s", 0, 0)

        P.barrier()
        P.emit(block, sems, ring_sems)
    return nc


_W_NAMES = {
    "w_gate": "ffn_w_gate", "w_up": "ffn_w_up", "w_down": "ffn_w_down",
}


def make_in_map(inp, pb, sb_, cf, cb, vec):
    f = lambda a: np.ascontiguousarray(a, dtype=np.float32)
    xs = f(inp["x_sample"][sb_])
    ps_ = f(inp["p_sample"][:, sb_])
    m = {
        "xp": f(inp["x_prompt"][pb]),
        "xs": xs.reshape(-1, D),
        "pp": f(inp["p_prompt"][:, pb]),
        "psm": ps_.reshape(2, -1, PLE),
        "sssm": f(inp["state_ssm"][0, sb_]),
        "sconv": f(inp["state_conv"][0, sb_]),
        "ck": f(inp["cache_k"][sb_]).reshape(xs.shape[0], -1, 256),
        "cv": f(inp["cache_v"][sb_]).reshape(xs.shape[0], -1, 256),
        "w_gate": f(inp["ffn_w_gate"]), "w_up": f(inp["ffn_w_up"]), "w_down": f(inp["ffn_w_down"]),
        "w_in": f(inp["ssm_w_in"][0]), "w_out": f(inp["ssm_w_out"][0]),
        "w_k": f(inp["w_k"]), "w_v": f(inp["w_v"]), "w_q": f(inp["sb_w_q"][0]), "w_o": f(inp["sb_w_o"][0]),
        "w_pg": f(inp["ple_w_gate"]), "w_pp": f(inp["ple_w_proj"]),
        "cf": cf, "cb": cb, "vec": vec,
    }
    return m


def run(inp, n_cores, phases=("ffn", "mamba", "ple", "kv", "attn"), trace=False):
    inp = {k: np.asarray(v) for k, v in inp.items()}
    B, S = inp["x_prompt"].shape[0], inp["x_prompt"].shape[1]
    BS = inp["x_sample"].shape[0]
    PAST = inp["cache_k"].shape[1]
    NPB, NSB = B // n_cores, BS // n_cores
    nc = build_program(NPB, S, NSB, PAST, phases=phases)
    cf, cb = _const_tables()
    vec = _vec_table(inp)
    in_maps = []
    for c in range(n_cores):
        in_maps.append(make_in_map(inp, slice(c * NPB, (c + 1) * NPB), slice(c * NSB, (c + 1) * NSB), cf, cb, vec))
    res = run_bass_kernel_spmd(nc, in_maps, core_ids=list(range(n_cores)), trace=trace)
    R = res.results
    cat = lambda k: np.concatenate([r[k] for r in R], axis=0)
    y_p = cat("y_p")
    y_s = cat("y_s").reshape(BS, 32, D)
    ssm_p = cat("ssm_p")[None]
    conv_p = cat("conv_p")[None]
    k_p = cat("k_p").reshape(B, S, 4, 64)
    v_p = cat("v_p").reshape(B, S, 4, 64)
    ssm_s = cat("ssm_s")[None]
    conv_s = cat("conv_s")[None]
    k_s = cat("k_s").reshape(BS, 32, 4, 64)
    v_s = cat("v_s").reshape(BS, 32, 4, 64)
    outs = (y_p, y_s, ssm_p, conv_p, k_p, v_p, ssm_s, conv_s, k_s, v_s)
    outs = tuple(np.ascontiguousarray(o, dtype=np.float32) for o in outs)
    return outs, res


def kernel(**inputs):
    outs, _ = run(inputs, 8)
    return outs
```
